# Optimizing a Trainium2 kernel written in Bass

```python
import math
import jax, jax.numpy as jnp
from jax import lax
import numpy as np

D_MODEL = 1024
BATCH = 8
SEQ = 4096
DEPTH = 2

PLE_DIM = 256
N_EVEN = (DEPTH + 1) // 2
N_ODD = DEPTH // 2
DEEPNORM_ALPHA = (2.0 * DEPTH) ** 0.25
DEEPNORM_BETA = (8.0 * DEPTH) ** -0.25
BLOCK = 128
EPS = 1e-6
A_HEAD_DIM = 64
A_HEADS = (D_MODEL // 2) // A_HEAD_DIM
A_KV_HEADS = A_HEADS // 4
A_WIDTH = A_HEADS * A_HEAD_DIM
WINDOW = 128
B_WIDTH = D_MODEL // 2
CONV_W = 3
C_HEADS = 8
C_NOPE = 64
C_ROPE = 32
C_V = 64
C_WIDTH = C_HEADS * C_V
C_Q_RANK = D_MODEL // 4
C_KV_RANK = D_MODEL // 8
ROPE_THETA = 10000.0
D_WIDTH = D_MODEL // 2
D_GROUPS = 4
D_GROUP_DIM = D_WIDTH // D_GROUPS
D_CHUNK = 128

EVEN_SIZES = (A_WIDTH, A_KV_HEADS * A_HEAD_DIM, A_KV_HEADS * A_HEAD_DIM,
              B_WIDTH, B_WIDTH, B_WIDTH, A_WIDTH + B_WIDTH)
ODD_SIZES = (C_Q_RANK, C_KV_RANK, C_ROPE, D_WIDTH, D_WIDTH, C_WIDTH + D_WIDTH)
EVEN_IN = sum(EVEN_SIZES)
ODD_IN = sum(ODD_SIZES)
MIX_OUT = D_MODEL

kernel_name = "hybrid_swa_shortconv_mla_gmlp_deepnorm"


def _split(h, sizes):
    idx = np.cumsum(np.array(sizes))[:-1].tolist()
    return jnp.split(h, idx, axis=-1)


def rms_norm(x, g):
    xf = x.astype(jnp.float32)
    y = xf * lax.rsqrt(jnp.mean(xf * xf, axis=-1, keepdims=True) + EPS)
    return (y * g.astype(jnp.float32)).astype(x.dtype)


def layer_norm(x, g, b):
    xf = x.astype(jnp.float32)
    mu = jnp.mean(xf, axis=-1, keepdims=True)
    xc = xf - mu
    var = jnp.mean(xc * xc, axis=-1, keepdims=True)
    y = xc * lax.rsqrt(var + EPS) * g.astype(jnp.float32) + b.astype(jnp.float32)
    return y.astype(x.dtype)


def rope(x, pos):
    half = x.shape[-1] // 2
    inv = ROPE_THETA ** (-jnp.arange(half, dtype=jnp.float32) / half)
    ang = pos.astype(jnp.float32)[..., None] * inv
    cos = jnp.cos(ang)[:, :, None, :]
    sin = jnp.sin(ang)[:, :, None, :]
    xf = x.astype(jnp.float32)
    x1, x2 = xf[..., :half], xf[..., half:]
    return jnp.concatenate([x1 * cos - x2 * sin, x1 * sin + x2 * cos], axis=-1).astype(x.dtype)


def _band_windows(t, nb):
    tp = jnp.pad(t, [(0, 0), (BLOCK, BLOCK)] + [(0, 0)] * (t.ndim - 2))
    tb = tp.reshape(t.shape[0], nb + 2, BLOCK, *t.shape[2:])
    return jnp.concatenate([tb[:, :-2], tb[:, 1:-1], tb[:, 2:]], axis=2)


def windowed_gqa_sink(q, k, v, pos, sink):
    Bn, S, H, dh = q.shape
    Hk = k.shape[2]
    G = H // Hk
    nb = S // BLOCK
    kw = _band_windows(k, nb)
    vw = _band_windows(v, nb)
    posk = _band_windows(pos, nb)
    posq = pos.reshape(Bn, nb, BLOCK)
    q_idx = jnp.arange(BLOCK)[:, None]
    w_idx = jnp.arange(3 * BLOCK)[None, :]
    in_band = jnp.abs(w_idx - BLOCK - q_idx) <= WINDOW
    k_glob = jnp.arange(nb)[:, None] * BLOCK + jnp.arange(3 * BLOCK)[None, :] - BLOCK
    in_seq = (k_glob >= 0) & (k_glob < S)
    valid = in_band[None] & in_seq[:, None, :]
    slopes = jnp.exp2(-8.0 * jnp.arange(1, H + 1, dtype=jnp.float32) / H).reshape(Hk, G, 1, 1)
    dist = jnp.abs(posq[..., :, None] - posk[..., None, :]).astype(jnp.float32)
    qb = q.reshape(Bn, nb, BLOCK, Hk, G, dh)
    logits = jnp.einsum('bnqkgd,bnskd->bnkgqs', qb, kw).astype(jnp.float32) * (dh ** -0.5)
    logits = logits - slopes * dist[:, :, None, None]
    logits = jnp.where(valid[None, :, None, None], logits, -jnp.inf)
    sink_l = jnp.broadcast_to(sink.astype(jnp.float32).reshape(Hk, G, 1, 1), logits.shape[:-1] + (1,))
    probs = jax.nn.softmax(jnp.concatenate([logits, sink_l], axis=-1), axis=-1)[..., :-1]
    out = jnp.einsum('bnkgqs,bnskd->bnqkgd', probs.astype(v.dtype), vw)
    return out.reshape(Bn, S, H * dh)


def short_conv_mixer(bg, cg, xin, conv_w):
    z = cg * xin
    C = z.shape[-1]
    y = lax.conv_general_dilated(z, conv_w[:, None, :].astype(z.dtype), window_strides=(1,),
                                 padding=((CONV_W // 2, CONV_W // 2),),
                                 dimension_numbers=('NWC', 'WIO', 'NWC'), feature_group_count=C)
    return bg * y


def mla_attention(q_nope, q_rope, k_nope, k_rope, v):
    Bn, S, H, _ = q_nope.shape
    nb = S // BLOCK
    scale = (C_NOPE + C_ROPE) ** -0.5

    def to_blocks(t):
        return jnp.moveaxis(t.reshape(Bn, nb, BLOCK, *t.shape[2:]), 1, 0)

    def attend(blk):
        qn, qr = blk
        s = (jnp.einsum('bqhd,bshd->bhqs', qn, k_nope)
             + jnp.einsum('bqhr,bsr->bhqs', qr, k_rope)).astype(jnp.float32) * scale
        pr = jax.nn.softmax(s, axis=-1).astype(v.dtype)
        return jnp.einsum('bhqs,bshd->bqhd', pr, v)

    out = lax.map(attend, (to_blocks(q_nope), to_blocks(q_rope)))
    return jnp.moveaxis(out, 0, 1).reshape(Bn, S, H * v.shape[-1])


def chunked_spatial_gate(u, v, v_ln_g, v_ln_b, w_s, b_s):
    Bn, S, _ = v.shape
    nc = S // D_CHUNK
    vn = layer_norm(v, v_ln_g, v_ln_b)
    vc = vn.reshape(Bn, nc, D_CHUNK, D_GROUPS, D_GROUP_DIM)
    mixed = jnp.einsum('gts,bcsgd->bctgd', w_s.astype(v.dtype), vc) + b_s.T.astype(v.dtype)[None, None, :, :, None]
    return u * mixed.reshape(Bn, S, D_WIDTH)


def even_mixer(x, pos, w_in, conv_w, sink, a_norm, b_norm, w_out):
    Bn, S, _ = x.shape
    q, k, v, bg, cg, xin, z = _split(x @ w_in, EVEN_SIZES)
    q = q.reshape(Bn, S, A_HEADS, A_HEAD_DIM)
    k = k.reshape(Bn, S, A_KV_HEADS, A_HEAD_DIM)
    v = v.reshape(Bn, S, A_KV_HEADS, A_HEAD_DIM)
    ya = windowed_gqa_sink(q, k, v, pos, sink)
    yb = short_conv_mixer(bg, cg, xin, conv_w)
    y = jnp.concatenate([rms_norm(ya, a_norm), rms_norm(yb, b_norm)], axis=-1) * jax.nn.silu(z)
    return y @ w_out


def odd_mixer(x, pos, w_in, q_norm, w_uq, kv_norm, w_ukv, v_ln_g, v_ln_b, w_s, b_s, c_norm, d_norm, w_out):
    Bn, S, _ = x.shape
    cq, ckv, kr, du, dv, z = _split(x @ w_in, ODD_SIZES)
    q = (rms_norm(cq, q_norm) @ w_uq).reshape(Bn, S, C_HEADS, C_NOPE + C_ROPE)
    q_nope = q[..., :C_NOPE]
    q_rope = rope(q[..., C_NOPE:], pos)
    kv = (rms_norm(ckv, kv_norm) @ w_ukv).reshape(Bn, S, C_HEADS, C_NOPE + C_V)
    k_nope, v = kv[..., :C_NOPE], kv[..., C_NOPE:]
    k_rope = rope(kr[:, :, None, :], pos)[:, :, 0]
    yc = mla_attention(q_nope, q_rope, k_nope, k_rope, v)
    yd = chunked_spatial_gate(jax.nn.gelu(du), jax.nn.gelu(dv), v_ln_g, v_ln_b, w_s, b_s)
    y = jnp.concatenate([rms_norm(yc, c_norm), rms_norm(yd, d_norm)], axis=-1) * jax.nn.silu(z)
    return y @ w_out


def _normal(k, shape, std):
    return std * jax.random.normal(k, shape, jnp.float32)


def setup_inputs(seed: int = 0) -> dict:
    key = jax.random.key(seed)
    ks = iter(jax.random.split(key, 32))
    E, O, L = N_EVEN, N_ODD, DEPTH
    d = {}
    d['x'] = _normal(next(ks), (BATCH, SEQ, D_MODEL), 1.0)
    d['p'] = _normal(next(ks), (DEPTH, BATCH, SEQ, PLE_DIM), 1.0)
    d['positions'] = jnp.broadcast_to(jnp.arange(SEQ, dtype=jnp.int32), (BATCH, SEQ))
    d['ev_w_in'] = _normal(next(ks), (E, D_MODEL, EVEN_IN), D_MODEL ** -0.5)
    d['ev_conv_w'] = _normal(next(ks), (E, CONV_W, B_WIDTH), CONV_W ** -0.5)
    d['ev_sink'] = _normal(next(ks), (E, A_HEADS), 0.5)
    d['ev_a_norm'] = 1.0 + _normal(next(ks), (E, A_WIDTH), 0.02)
    d['ev_b_norm'] = 1.0 + _normal(next(ks), (E, B_WIDTH), 0.02)
    d['ev_w_out'] = _normal(next(ks), (E, MIX_OUT, D_MODEL), DEEPNORM_BETA * MIX_OUT ** -0.5)
    d['od_w_in'] = _normal(next(ks), (O, D_MODEL, ODD_IN), D_MODEL ** -0.5)
    d['od_q_norm'] = 1.0 + _normal(next(ks), (O, C_Q_RANK), 0.02)
    d['od_w_uq'] = _normal(next(ks), (O, C_Q_RANK, C_HEADS * (C_NOPE + C_ROPE)), C_Q_RANK ** -0.5)
    d['od_kv_norm'] = 1.0 + _normal(next(ks), (O, C_KV_RANK), 0.02)
    d['od_w_ukv'] = _normal(next(ks), (O, C_KV_RANK, C_HEADS * (C_NOPE + C_V)), C_KV_RANK ** -0.5)
    d['od_v_ln_g'] = 1.0 + _normal(next(ks), (O, D_WIDTH), 0.02)
    d['od_v_ln_b'] = _normal(next(ks), (O, D_WIDTH), 0.02)
    d['od_w_s'] = _normal(next(ks), (O, D_GROUPS, D_CHUNK, D_CHUNK), D_CHUNK ** -0.5)
    d['od_b_s'] = 1.0 + _normal(next(ks), (O, D_GROUPS, D_CHUNK), 0.1)
    d['od_c_norm'] = 1.0 + _normal(next(ks), (O, C_WIDTH), 0.02)
    d['od_d_norm'] = 1.0 + _normal(next(ks), (O, D_WIDTH), 0.02)
    d['od_w_out'] = _normal(next(ks), (O, MIX_OUT, D_MODEL), DEEPNORM_BETA * MIX_OUT ** -0.5)
    d['post_ln_g'] = 1.0 + _normal(next(ks), (L, D_MODEL), 0.02)
    d['post_ln_b'] = _normal(next(ks), (L, D_MODEL), 0.02)
    d['ple_proj'] = _normal(next(ks), (L, PLE_DIM, D_MODEL), PLE_DIM ** -0.5)
    d['ple_gate'] = _normal(next(ks), (L, D_MODEL, D_MODEL), D_MODEL ** -0.5)
    return d


def reference(x, p, positions, ev_w_in, ev_conv_w, ev_sink, ev_a_norm, ev_b_norm, ev_w_out,
              od_w_in, od_q_norm, od_w_uq, od_kv_norm, od_w_ukv, od_v_ln_g, od_v_ln_b, od_w_s, od_b_s,
              od_c_norm, od_d_norm, od_w_out, post_ln_g, post_ln_b, ple_proj, ple_gate):
    for i in range(DEPTH):
        j = i // 2
        if i % 2 == 0:
            y = even_mixer(x, positions, ev_w_in[j], ev_conv_w[j], ev_sink[j],
                           ev_a_norm[j], ev_b_norm[j], ev_w_out[j])
        else:
            y = odd_mixer(x, positions, od_w_in[j], od_q_norm[j], od_w_uq[j], od_kv_norm[j], od_w_ukv[j],
                          od_v_ln_g[j], od_v_ln_b[j], od_w_s[j], od_b_s[j], od_c_norm[j], od_d_norm[j],
                          od_w_out[j])
        h = layer_norm(DEEPNORM_ALPHA * x + y, post_ln_g[i], post_ln_b[i])
        x = h + jax.nn.sigmoid(h @ ple_gate[i]) * (p[i] @ ple_proj[i])
    return x
```

```python
import contextlib
import numpy as np
import concourse.bass as bass
import concourse.mybir as mybir
from concourse.bass_utils import run_bass_kernel_spmd

F32 = mybir.dt.float32
BF16 = mybir.dt.bfloat16
I32 = mybir.dt.int32
AF = mybir.ActivationFunctionType
ALU = mybir.AluOpType

NCORES = 8
SEQ = 4096
DM = 1024
NT = SEQ // 128
ALPHA = 4.0 ** 0.25
EPS = 1e-6
BIG = 30000.0

ENG_NAMES = ("pe", "act", "dve", "pool", "sp")
EPOCH = 30000
INLINE_WAIT = True


class Buf:
    __slots__ = ("name", "writers", "readers", "excl")

    def __init__(self, name="", excl=False):
        self.name = name
        self.writers = []
        self.readers = []
        self.excl = excl


class Op:
    __slots__ = ("eng", "fn", "deps", "is_dma", "sig", "seq", "dsem", "dval")

    def __init__(self, eng, fn, is_dma):
        self.eng = eng
        self.fn = fn
        self.deps = []
        self.is_dma = is_dma
        self.sig = False
        self.seq = 0
        self.dsem = None
        self.dval = 0


class _Rec:
    def __init__(self):
        self.call = None

    def __getattr__(self, name):
        def f(*a, **k):
            self.call = (name, a, k)
        return f


def _replay(call):
    name, a, k = call
    return lambda e: getattr(e, name)(*a, **k)


class Sched:
    def __init__(self, nc, n_dma_sems=24):
        self.nc = nc
        self.ops = {e: [] for e in ENG_NAMES}
        self.n_dma_sems = n_dma_sems

    def op(self, eng, fn, reads=(), writes=(), dma=False):
        rec = _Rec()
        fn(rec)
        o = Op(eng, _replay(rec.call), dma)
        deps = []
        for b in reads:
            deps.extend(b.writers)
            if b.excl:
                deps.extend(r for r in b.readers if r.eng != eng)
        for b in writes:
            for r in b.readers:
                if r.is_dma or dma or r.eng != eng or eng != "pe":
                    deps.append(r)
            for w in b.writers:
                if w.is_dma or dma or w.eng != eng or eng != "pe":
                    deps.append(w)
        seen = set()
        for d in deps:
            if id(d) not in seen and d is not o:
                seen.add(id(d))
                o.deps.append(d)
                d.sig = True
        for b in reads:
            b.readers.append(o)
        for b in writes:
            if b.readers:
                b.readers = [r for r in b.readers if r is o]
                b.writers = [o]
            else:
                b.writers.append(o)
        self.ops[eng].append(o)
        return o

    def barrier(self):
        lasts = []
        for e in ENG_NAMES:
            nd = [o for o in self.ops[e] if not o.is_dma]
            if nd:
                lasts.append(nd[-1])
            dl = [o for o in self.ops[e] if o.is_dma]
            lasts.extend(dl[-self.n_dma_sems:])
        for e in ENG_NAMES:
            o = Op(e, lambda eng: eng.nop(), False)
            for d in lasts:
                if d.is_dma or d.eng != e:
                    o.deps.append(d)
                    d.sig = True
            self.ops[e].append(o)

    def emit(self):
        nc = self.nc
        with contextlib.ExitStack() as es:
            eng_sems = {}
            for e in ENG_NAMES:
                n = 0
                for o in self.ops[e]:
                    if not o.is_dma and o.sig:
                        n += 1
                        o.seq = n
                nep = max((n + EPOCH - 1) // EPOCH, 1)
                eng_sems[e] = [es.enter_context(nc.semaphore(f"s_{e}_{k}")) for k in range(nep)]
            for e in ENG_NAMES:
                dl = [o for o in self.ops[e] if o.is_dma]
                if not dl:
                    continue
                pool = [es.enter_context(nc.semaphore(f"d_{e}_{k}")) for k in range(self.n_dma_sems)]
                cnt = [0] * len(pool)
                for k, o in enumerate(dl):
                    j = k % len(pool)
                    cnt[j] += 1
                    o.dsem = pool[j]
                    o.dval = 16 * cnt[j]
            block = es.enter_context(nc.Block())

            def run_engine(ename, engobj):
                waited = {}

                def need(sem, val):
                    if waited.get(sem.num, 0) >= val:
                        return
                    waited[sem.num] = val
                    engobj.wait_ge(sem, val)

                for o in self.ops[ename]:
                    pend = []

                    def need2(sem, val):
                        if waited.get(sem.num, 0) >= val:
                            return
                        waited[sem.num] = val
                        pend.append((sem, val))

                    for d in o.deps:
                        if d.is_dma:
                            need2(d.dsem, d.dval)
                        else:
                            k = (d.seq - 1) // EPOCH
                            need2(eng_sems[d.eng][k], d.seq - k * EPOCH)
                    if o.is_dma and o.dval > 16:
                        need2(o.dsem, o.dval - 16)
                    inline = None
                    if pend and INLINE_WAIT and not o.is_dma:
                        inline = pend.pop()
                    for sem, val in pend:
                        engobj.wait_ge(sem, val)
                    ins = o.fn(engobj)
                    if inline is not None:
                        ins._wait_ge(inline[0], inline[1])
                    if o.is_dma:
                        ins.then_inc(o.dsem, 16)
                    elif o.sig:
                        k = (o.seq - 1) // EPOCH
                        ins.then_inc(eng_sems[ename][k], 1)
                last = {}
                for o in self.ops[ename]:
                    if o.is_dma:
                        last[o.dsem.num] = (o.dsem, o.dval)
                for sem, val in last.values():
                    need(sem, val)

            block.tensor(lambda eng: run_engine("pe", eng))
            block.scalar(lambda eng: run_engine("act", eng))
            block.vector(lambda eng: run_engine("dve", eng))
            block.gpsimd(lambda eng: run_engine("pool", eng))
            block.sync(lambda eng: run_engine("sp", eng))


class T:
    def __init__(self, t, name):
        self.t = t
        self.b = Buf(name)

    def __getitem__(self, k):
        return self.t[k]


def _bufs(xs):
    out = []
    for x in xs:
        if x is None:
            continue
        out.append(x.b if isinstance(x, T) else x)
    return out


class Ctx:
    def __init__(self, nc):
        self.nc = nc
        self.S = Sched(nc)
        self.n = 0
        self.stack = [contextlib.ExitStack()]
        self.rguards = []

    def sb(self, shape, dt, name=None):
        self.n += 1
        name = f"{name or 'sb'}_{self.n}"
        t = self.stack[-1].enter_context(self.nc.sbuf_tensor(name, list(shape), dt))
        return T(t, name)

    def sbr(self, shape, dt, name=None):
        self.n += 1
        name = f"{name or 'sbr'}_{self.n}"
        g = self.nc.sbuf_tensor(name, list(shape), dt, side="right")
        t = g.__enter__()
        self.rguards.append(g)
        return T(t, name)

    def rpop(self, n):
        for _ in range(n):
            self.rguards.pop().__exit__(None, None, None)

    @contextlib.contextmanager
    def scope(self):
        es = contextlib.ExitStack()
        self.stack.append(es)
        try:
            yield
        finally:
            self.S.barrier()
            self.stack.pop()
            es.close()

    def ps(self, shape, dt, name=None):
        self.n += 1
        name = f"{name or 'ps'}_{self.n}"
        t = T(self.nc.alloc_psum_tensor(name, list(shape), dt), name)
        t.b.excl = True
        return t

    def op(self, eng, fn, r=(), w=()):
        return self.S.op(eng, fn, _bufs(r), _bufs(w))

    def dma(self, q, out, in_, r=(), w=()):
        return self.S.op(q, lambda e: e.dma_start(out=out, in_=in_), _bufs(r), _bufs(w), dma=True)


def build(layers, debug=False):
    nc = bass.Bass("TRN2", target_bir_lowering=False)
    C = Ctx(nc)
    dbg_names = []

    def din(name, shape, dt=F32):
        return nc.dram_tensor(name, list(shape), dt, kind="ExternalInput").ap()

    def dscr(name, shape, dt=F32):
        if debug:
            dbg_names.append(name)
            return nc.dram_tensor(name, list(shape), dt, kind="ExternalOutput").ap()
        return nc.dram_tensor(name, list(shape), dt, kind="Internal").ap()

    def dump(name, tt, shape, dt=F32):
        if not debug:
            return
        dbg_names.append(name)
        o = nc.dram_tensor(name, list(shape), dt, kind="ExternalOutput").ap()
        C.dma("sp", o, tt[:], r=[tt])

    x_in = din("x", [SEQ, DM])
    out_d = nc.dram_tensor("out", [SEQ, DM], F32, kind="ExternalOutput").ap()
    pos_d = din("pos", [SEQ], I32)
    posT_d = din("posT", [128, NT], I32)
    p_d = {l: din(f"p{l}", [SEQ, 256]) for l in layers}
    lng_d = {l: din(f"ln_g{l}", [DM]) for l in layers}
    lnb_d = {l: din(f"ln_b{l}", [DM]) for l in layers}
    wout_d = {l: din(f"w_out{l}", [DM, DM]) for l in layers}
    gate_d = {l: din(f"gate{l}", [DM, DM]) for l in layers}
    proj_d = {l: din(f"proj{l}", [256, DM]) for l in layers}
    if 0 in layers:
        w_in0_d = din("w_in0", [DM, 3328])
        convw_d = din("conv_w", [3, 512])
        sink_d = din("sink", [8])
        anorm_d = din("a_norm", [512])
        bnorm_d = din("b_norm", [512])
    if 1 in layers:
        w_in1_d = din("w_in1", [DM, 2464])
        w_uq_d = din("w_uq", [256, 768])
        w_ukv_d = din("w_ukv", [128, 1024])
        w_sT_d = din("w_sT", [128, 4, 128])
        b_sT_d = din("b_sT", [128, 4])
        qn_d = din("q_norm", [256])
        kvn_d = din("kv_norm", [128])
        vlg_d = din("v_ln_g", [512])
        vlb_d = din("v_ln_b", [512])
        cn_d = din("c_norm", [512])
        dn_d = din("d_norm", [512])
        invf_d = din("inv_freq", [16])
    x_mid = dscr("x_mid", [SEQ, DM]) if len(layers) == 2 else None

    yg_s = dscr("yg_s", [SEQ, DM], BF16)
    ygB = [Buf(f"yg{i}") for i in range(NT)]
    xmidB = [Buf(f"xm{i}") for i in range(NT)]

    ident_f = C.sb([128, 128], F32, "identf")
    ident = C.sb([128, 128], BF16, "ident")
    eps_t = C.sb([128, 1], F32, "eps")
    posk_i = C.sb([128, NT], I32, "poski")
    posk = C.sb([128, NT], F32, "posk")
    junk = C.sb([128, DM], BF16, "junk")
    PT = [C.ps([128, 1024], BF16, "pt") for _ in range(2)]
    PF = [C.ps([128, 512], F32, "pf") for _ in range(6)]
    st = {"pt": 0}

    def next_pt():
        st["pt"] ^= 1
        return PT[st["pt"]]

    C.op("pool", lambda e: e.memset(ident_f[:], 0.0), w=[ident_f])
    C.op("pool", lambda e: e.affine_select(out=ident_f[:], in_=ident_f[:], pattern=[[-1, 128]],
                                           compare_op=ALU.not_equal, fill=1.0, base=0,
                                           channel_multiplier=1), r=[ident_f], w=[ident_f])
    C.op("dve", lambda e: e.tensor_copy(out=ident[:], in_=ident_f[:]), r=[ident_f], w=[ident])
    mlo = C.sb([128, 128], F32, "mlo")
    mhi = C.sb([128, 128], F32, "mhi")
    C.op("pool", lambda e: e.memset(mlo[:], 0.0), w=[mlo])
    C.op("pool", lambda e: e.memset(mhi[:], 0.0), w=[mhi])
    C.op("pool", lambda e: e.affine_select(out=mlo[:], in_=mlo[:], pattern=[[-1, 128]], compare_op=ALU.is_ge,
                                           fill=BIG, base=0, channel_multiplier=1), r=[mlo], w=[mlo])
    C.op("pool", lambda e: e.affine_select(out=mhi[:], in_=mhi[:], pattern=[[1, 128]], compare_op=ALU.is_ge,
                                           fill=BIG, base=0, channel_multiplier=-1), r=[mhi], w=[mhi])
    C.op("pool", lambda e: e.memset(eps_t[:], EPS), w=[eps_t])
    negh = C.sb([128, 1], F32, "negh")
    C.op("pool", lambda e: e.memset(negh[:], -0.5), w=[negh])
    C.dma("sp", posk_i[:], posT_d, w=[posk_i])
    C.op("dve", lambda e: e.tensor_copy(out=posk[:], in_=posk_i[:]), r=[posk_i], w=[posk])

    def rstd_from_sum(ssum, n, out):
        C.op("dve", lambda e: e.tensor_scalar(out=out[:], in0=ssum[:], scalar1=1.0 / n, scalar2=EPS, op0=ALU.mult,
                                              op1=ALU.add), r=[ssum], w=[out])
        C.op("pool", lambda e: e.tensor_tensor(out=out[:], in0=out[:], in1=negh[:], op=ALU.pow), r=[out, negh], w=[out])

    def transpose_to(src, nchunk, width, dst_ap_fn, dst, eng="act", rows=128):
        pt = next_pt()
        ptv = pt.t[:].rearrange("p (c t) -> p c t", t=128)
        for c in range(nchunk):
            C.op("pe", lambda e, c=c: e.transpose(out=ptv[0:width, c, :], in_=src[:, c * width:(c + 1) * width],
                                                  identity=ident[:]), r=[src, ident], w=[pt])
        if eng == "act":
            C.op("act", lambda e: e.copy(out=dst_ap_fn(), in_=ptv[0:width, 0:nchunk, :]), r=[pt], w=[dst])
        else:
            C.op(eng, lambda e: e.tensor_copy(out=dst_ap_fn(), in_=ptv[0:width, 0:nchunk, :]), r=[pt], w=[dst])

    def load_weight(dst_ap_fn, src_ap, nchunks, dstT, thunks=None):
        for c in range(nchunks):
            def go(c=c):
                C.dma("pool", dst_ap_fn(c), src_ap[c * 128:(c + 1) * 128, :], w=[dstT])
            if thunks is None:
                go()
            else:
                thunks.append(go)

    def alloc_post_weights(right):
        return (C.sbr if right else C.sb)([128, 8, 3072], BF16, "wpost")

    def load_post_weights(l, wbig, thunks=None):
        load_weight(lambda c: wbig.t[:, c, 0:1024], wout_d[l], 8, wbig, thunks)
        load_weight(lambda c: wbig.t[:, c, 1024:2048], gate_d[l], 8, wbig, thunks)
        load_weight(lambda c: wbig.t[:, c, 2048:3072], proj_d[l], 2, wbig, thunks)

    def post_stage(l, xsrc, xsrcB, dst, dstB, wts=None, thunks=None):
      with C.scope():
        if wts is None:
            wbig = alloc_post_weights(False)
            load_post_weights(l, wbig)
        else:
            wbig = wts
        lng = C.sb([128, DM], F32, "lng")
        lnb = C.sb([128, DM], F32, "lnb")
        C.dma("sp", lng[:], lng_d[l].partition_broadcast(128), w=[lng])
        C.dma("sp", lnb[:], lnb_d[l].partition_broadcast(128), w=[lnb])
        ygb = [C.sb([128, DM], BF16, "ygb") for _ in range(2)]
        ygT = [C.sb([128, 8, 128], BF16, "ygT") for _ in range(2)]
        xr = [C.sb([128, DM], F32, "xr") for _ in range(2)]
        pb = [C.sb([128, 256], BF16, "pb") for _ in range(2)]
        pT = [C.sb([128, 2, 128], BF16, "pT") for _ in range(3)]
        junk2 = C.sb([128, DM], BF16, "junk2")
        s = [C.sb([128, DM], F32, "s") for _ in range(2)]
        h2 = [C.sb([128, DM], F32, "h") for _ in range(2)]
        hb2 = [C.sb([128, DM], BF16, "hb") for _ in range(2)]
        hT2 = [C.sb([128, 8, 128], BF16, "hT") for _ in range(2)]
        sg = [C.sb([128, DM], F32, "sg") for _ in range(2)]
        msum = C.sb([128, 1], F32, "msum")
        nmean = C.sb([128, 1], F32, "nmean")
        vsum = C.sb([128, 1], F32, "vsum")
        rstd = C.sb([128, 1], F32, "rstd")
        py = (PF[0], PF[1])
        pg = (PF[2], PF[3])
        pp = (PF[4], PF[5])

        def loads(i):
            k = i % 2
            C.dma("sp", ygb[k][:], yg_s[i * 128:(i + 1) * 128, :], r=[ygB[i]], w=[ygb[k]])
            C.dma("sp", xr[k][:], xsrc[i * 128:(i + 1) * 128, :], r=([xsrcB[i]] if xsrcB else []), w=[xr[k]])
            C.dma("pool", pb[k][:], p_d[l][i * 128:(i + 1) * 128, :], w=[pb[k]])

        def phase1(i):
            k = i % 2
            transpose_to(ygb[k], 8, 128, lambda: ygT[k].t[:], ygT[k], eng="act")
            transpose_to(pb[k], 2, 128, lambda: pT[i % 3].t[:], pT[i % 3], eng="dve")
            for hf in range(2):
                for c in range(8):
                    C.op("pe", lambda e: e.matmul(py[hf].t[:], lhsT=ygT[k].t[:, c, :],
                                                  rhs=wbig.t[:, c, hf * 512:(hf + 1) * 512],
                                                  start=(c == 0), stop=(c == 7)),
                         r=[ygT[k], wbig], w=[py[hf]])
            for hf in range(2):
                C.op("dve", lambda e: e.scalar_tensor_tensor(
                    out=s[k].t[:, hf * 512:(hf + 1) * 512], in0=xr[k].t[:, hf * 512:(hf + 1) * 512], scalar=ALPHA,
                    in1=py[hf].t[:], op0=ALU.mult, op1=ALU.add), r=[xr[k], py[hf]], w=[s[k]])

        def phase2a(i):
            k = i % 2
            sk = s[k]
            h, hb = h2[k], hb2[k]
            C.op("act", lambda e: e.activation(out=junk[:], in_=sk[:], func=AF.Copy, accum_out=msum[:]),
                 r=[sk], w=[junk, msum])
            yield
            C.op("dve", lambda e: e.tensor_scalar(out=nmean[:], in0=msum[:], scalar1=-1.0 / DM, scalar2=None,
                                                  op0=ALU.mult), r=[msum], w=[nmean])
            C.op("act", lambda e: e.activation(out=sk[:], in_=sk[:], func=AF.Identity, bias=nmean[:, 0:1], scale=1.0),
                 r=[sk, nmean], w=[sk])
            yield
            C.op("act", lambda e: e.activation(out=junk2[:], in_=sk[:], func=AF.Square, accum_out=vsum[:]),
                 r=[sk], w=[junk2, vsum])
            yield
            rstd_from_sum(vsum, DM, rstd)
            yield
            C.op("dve", lambda e: e.scalar_tensor_tensor(out=h[:], in0=sk[:], scalar=rstd[:, 0:1], in1=lng[:],
                                                         op0=ALU.mult, op1=ALU.mult), r=[sk, rstd, lng], w=[h])
            yield
            C.op("dve", lambda e: e.tensor_tensor(out=hb[:], in0=h[:], in1=lnb[:], op=ALU.add), r=[h, lnb], w=[hb])
            C.op("dve", lambda e: e.tensor_tensor(out=h[:], in0=h[:], in1=lnb[:], op=ALU.add), r=[h, lnb], w=[h])
            yield

        def phase2b(i):
            k = i % 2
            h, hb, hT = h2[k], hb2[k], hT2[k]
            pTk = pT[i % 3]
            transpose_to(hb, 8, 128, lambda: hT.t[:], hT, eng="act")
            yield
            for hf in range(2):
                for c in range(8):
                    C.op("pe", lambda e: e.matmul(pg[hf].t[:], lhsT=hT.t[:, c, :],
                                                  rhs=wbig.t[:, c, 1024 + hf * 512:1024 + (hf + 1) * 512],
                                                  start=(c == 0), stop=(c == 7)),
                         r=[hT, wbig], w=[pg[hf]])
                for c in range(2):
                    C.op("pe", lambda e: e.matmul(pp[hf].t[:], lhsT=pTk.t[:, c, :],
                                                  rhs=wbig.t[:, c, 2048 + hf * 512:2048 + (hf + 1) * 512],
                                                  start=(c == 0), stop=(c == 1)),
                         r=[pTk, wbig], w=[pp[hf]])
                yield
            for hf in range(2):
                sl = slice(hf * 512, (hf + 1) * 512)
                C.op("act", lambda e: e.activation(out=sg[k].t[:, sl], in_=pg[hf].t[:], func=AF.Sigmoid),
                     r=[pg[hf]], w=[sg[k]])
                yield
                C.op("dve", lambda e: e.tensor_tensor(out=sg[k].t[:, sl], in0=sg[k].t[:, sl], in1=pp[hf].t[:],
                                                      op=ALU.mult), r=[sg[k], pp[hf]], w=[sg[k]])
                yield
            C.op("pool", lambda e: e.tensor_tensor(out=sg[k].t[:], in0=sg[k].t[:], in1=h[:], op=ALU.add),
                 r=[sg[k], h], w=[sg[k]])
            C.dma("sp", dst[i * 128:(i + 1) * 128, :], sg[k].t[:], r=[sg[k]], w=([dstB[i]] if dstB else []))
            yield

        def interleave(*gens):
            gens = [g for g in gens if g is not None]
            while gens:
                for g in list(gens):
                    try:
                        next(g)
                    except StopIteration:
                        gens.remove(g)

        loads(0)
        loads(1)
        phase1(0)
        for i in range(NT + 1):
            if i + 2 < NT:
                loads(i + 2)
            if thunks:
                thunks.pop(0)()
            interleave(phase2a(i) if i < NT else None, phase2b(i - 1) if i >= 1 else None)
            if i + 1 < NT:
                phase1(i + 1)

    def layer0(xsrc, xsrcB, dst, dstB, hooks=None):
      hooks = hooks or {}
      with C.scope():
        q_s = dscr("q_s", [SEQ, 512], BF16)
        zp_s = dscr("zp_s", [SEQ + 2, 512])
        bg_s = dscr("bg_s", [SEQ, 512])
        sz_s = dscr("sz_s", [SEQ, DM])
        qB = [Buf() for _ in range(NT)]
        zpB = [Buf() for _ in range(NT)]
        zpadB = Buf()
        bgB = [Buf() for _ in range(NT)]
        szB = [Buf() for _ in range(NT)]

        kT = C.sb([128, SEQ], BF16, "kT")
        vall = C.sb([128, NT, 2, 65], BF16, "vall")
        kTB = [Buf() for _ in range(NT)]
        vB = [Buf() for _ in range(NT)]
        C.op("pool", lambda e: e.memset(vall[:], 1.0), w=[vall] + vB)
        zero = C.sb([128, 512], F32, "zero")
        C.op("pool", lambda e: e.memset(zero[:], 0.0), w=[zero])
        C.dma("sp", zp_s[0:1, :], zero.t[0:1, :], r=[zero], w=[zpadB])
        C.dma("sp", zp_s[SEQ + 1:SEQ + 2, :], zero.t[0:1, :], r=[zero], w=[zpadB])
        cw = [C.sb([128, 512], F32, "cw") for _ in range(3)]
        for k3 in range(3):
            C.dma("sp", cw[k3][:], convw_d[k3].partition_broadcast(128), w=[cw[k3]])
        anb = C.sb([128, 512], F32, "anb")
        bnb = C.sb([128, 512], F32, "bnb")
        C.dma("sp", anb[:], anorm_d.partition_broadcast(128), w=[anb])
        C.dma("sp", bnb[:], bnorm_d.partition_broadcast(128), w=[bnb])
        esink = C.sb([128, 8], F32, "esink")
        C.dma("sp", esink[:], sink_d.partition_broadcast(128), w=[esink])
        C.op("act", lambda e: e.activation(out=esink[:], in_=esink[:], func=AF.Exp), r=[esink], w=[esink])
        nsI = C.sb([128, 8, 128], BF16, "nsI")
        for hh in range(8):
            C.op("dve", lambda e, hh=hh: e.tensor_scalar(out=nsI.t[:, hh, :], in0=ident_f[:],
                                                         scalar1=-8.0 * 2.0 ** (-(hh + 1)), scalar2=None,
                                                         op0=ALU.mult), r=[ident_f], w=[nsI])

        with C.scope():
            wbig = C.sb([128, 8, 3328], BF16, "w_in0")
            wcB = [Buf(f"w_in0_c{c}") for c in range(8)]
            for c in range(8):
                C.dma("pool", wbig.t[:, c, :], w_in0_d[c * 128:(c + 1) * 128, :], w=[wcB[c]])
            xb = [C.sb([128, DM], BF16, "xb") for _ in range(2)]
            xT = [C.sb([128, 8, 128], BF16, "xT") for _ in range(2)]
            qb = [C.sb([128, 512], BF16, "qb") for _ in range(2)]
            kb_ = C.sb([128, 128], BF16, "kb")
            cgs = C.sb([128, 512], F32, "cgs")
            zpt = [C.sb([128, 512], F32, "zpt") for _ in range(2)]
            bgt = [C.sb([128, 512], F32, "bgt") for _ in range(2)]
            szt = [C.sb([128, DM], F32, "szt") for _ in range(2)]
            groups = [(0, 512), (512, 256), (768, 512), (1280, 512), (1792, 512), (2304, 512), (2816, 512)]

            def ld1(i):
                C.dma("pool", xb[i % 2][:], xsrc[i * 128:(i + 1) * 128, :], r=([xsrcB[i]] if xsrcB else []), w=[xb[i % 2]])

            ld1(0)
            for i in range(NT):
                k = i % 2
                if i + 1 < NT:
                    ld1(i + 1)
                transpose_to(xb[k], 8, 128, lambda k=k: xT[k].t[:], xT[k], eng="act")
                banks = []
                for gi, (n0, wd) in enumerate(groups):
                    bank = PF[gi % 6]
                    banks.append(bank)
                    for c in range(8):
                        C.op("pe", lambda e, c=c, n0=n0, wd=wd, bank=bank, k=k: e.matmul(
                            bank.t[:, 0:wd], lhsT=xT[k].t[:, c, :], rhs=wbig.t[:, c, n0:n0 + wd],
                            start=(c == 0), stop=(c == 7)), r=[xT[k], wcB[c]], w=[bank])
                    if gi == 0:
                        C.op("act", lambda e, bank=bank, k=k: e.copy(out=qb[k][:], in_=bank.t[:]), r=[bank], w=[qb[k]])
                        C.dma("sp", q_s[i * 128:(i + 1) * 128, :], qb[k][:], r=[qb[k]], w=[qB[i]])
                    elif gi == 1:
                        C.op("dve", lambda e, bank=bank: e.tensor_copy(out=kb_[:], in_=bank.t[:, 0:128]), r=[bank], w=[kb_])
                        C.op("dve", lambda e, bank=bank, i=i: e.tensor_copy(
                            out=vall.t[:, i, :, 0:64], in_=bank.t[:, 128:256].rearrange("p (h d) -> p h d", h=2)),
                            r=[bank], w=[vB[i]])
                        transpose_to(kb_, 1, 128, lambda i=i: kT.t[:, i * 128:(i + 1) * 128].unsqueeze(1), kTB[i], eng="dve")
                    elif gi == 2:
                        C.op("act", lambda e, bank=bank, k=k: e.copy(out=bgt[k][:], in_=bank.t[:]), r=[bank], w=[bgt[k]])
                        C.dma("sp", bg_s[i * 128:(i + 1) * 128, :], bgt[k][:], r=[bgt[k]], w=[bgB[i]])
                    elif gi == 3:
                        C.op("act", lambda e, bank=bank: e.copy(out=cgs[:], in_=bank.t[:]), r=[bank], w=[cgs])
                    elif gi == 4:
                        C.op("dve", lambda e, bank=bank, k=k: e.tensor_tensor(out=zpt[k][:], in0=bank.t[:], in1=cgs[:],
                                                                             op=ALU.mult), r=[bank, cgs], w=[zpt[k]])
                        C.dma("sp", zp_s[1 + i * 128:1 + (i + 1) * 128, :], zpt[k][:], r=[zpt[k]], w=[zpB[i]])
                    else:
                        hf = gi - 5
                        C.op("act", lambda e, bank=bank, k=k, hf=hf: e.activation(
                            out=szt[k].t[:, hf * 512:(hf + 1) * 512], in_=bank.t[:], func=AF.Silu), r=[bank], w=[szt[k]])
                        if hf == 1:
                            C.dma("sp", sz_s[i * 128:(i + 1) * 128, :], szt[k][:], r=[szt[k]], w=[szB[i]])

        if "pre_s2" in hooks:
            hooks["pre_s2"]()
        with C.scope():
            q2 = [C.sb([128, 512], BF16, "q2") for _ in range(2)]
            qT = [C.sb([128, 4, 128], BF16, "qT") for _ in range(2)]
            zw = [[C.sb([128, 512], F32, "zw") for _ in range(3)] for _ in range(2)]
            bg2 = [C.sb([128, 512], F32, "bg2") for _ in range(2)]
            sz2 = [C.sb([128, DM], F32, "sz2") for _ in range(2)]
            pq_i = [C.sb([128, 128], I32, "pqi") for _ in range(2)]
            pq = [C.sb([128, 128], F32, "pq") for _ in range(2)]
            dmf = [C.sb([128, 3, 128], F32, "dmf") for _ in range(2)]
            Dm = [C.sb([128, 3, 128], BF16, "Dm") for _ in range(2)]
            pTt = [C.sb([128, 3, 128], BF16, "pTt") for _ in range(4)]
            ya = C.sb([128, 8, 64], F32, "ya")
            den = C.sb([128, 8], F32, "den")
            ssa = C.sb([128, 1], F32, "ssa")
            ra = C.sb([128, 1], F32, "ra")
            ssb = C.sb([128, 1], F32, "ssb")
            rb = C.sb([128, 1], F32, "rb")
            yg = [C.sb([128, DM], F32, "yg") for _ in range(2)]
            ygo = [C.sb([128, DM], BF16, "ygo") for _ in range(2)]
            po = (PF[4], PF[5])

            def ld2(j):
                k = j % 2
                C.dma("sp", q2[k][:], q_s[j * 128:(j + 1) * 128, :], r=[qB[j]], w=[q2[k]])
                for d3 in range(3):
                    rd = [zpB[j]]
                    if d3 == 0:
                        rd.append(zpB[j - 1] if j > 0 else zpadB)
                    if d3 == 2:
                        rd.append(zpB[j + 1] if j + 1 < NT else zpadB)
                    C.dma("sp", zw[k][d3][:], zp_s[j * 128 + d3:j * 128 + d3 + 128, :], r=rd, w=[zw[k][d3]])
                C.dma("sp", bg2[k][:], bg_s[j * 128:(j + 1) * 128, :], r=[bgB[j]], w=[bg2[k]])
                C.dma("sp", sz2[k][:], sz_s[j * 128:(j + 1) * 128, :], r=[szB[j]], w=[sz2[k]])
                C.dma("sp", pq_i[k][:], pos_d[j * 128:(j + 1) * 128].partition_broadcast(128), w=[pq_i[k]])

            def kbs_of(j):
                return [kb for kb in (j - 1, j, j + 1) if 0 <= kb < NT]

            def prep_attn(j):
                k = j % 2
                transpose_to(q2[k], 4, 128, lambda: qT[k].t[:], qT[k], eng="dve")
                C.op("dve", lambda e: e.tensor_copy(out=pq[k][:], in_=pq_i[k][:]), r=[pq_i[k]], w=[pq[k]])
                kbs = kbs_of(j)
                s0 = 0 if j > 0 else 1
                for kb in kbs:
                    s = kb - j + 1
                    C.op("act", lambda e: e.activation(out=dmf[k].t[:, s, :], in_=pq[k][:], func=AF.Abs,
                                                       bias=posk.t[:, kb:kb + 1], scale=-1.0),
                         r=[pq[k], posk], w=[dmf[k]])
                    if s == 0:
                        C.op("pool", lambda e: e.tensor_tensor(out=dmf[k].t[:, 0, :], in0=dmf[k].t[:, 0, :],
                                                               in1=mlo[:], op=ALU.add), r=[dmf[k], mlo], w=[dmf[k]])
                    elif s == 2:
                        C.op("pool", lambda e: e.tensor_tensor(out=dmf[k].t[:, 2, :], in0=dmf[k].t[:, 2, :],
                                                               in1=mhi[:], op=ALU.add), r=[dmf[k], mhi], w=[dmf[k]])
                C.op("pool", lambda e: e.tensor_copy(out=Dm[k].t[:, s0:s0 + len(kbs), :],
                                                     in_=dmf[k].t[:, s0:s0 + len(kbs), :]), r=[dmf[k]], w=[Dm[k]])

            def conv_branch(j):
                k = j % 2
                z_m1, z_0, z_p1 = zw[k]
                C.op("pool", lambda e: e.tensor_tensor(out=z_m1[:], in0=z_m1[:], in1=cw[0][:], op=ALU.mult),
                     r=[z_m1, cw[0]], w=[z_m1])
                C.op("pool", lambda e: e.tensor_tensor(out=z_0[:], in0=z_0[:], in1=cw[1][:], op=ALU.mult),
                     r=[z_0, cw[1]], w=[z_0])
                C.op("pool", lambda e: e.tensor_tensor(out=z_p1[:], in0=z_p1[:], in1=cw[2][:], op=ALU.mult),
                     r=[z_p1, cw[2]], w=[z_p1])
                C.op("dve", lambda e: e.tensor_tensor(out=z_0[:], in0=z_0[:], in1=z_m1[:], op=ALU.add),
                     r=[z_0, z_m1], w=[z_0])
                C.op("dve", lambda e: e.tensor_tensor(out=z_0[:], in0=z_0[:], in1=z_p1[:], op=ALU.add),
                     r=[z_0, z_p1], w=[z_0])
                C.op("dve", lambda e: e.tensor_tensor(out=z_0[:], in0=z_0[:], in1=bg2[k][:], op=ALU.mult),
                     r=[z_0, bg2[k]], w=[z_0])
                C.op("act", lambda e: e.activation(out=junk.t[:, 512:1024], in_=z_0[:], func=AF.Square,
                                                   accum_out=ssb[:]), r=[z_0], w=[junk, ssb])
                rstd_from_sum(ssb, 512, rb)
                C.op("dve", lambda e: e.scalar_tensor_tensor(out=yg[k].t[:, 512:1024], in0=z_0[:], scalar=rb[:, 0:1],
                                                             in1=bnb[:], op0=ALU.mult, op1=ALU.mult),
                     r=[z_0, rb, bnb], w=[yg[k]])

            hc = {"n": 0}

            def heads(j, mid=None):
                k = j % 2
                kbs = kbs_of(j)
                ns = len(kbs)
                s0 = 0 if j > 0 else 1
                slots = {}

                def front(hh):
                    hk, gq = hh // 4, hh % 4
                    n = hc["n"]
                    hc["n"] += 1
                    stb = PF[n % 4]
                    ptt = pTt[n % 4]
                    slots[hh] = ptt
                    stv = stb.t[:, 0:384].rearrange("p (s q) -> p s q", s=3)
                    for kb in kbs:
                        s = kb - j + 1
                        C.op("pe", lambda e: e.matmul(
                            stv[:, s, :], lhsT=kT.t[hk * 64:(hk + 1) * 64, kb * 128:(kb + 1) * 128],
                            rhs=qT[k].t[hk * 64:(hk + 1) * 64, gq, :], start=True, stop=False),
                            r=[kTB[kb], qT[k]], w=[stb])
                        C.op("pe", lambda e: e.matmul(
                            stv[:, s, :], lhsT=nsI.t[:, hh, :], rhs=Dm[k].t[:, s, :], start=False, stop=True),
                            r=[nsI, Dm[k]], w=[stb])
                    C.op("act", lambda e: e.activation(
                        out=ptt.t[:, s0:s0 + ns, :], in_=stv[:, s0:s0 + ns, :], func=AF.Exp, scale=0.125),
                        r=[stb], w=[ptt])

                def back(hh):
                    hk = hh // 4
                    ptt = slots[hh]
                    pob = po[hh // 4]
                    pov = pob.t[:, 0:260].rearrange("p (h d) -> p h d", h=4)
                    for n_, kb in enumerate(kbs):
                        s = kb - j + 1
                        C.op("pe", lambda e: e.matmul(
                            pov[:, hh % 4, :], lhsT=ptt.t[:, s, :], rhs=vall.t[:, kb, hk, :],
                            start=(n_ == 0), stop=(n_ == ns - 1)), r=[ptt, vB[kb]], w=[pob])

                LOOK = 3
                for hh in range(8 + LOOK):
                    if hh < 8:
                        front(hh)
                    if hh == 5 and mid is not None:
                        mid()
                    if hh >= LOOK:
                        back(hh - LOOK)

            def tail(j):
                k = j % 2
                for g2 in range(2):
                    pov = po[g2].t[:, 0:260].rearrange("p (h d) -> p h d", h=4)
                    C.op("dve", lambda e: e.tensor_tensor(
                        out=den.t[:, g2 * 4:(g2 + 1) * 4], in0=pov[:, :, 64], in1=esink.t[:, g2 * 4:(g2 + 1) * 4],
                        op=ALU.add), r=[po[g2], esink], w=[den])
                C.op("dve", lambda e: e.reciprocal(out=den[:], in_=den[:]), r=[den], w=[den])
                for g2 in range(2):
                    pov = po[g2].t[:, 0:260].rearrange("p (h d) -> p h d", h=4)
                    C.op("dve", lambda e: e.tensor_tensor(
                        out=ya.t[:, g2 * 4:(g2 + 1) * 4, :], in0=pov[:, :, 0:64],
                        in1=den.t[:, g2 * 4:(g2 + 1) * 4].unsqueeze(2).to_broadcast([128, 4, 64]), op=ALU.mult),
                        r=[po[g2], den], w=[ya])
                yaf = ya.t[:].rearrange("p h d -> p (h d)")
                C.op("act", lambda e: e.activation(out=junk.t[:, 0:512], in_=yaf, func=AF.Square, accum_out=ssa[:]),
                     r=[ya], w=[junk, ssa])
                rstd_from_sum(ssa, 512, ra)
                C.op("dve", lambda e: e.scalar_tensor_tensor(out=yg[k].t[:, 0:512], in0=yaf, scalar=ra[:, 0:1],
                                                             in1=anb[:], op0=ALU.mult, op1=ALU.mult),
                     r=[ya, ra, anb], w=[yg[k]])
                C.op("dve", lambda e: e.tensor_tensor(out=ygo[k][:], in0=yg[k][:], in1=sz2[k][:], op=ALU.mult),
                     r=[yg[k], sz2[k]], w=[ygo[k]])
                C.dma("sp", yg_s[j * 128:(j + 1) * 128, :], ygo[k][:], r=[ygo[k]], w=[ygB[j]])

            ld2(0)
            prep_attn(0)
            conv_branch(0)
            for j in range(NT):
                if j + 1 < NT:
                    ld2(j + 1)
                if hooks.get("s2_thunks"):
                    hooks["s2_thunks"].pop(0)()
                heads(j, mid=(lambda: prep_attn(j + 1)) if j + 1 < NT else None)
                tail(j)
                if j + 1 < NT:
                    conv_branch(j + 1)

      if "pre_p" in hooks:
          hooks["pre_p"]()
      post_stage(0, xsrc, xsrcB, dst, dstB, wts=hooks.get("wpost"), thunks=hooks.get("p_thunks"))

    def alloc_l1_weights(right):
        al = C.sbr if right else C.sb
        return (al([128, 8, 2464], BF16, "w_in1"), al([128, 2, 768], BF16, "w_uq"), al([128, 1024], BF16, "w_ukv"),
                al([128, 4, 128], BF16, "w_sT"))

    def load_l1_weights(ws, thunks=None):
        w1, wuq, wukv, wsT = ws
        load_weight(lambda c: w1.t[:, c, :], w_in1_d, 8, w1, thunks)
        load_weight(lambda c: wuq.t[:, c, :], w_uq_d, 2, wuq, thunks)
        g1 = lambda: C.dma("pool", wukv[:], w_ukv_d, w=[wukv])
        g2 = lambda: C.dma("pool", wsT[:], w_sT_d, w=[wsT])
        for g in (g1, g2):
            if thunks is None:
                g()
            else:
                thunks.append(g)

    def layer1(xsrc, xsrcB, dst, dstB, hooks=None):
      hooks = hooks or {}
      import math
      TWO_PI = 2.0 * math.pi
      C1 = 6.28125
      C2 = TWO_PI - C1
      PI_S = 3.1415925
      SCALE = 96.0 ** -0.5
      with C.scope():
        qT_s = dscr("qT_s", [96, 8, SEQ], BF16)
        szc_s = dscr("szc_s", [SEQ, 512])
        qTB = [Buf() for _ in range(NT)]
        szcB = [Buf() for _ in range(NT)]
        KT = C.sb([128, 8, SEQ], BF16, "KT")
        vall = C.sb([128, NT, 8, 65], BF16, "vall1")
        KTB = [Buf() for _ in range(NT)]
        vB = [Buf() for _ in range(NT)]
        C.op("pool", lambda e: e.memset(vall[:], 1.0), w=[vall] + vB)
        KTpad = Buf("KTpad")

        with C.scope():
            if "w_s1" in hooks:
                w1, wuq, wukv, wsT = hooks["w_s1"]
            else:
                w1, wuq, wukv, wsT = ws_ = alloc_l1_weights(False)
                load_l1_weights(ws_)
            bsT = C.sb([128, 4], F32, "bsT")
            C.dma("sp", bsT[:], b_sT_d, w=[bsT])

            def bc(src, n, name):
                t = C.sb([128, n], F32, name)
                C.dma("sp", t[:], src.partition_broadcast(128), w=[t])
                return t

            qnb = bc(qn_d, 256, "qnb")
            kvnb = bc(kvn_d, 128, "kvnb")
            vlgb = bc(vlg_d, 512, "vlgb")
            vlbb = bc(vlb_d, 512, "vlbb")
            dnb = bc(dn_d, 512, "dnb")
            invb = bc(invf_d, 16, "invb")
            negpi = C.sb([128, 1], F32, "negpi")
            sc = C.sb([128, NT, 32], F32, "sc")
            with C.scope():
                ang = C.sb([128, NT, 32], F32, "ang")
                kf = C.sb([128, NT, 32], F32, "kf")
                ki = C.sb([128, NT, 32], I32, "ki")
                C.op("dve", lambda e: e.tensor_tensor(out=ang.t[:, :, 0:16],
                                                      in0=posk.t[:].unsqueeze(2).to_broadcast([128, NT, 16]),
                                                      in1=invb.t[:].unsqueeze(1).to_broadcast([128, NT, 16]),
                                                      op=ALU.mult), r=[posk, invb], w=[ang])
                C.op("dve", lambda e: e.tensor_scalar(out=ang.t[:, :, 16:32], in0=ang.t[:, :, 0:16], scalar1=0.5 * math.pi,
                                                      scalar2=None, op0=ALU.add), r=[ang], w=[ang])
                C.op("dve", lambda e: e.tensor_scalar(out=kf[:], in0=ang[:], scalar1=1.0 / TWO_PI, scalar2=None,
                                                      op0=ALU.mult), r=[ang], w=[kf])
                C.op("dve", lambda e: e.tensor_copy(out=ki[:], in_=kf[:]), r=[kf], w=[ki])
                C.op("dve", lambda e: e.tensor_copy(out=kf[:], in_=ki[:]), r=[ki], w=[kf])
                C.op("dve", lambda e: e.scalar_tensor_tensor(out=ang[:], in0=kf[:], scalar=-C1, in1=ang[:],
                                                             op0=ALU.mult, op1=ALU.add), r=[kf, ang], w=[ang])
                C.op("dve", lambda e: e.scalar_tensor_tensor(out=ang[:], in0=kf[:], scalar=-C2, in1=ang[:],
                                                             op0=ALU.mult, op1=ALU.add), r=[kf, ang], w=[ang])
                C.op("dve", lambda e: e.tensor_scalar(out=ang[:], in0=ang[:], scalar1=-PI_S, scalar2=PI_S,
                                                      op0=ALU.max, op1=ALU.min), r=[ang], w=[ang])
                C.op("act", lambda e: e.activation(out=sc[:], in_=ang[:], func=AF.Sin), r=[ang], w=[sc])

            xb = [C.sb([128, DM], BF16, "xb") for _ in range(2)]
            xT = [C.sb([128, 8, 128], BF16, "xT") for _ in range(2)]
            cqn = [C.sb([128, 256], BF16, "cqn") for _ in range(2)]
            cqnT = C.sb([128, 2, 128], BF16, "cqnT")
            cn = [C.sb([128, 128], BF16, "cn") for _ in range(2)]
            cnT = C.sb([128, 128], BF16, "cnT")
            qfull = C.sb([128, 8 * 96], BF16, "qfull")
            kfull = C.sb([128, 8 * 96], BF16, "kfull")
            qTt = [C.sb([128, 8, 128], BF16, "qTt") for _ in range(2)]
            kro = [C.sb([128, 32], F32, "kro") for _ in range(2)]
            tq = [C.sb([128, 4, 16], F32, "tq") for _ in range(4)]
            tk = [C.sb([128, 16], F32, "tk") for _ in range(4)]
            gu = [C.sb([128, 512], F32, "gu") for _ in range(2)]
            gtmp2 = [C.sb([128, 512], F32, "gtmp") for _ in range(2)]
            xstg = [C.sb([128, 512], F32, "xstg") for _ in range(3)]
            gv = [C.sb([128, 512], F32, "gv") for _ in range(2)]
            vn = C.sb([128, 512], BF16, "vn")
            szd = [C.sb([128, 512], F32, "szd") for _ in range(2)]
            szc = [C.sb([128, 512], F32, "szc") for _ in range(2)]
            yd = C.sb([128, 512], F32, "yd")
            ygd = [C.sb([128, 512], BF16, "ygd") for _ in range(2)]
            st1 = {n: C.sb([128, 1], F32, n) for n in ("ssq", "rq", "ssk", "rk", "ms", "nm", "vs", "rv", "ssd", "rd")}
            groups = [(0, 416), (416, 512), (928, 512), (1440, 512), (1952, 512)]
            gbank = [PF[0], PF[1], PF[2], PF[0], PF[1]]

            def ld1(i):
                C.dma("pool", xb[i % 2][:], xsrc[i * 128:(i + 1) * 128, :], r=([xsrcB[i]] if xsrcB else []),
                      w=[xb[i % 2]])

            def silu_from(bank_, out_):
                zs = xstg[2]
                C.op("act", lambda e: e.activation(out=out_[:], in_=bank_.t[:], func=AF.Sigmoid), r=[bank_], w=[out_])
                C.op("act", lambda e: e.copy(out=zs[:], in_=bank_.t[:]), r=[bank_], w=[zs])
                C.op("dve", lambda e: e.tensor_tensor(out=out_[:], in0=out_[:], in1=zs[:], op=ALU.mult),
                     r=[out_, zs], w=[out_])

            def gelu_from(bank_, g_, which):
                gtmp = gtmp2[which]
                xs = xstg[which]
                C.op("act", lambda e: e.activation(out=gtmp[:], in_=bank_.t[:], func=AF.Square), r=[bank_], w=[gtmp])
                C.op("act", lambda e: e.copy(out=xs[:], in_=bank_.t[:]), r=[bank_], w=[xs])
                C.op("dve", lambda e: e.tensor_scalar(out=gtmp[:], in0=gtmp[:], scalar1=0.044715, scalar2=1.0,
                                                      op0=ALU.mult, op1=ALU.add), r=[gtmp], w=[gtmp])
                C.op("pool", lambda e: e.tensor_tensor(out=gtmp[:], in0=gtmp[:], in1=xs[:], op=ALU.mult),
                     r=[gtmp, xs], w=[gtmp])
                C.op("act", lambda e: e.activation(out=gtmp[:], in_=gtmp[:], func=AF.Sigmoid,
                                                   scale=1.5957691216057308), r=[gtmp], w=[gtmp])
                C.op("pool", lambda e: e.tensor_tensor(out=g_[:], in0=gtmp[:], in1=xs[:], op=ALU.mult),
                     r=[gtmp, xs], w=[g_])

            def mm_group(i, gi):
                k = i % 2
                n0, wd = groups[gi]
                bank = gbank[gi]
                for c in range(8):
                    C.op("pe", lambda e: e.matmul(bank.t[:, 0:wd], lhsT=xT[k].t[:, c, :], rhs=w1.t[:, c, n0:n0 + wd],
                                                  start=(c == 0), stop=(c == 7)), r=[xT[k], w1], w=[bank])

            def A1(i):
                k = i % 2
                if i < 8:
                    C.op("pool", lambda e: e.memset(KT.t[96:128, i, :], 0.0), w=[KTpad])
                transpose_to(xb[k], 8, 128, lambda: xT[k].t[:], xT[k], eng="act")
                mm_group(i, 0)
                mm_group(i, 1)
                mm_group(i, 2)

            def A2(i):
                k = i % 2
                g0 = gbank[0]
                C.op("act", lambda e: e.activation(out=junk.t[:, 0:256], in_=g0.t[:, 0:256], func=AF.Square,
                                                   accum_out=st1["ssq"][:]), r=[g0], w=[junk, st1["ssq"]])
                C.op("act", lambda e: e.activation(out=junk.t[:, 256:384], in_=g0.t[:, 256:384], func=AF.Square,
                                                   accum_out=st1["ssk"][:]), r=[g0], w=[junk, st1["ssk"]])
                sinb = sc.t[:, i, 0:16]
                cosb = sc.t[:, i, 16:32]
                x1 = g0.t[:, 384:400]
                x2 = g0.t[:, 400:416]
                C.op("dve", lambda e: e.tensor_tensor(out=tk[0][:], in0=x1, in1=cosb, op=ALU.mult), r=[g0, sc], w=[tk[0]])
                C.op("dve", lambda e: e.tensor_tensor(out=tk[1][:], in0=x2, in1=sinb, op=ALU.mult), r=[g0, sc], w=[tk[1]])
                C.op("dve", lambda e: e.tensor_tensor(out=tk[2][:], in0=x1, in1=sinb, op=ALU.mult), r=[g0, sc], w=[tk[2]])
                C.op("dve", lambda e: e.tensor_tensor(out=tk[3][:], in0=x2, in1=cosb, op=ALU.mult), r=[g0, sc], w=[tk[3]])
                rstd_from_sum(st1["ssq"], 256, st1["rq"])
                rstd_from_sum(st1["ssk"], 128, st1["rk"])
                C.op("dve", lambda e: e.tensor_tensor(out=kro[k].t[:, 0:16], in0=tk[0][:], in1=tk[1][:],
                                                      op=ALU.subtract), r=[tk[0], tk[1]], w=[kro[k]])
                C.op("dve", lambda e: e.tensor_tensor(out=kro[k].t[:, 16:32], in0=tk[2][:], in1=tk[3][:], op=ALU.add),
                     r=[tk[2], tk[3]], w=[kro[k]])
                C.op("dve", lambda e: e.scalar_tensor_tensor(out=cqn[k][:], in0=g0.t[:, 0:256],
                                                             scalar=st1["rq"].t[:, 0:1], in1=qnb[:], op0=ALU.mult,
                                                             op1=ALU.mult), r=[g0, st1["rq"], qnb], w=[cqn[k]])
                C.op("dve", lambda e: e.scalar_tensor_tensor(out=cn[k][:], in0=g0.t[:, 256:384],
                                                             scalar=st1["rk"].t[:, 0:1], in1=kvnb[:], op0=ALU.mult,
                                                             op1=ALU.mult), r=[g0, st1["rk"], kvnb], w=[cn[k]])

            def A2b(i):
                k = i % 2
                mm_group(i, 3)
                gelu_from(gbank[1], gu[k], 0)
                mm_group(i, 4)
                gelu_from(gbank[2], gv[k], 1)
                silu_from(gbank[3], szc[k])
                C.dma("sp", szc_s[i * 128:(i + 1) * 128, :], szc[k][:], r=[szc[k]], w=[szcB[i]])
                silu_from(gbank[4], szd[k])

            qb_ = (PF[3], PF[4])
            kvb = (PF[5], PF[3])

            def B1(i):
                k = i % 2
                transpose_to(cqn[k], 2, 128, lambda: cqnT.t[:], cqnT, eng="dve")
                transpose_to(cn[k], 1, 128, lambda: cnT.t[:].unsqueeze(1), cnT, eng="dve")
                for b2 in range(2):
                    for c in range(2):
                        C.op("pe", lambda e: e.matmul(qb_[b2].t[:, 0:384], lhsT=cqnT.t[:, c, :],
                                                      rhs=wuq.t[:, c, b2 * 384:(b2 + 1) * 384], start=(c == 0),
                                                      stop=(c == 1)), r=[cqnT, wuq], w=[qb_[b2]])
                C.op("pe", lambda e: e.matmul(kvb[0].t[:], lhsT=cnT.t[:], rhs=wukv.t[:, 0:512],
                                              start=True, stop=True), r=[cnT, wukv], w=[kvb[0]])

            def B2(i):
                k = i % 2
                qf3 = qfull.t[:].rearrange("p (h d) -> p h d", h=8)
                kf3 = kfull.t[:].rearrange("p (h d) -> p h d", h=8)
                cos4 = sc.t[:, i, 16:32].unsqueeze(1).to_broadcast([128, 4, 16])
                sin4 = sc.t[:, i, 0:16].unsqueeze(1).to_broadcast([128, 4, 16])
                gvk = gv[k]
                C.op("act", lambda e: e.activation(out=junk.t[:, 0:512], in_=gvk[:], func=AF.Copy,
                                                   accum_out=st1["ms"][:]), r=[gvk], w=[junk, st1["ms"]])
                for b2 in range(2):
                    hs = slice(b2 * 4, b2 * 4 + 4)
                    qv = qb_[b2].t[:, 0:384].rearrange("p (h d) -> p h d", h=4)
                    C.op("act", lambda e: e.copy(out=qf3[:, hs, 0:64], in_=qv[:, :, 0:64]), r=[qb_[b2]], w=[qfull])
                    C.op("dve", lambda e: e.tensor_tensor(out=tq[0][:], in0=qv[:, :, 64:80], in1=cos4, op=ALU.mult),
                         r=[qb_[b2], sc], w=[tq[0]])
                    C.op("dve", lambda e: e.tensor_tensor(out=tq[1][:], in0=qv[:, :, 80:96], in1=sin4, op=ALU.mult),
                         r=[qb_[b2], sc], w=[tq[1]])
                    C.op("dve", lambda e: e.tensor_tensor(out=tq[2][:], in0=qv[:, :, 64:80], in1=sin4, op=ALU.mult),
                         r=[qb_[b2], sc], w=[tq[2]])
                    C.op("dve", lambda e: e.tensor_tensor(out=tq[3][:], in0=qv[:, :, 80:96], in1=cos4, op=ALU.mult),
                         r=[qb_[b2], sc], w=[tq[3]])
                    C.op("dve", lambda e: e.tensor_tensor(out=qf3[:, hs, 64:80], in0=tq[0][:], in1=tq[1][:],
                                                          op=ALU.subtract), r=[tq[0], tq[1]], w=[qfull])
                    C.op("dve", lambda e: e.tensor_tensor(out=qf3[:, hs, 80:96], in0=tq[2][:], in1=tq[3][:],
                                                          op=ALU.add), r=[tq[2], tq[3]], w=[qfull])
                    if b2 == 0:
                        C.op("pe", lambda e: e.matmul(kvb[1].t[:], lhsT=cnT.t[:], rhs=wukv.t[:, 512:1024],
                                                      start=True, stop=True), r=[cnT, wukv], w=[kvb[1]])
                        C.op("dve", lambda e: e.tensor_scalar(out=st1["nm"][:], in0=st1["ms"][:], scalar1=-1.0 / 512,
                                                              scalar2=None, op0=ALU.mult), r=[st1["ms"]], w=[st1["nm"]])
                        C.op("act", lambda e: e.activation(out=gvk[:], in_=gvk[:], func=AF.Identity,
                                                           bias=st1["nm"].t[:, 0:1], scale=1.0),
                             r=[gvk, st1["nm"]], w=[gvk])
                        C.op("act", lambda e: e.activation(out=junk.t[:, 512:1024], in_=gvk[:], func=AF.Square,
                                                           accum_out=st1["vs"][:]), r=[gvk], w=[junk, st1["vs"]])
                for b2 in range(2):
                    hs = slice(b2 * 4, b2 * 4 + 4)
                    kvv = kvb[b2].t[:].rearrange("p (h d) -> p h d", h=4)
                    C.op("act", lambda e: e.copy(out=kf3[:, hs, 0:64], in_=kvv[:, :, 0:64]), r=[kvb[b2]], w=[kfull])
                    C.op("act", lambda e: e.copy(out=vall.t[:, i, hs, 0:64], in_=kvv[:, :, 64:128]), r=[kvb[b2]],
                         w=[vB[i]])
                C.op("dve", lambda e: e.tensor_copy(out=kf3[:, :, 64:96],
                                                    in_=kro[k].t[:].unsqueeze(1).to_broadcast([128, 8, 32])),
                     r=[kro[k]], w=[kfull])
                rstd_from_sum(st1["vs"], 512, st1["rv"])
                C.op("dve", lambda e: e.scalar_tensor_tensor(out=gvk[:], in0=gvk[:], scalar=st1["rv"].t[:, 0:1],
                                                             in1=vlgb[:], op0=ALU.mult, op1=ALU.mult),
                     r=[gvk, st1["rv"], vlgb], w=[gvk])
                C.op("dve", lambda e: e.tensor_tensor(out=vn[:], in0=gvk[:], in1=vlbb[:], op=ALU.add),
                     r=[gvk, vlbb], w=[vn])

            def B3(i):
                k = i % 2
                guk = gu[k]
                transpose_to(qfull, 8, 96, lambda: qTt[k].t[0:96, :, :], qTt[k], eng="act")
                C.dma("sp", qT_s[:, :, i * 128:(i + 1) * 128], qTt[k].t[0:96, :, :], r=[qTt[k]], w=[qTB[i]])
                transpose_to(kfull, 8, 96, lambda: KT.t[0:96, :, i * 128:(i + 1) * 128], KTB[i], eng="dve")
                mixb = PF[4]
                for g4 in range(4):
                    C.op("pe", lambda e: e.matmul(mixb.t[:, g4 * 128:(g4 + 1) * 128], lhsT=wsT.t[:, g4, :],
                                                  rhs=vn.t[:, g4 * 128:(g4 + 1) * 128], start=True, stop=True),
                         r=[wsT, vn], w=[mixb])
                for g4 in range(4):
                    sl = slice(g4 * 128, (g4 + 1) * 128)
                    C.op("dve", lambda e: e.scalar_tensor_tensor(out=yd.t[:, sl], in0=mixb.t[:, sl],
                                                                 scalar=bsT.t[:, g4:g4 + 1], in1=guk.t[:, sl],
                                                                 op0=ALU.add, op1=ALU.mult),
                         r=[mixb, bsT, guk], w=[yd])
                C.op("act", lambda e: e.activation(out=junk.t[:, 0:512], in_=yd[:], func=AF.Square,
                                                   accum_out=st1["ssd"][:]), r=[yd], w=[junk, st1["ssd"]])
                rstd_from_sum(st1["ssd"], 512, st1["rd"])
                C.op("dve", lambda e: e.scalar_tensor_tensor(out=yd[:], in0=yd[:], scalar=st1["rd"].t[:, 0:1],
                                                             in1=dnb[:], op0=ALU.mult, op1=ALU.mult),
                     r=[yd, st1["rd"], dnb], w=[yd])
                C.op("dve", lambda e: e.tensor_tensor(out=ygd[k][:], in0=yd[:], in1=szd[k][:], op=ALU.mult),
                     r=[yd, szd[k]], w=[ygd[k]])
                C.dma("sp", yg_s[i * 128:(i + 1) * 128, 512:1024], ygd[k][:], r=[ygd[k]], w=[ygB[i]])

            ld1(0)
            ld1(1)
            A1(0)
            A2(0)
            A2b(0)
            for i in range(NT):
                if i + 2 < NT:
                    ld1(i + 2)
                B1(i)
                if i + 1 < NT:
                    A1(i + 1)
                if i + 1 < NT:
                    A2(i + 1)
                B2(i)
                if i + 1 < NT:
                    A2b(i + 1)
                B3(i)

        if "pre_s2" in hooks:
            hooks["pre_s2"]()
        with C.scope():
            cnb = C.sb([128, 512], F32, "cnb")
            C.dma("sp", cnb[:], cn_d.partition_broadcast(128), w=[cnb])
            qTg = [C.sb([128, 8, 512], BF16, "qTg") for _ in range(2)]
            ptt = [C.sb([128, 512], BF16, "ptt") for _ in range(4)]
            ycg = C.sb([128, 4, 512], F32, "ycg")
            szc2 = [C.sb([128, 512], F32, "szc2") for _ in range(2)]
            ygc = [C.sb([128, 512], BF16, "ygc") for _ in range(2)]
            rden = C.sb([128, 4], F32, "rden")
            ssc = C.sb([128, 1], F32, "ssc")
            rc = C.sb([128, 1], F32, "rc")
            NG = SEQ // 512

            def ldq(g8):
                C.dma("sp", qTg[g8 % 2].t[0:96, :, :], qT_s[:, :, g8 * 512:(g8 + 1) * 512],
                      r=[qTB[g8 * 4 + t4] for t4 in range(4)], w=[qTg[g8 % 2]])

            for qg_ in qTg:
                C.op("pool", lambda e: e.memset(qg_[:], 0.0), w=[qg_])
            ldq(0)
            for g_ in hooks.get("s2_thunks", []):
                g_()
            iters = [(g8, hh, kt) for g8 in range(NG) for hh in range(8) for kt in range(NT)]
            LOOK = 3

            def front(n):
                g8, hh, kt = iters[n]
                if hh == 0 and kt == 0 and g8 + 1 < NG:
                    ldq(g8 + 1)
                qg = qTg[g8 % 2]
                stb = PF[n % 4]
                pt_ = ptt[n % 4]
                C.op("pe", lambda e: e.matmul(stb.t[:], lhsT=KT.t[:, hh, kt * 128:(kt + 1) * 128],
                                              rhs=qg.t[:, hh, :], start=True, stop=True),
                     r=[KTB[kt], KTpad, qg], w=[stb])
                C.op("act", lambda e: e.activation(out=pt_[:], in_=stb.t[:], func=AF.Exp, scale=SCALE),
                     r=[stb], w=[pt_])

            def back(n):
                g8, hh, kt = iters[n]
                pt_ = ptt[n % 4]
                pob = PF[4 + hh % 2]
                pov = pob.t[:, 0:260].rearrange("p (q d) -> p q d", q=4)
                for qi in range(4):
                    C.op("pe", lambda e: e.matmul(pov[:, qi, :], lhsT=pt_.t[:, qi * 128:(qi + 1) * 128],
                                                  rhs=vall.t[:, kt, hh, :], start=(kt == 0 and qi == 0),
                                                  stop=(kt == NT - 1), skip_group_check=True),
                         r=[pt_, vB[kt]], w=[pob])
                if kt != NT - 1:
                    return
                C.op("dve", lambda e: e.reciprocal(out=rden[:], in_=pov[:, :, 64]), r=[pob], w=[rden])
                C.op("dve", lambda e: e.tensor_tensor(out=ycg.t[:, :, hh * 64:(hh + 1) * 64], in0=pov[:, :, 0:64],
                                                      in1=rden.t[:].unsqueeze(2).to_broadcast([128, 4, 64]),
                                                      op=ALU.mult), r=[pob, rden], w=[ycg])
                if hh != 7:
                    return
                for qi in range(4):
                    ti = g8 * 4 + qi
                    k = ti % 2
                    C.dma("sp", szc2[k][:], szc_s[ti * 128:(ti + 1) * 128, :], r=[szcB[ti]], w=[szc2[k]])
                    C.op("act", lambda e: e.activation(out=junk.t[:, 0:512], in_=ycg.t[:, qi, :], func=AF.Square,
                                                       accum_out=ssc[:]), r=[ycg], w=[junk, ssc])
                    rstd_from_sum(ssc, 512, rc)
                    C.op("dve", lambda e: e.scalar_tensor_tensor(out=ycg.t[:, qi, :], in0=ycg.t[:, qi, :],
                                                                 scalar=rc.t[:, 0:1], in1=cnb[:], op0=ALU.mult,
                                                                 op1=ALU.mult), r=[ycg, rc, cnb], w=[ycg])
                    C.op("dve", lambda e: e.tensor_tensor(out=ygc[k][:], in0=ycg.t[:, qi, :], in1=szc2[k][:],
                                                          op=ALU.mult), r=[ycg, szc2[k]], w=[ygc[k]])
                    C.dma("sp", yg_s[ti * 128:(ti + 1) * 128, 0:512], ygc[k][:], r=[ygc[k]], w=[ygB[ti]])

            for n in range(len(iters) + LOOK):
                if n < len(iters):
                    front(n)
                if n >= LOOK:
                    back(n - LOOK)
      post_stage(1, xsrc, xsrcB, dst, dstB, wts=hooks.get("wpost"))

    if layers == (0,):
        layer0(x_in, None, out_d, None)
    elif layers == (1,):
        layer1(x_in, None, out_d, None)
    else:
        pre = {}

        def l0_pre_s2():
            pre["w_s1"] = alloc_l1_weights(True)
            pre["wpost0"] = alloc_post_weights(True)
            h0["wpost"] = pre["wpost0"]
            h0["s2_thunks"] = []
            load_post_weights(0, pre["wpost0"], h0["s2_thunks"])

        def l0_pre_p():
            while h0["s2_thunks"]:
                h0["s2_thunks"].pop(0)()
            h0["p_thunks"] = []
            load_l1_weights(pre["w_s1"], h0["p_thunks"])

        h0 = {"pre_s2": l0_pre_s2, "pre_p": l0_pre_p}
        layer0(x_in, None, x_mid, xmidB, hooks=h0)
        while h0.get("p_thunks"):
            h0["p_thunks"].pop(0)()
        C.rpop(1)
        h1 = {"w_s1": pre["w_s1"]}

        def l1_pre_s2():
            C.rpop(4)
            pre["wpost1"] = alloc_post_weights(True)
            h1["wpost"] = pre["wpost1"]
            h1["s2_thunks"] = []
            load_post_weights(1, pre["wpost1"], h1["s2_thunks"])

        h1["pre_s2"] = l1_pre_s2
        layer1(x_mid, xmidB, out_d, None, hooks=h1)
        C.rpop(1)
    C.S.emit()
    nc.dbg_names = dbg_names
    return nc


def _host_inputs(inputs, layers):
    f = lambda a: np.ascontiguousarray(np.asarray(a))
    x = f(inputs["x"])
    p = f(inputs["p"])
    pos = f(inputs["positions"]).astype(np.int32)
    shared = {}
    for l in layers:
        shared[f"ln_g{l}"] = f(inputs["post_ln_g"][l])
        shared[f"ln_b{l}"] = f(inputs["post_ln_b"][l])
        shared[f"proj{l}"] = f(inputs["ple_proj"][l])
        shared[f"gate{l}"] = f(inputs["ple_gate"][l])
    if 0 in layers:
        w = f(inputs["ev_w_in"][0])
        qcols = w[:, 0:512].reshape(1024, 8, 64)
        order = [0, 4, 1, 5, 2, 6, 3, 7]
        wq = qcols[:, order, :].reshape(1024, 512)
        shared["w_in0"] = f(np.concatenate([wq, w[:, 512:]], axis=1))
        shared["conv_w"] = f(inputs["ev_conv_w"][0])
        shared["sink"] = f(inputs["ev_sink"][0])
        shared["a_norm"] = f(inputs["ev_a_norm"][0])
        shared["b_norm"] = f(inputs["ev_b_norm"][0])
        shared["w_out0"] = f(inputs["ev_w_out"][0])
    if 1 in layers:
        shared["w_in1"] = f(inputs["od_w_in"][0])
        shared["w_uq"] = f(inputs["od_w_uq"][0])
        shared["w_ukv"] = f(inputs["od_w_ukv"][0])
        shared["w_sT"] = f(np.transpose(np.asarray(inputs["od_w_s"][0]), (2, 0, 1)))
        shared["b_sT"] = f(np.transpose(np.asarray(inputs["od_b_s"][0]), (1, 0)))
        shared["q_norm"] = f(inputs["od_q_norm"][0])
        shared["kv_norm"] = f(inputs["od_kv_norm"][0])
        shared["v_ln_g"] = f(inputs["od_v_ln_g"][0])
        shared["v_ln_b"] = f(inputs["od_v_ln_b"][0])
        shared["c_norm"] = f(inputs["od_c_norm"][0])
        shared["d_norm"] = f(inputs["od_d_norm"][0])
        shared["w_out1"] = f(inputs["od_w_out"][0])
        half = 16
        shared["inv_freq"] = (10000.0 ** (-np.arange(half, dtype=np.float32) / half)).astype(np.float32)
    maps = []
    for b in range(x.shape[0]):
        m = dict(shared)
        m["x"] = f(x[b])
        for l in layers:
            m[f"p{l}"] = f(p[l, b])
        m["pos"] = f(pos[b])
        m["posT"] = f(pos[b].reshape(NT, 128).T)
        maps.append(m)
    return maps


_CACHE = {}


def _run(layers, inputs, ncores=NCORES, debug=False):
    key = (layers, debug)
    if key not in _CACHE:
        _CACHE[key] = build(layers, debug)
    nc = _CACHE[key]
    maps = _host_inputs(inputs, layers)[:ncores]
    res = run_bass_kernel_spmd(nc, maps, core_ids=list(range(len(maps))))
    if debug:
        return res.results
    return np.stack([r["out"] for r in res.results], axis=0)


FUSED = True


def kernel(**inputs):
    if FUSED:
        out = _run((0, 1), inputs)
    else:
        x1 = _run((0,), inputs)
        inputs1 = dict(inputs)
        inputs1["x"] = x1
        out = _run((1,), inputs1)
    return np.ascontiguousarray(out, dtype=np.float32)
```

```python
import contextlib
import numpy as np
import concourse.bass as bass
import concourse.mybir as mybir
from concourse.bass_utils import run_bass_kernel_spmd

F32 = mybir.dt.float32
BF16 = mybir.dt.bfloat16
I32 = mybir.dt.int32
AF = mybir.ActivationFunctionType
ALU = mybir.AluOpType

NCORES = 8
SEQ = 4096
DM = 1024
NT = SEQ // 128
ALPHA = 4.0 ** 0.25
EPS = 1e-6
BIG = 30000.0

ENG_NAMES = ("pe", "act", "dve", "pool", "sp")
EPOCH = 30000
INLINE_WAIT = True


class Buf:
    __slots__ = ("name", "writers", "readers", "excl")

    def __init__(self, name="", excl=False):
        self.name = name
        self.writers = []
        self.readers = []
        self.excl = excl


class Op:
    __slots__ = ("eng", "fn", "deps", "is_dma", "sig", "seq", "dsem", "dval")

    def __init__(self, eng, fn, is_dma):
        self.eng = eng
        self.fn = fn
        self.deps = []
        self.is_dma = is_dma
        self.sig = False
        self.seq = 0
        self.dsem = None
        self.dval = 0


class _Rec:
    def __init__(self):
        self.call = None

    def __getattr__(self, name):
        def f(*a, **k):
            self.call = (name, a, k)
        return f


def _replay(call):
    name, a, k = call
    return lambda e: getattr(e, name)(*a, **k)


class Sched:
    def __init__(self, nc, n_dma_sems=24):
        self.nc = nc
        self.ops = {e: [] for e in ENG_NAMES}
        self.n_dma_sems = n_dma_sems

    def op(self, eng, fn, reads=(), writes=(), dma=False):
        rec = _Rec()
        fn(rec)
        o = Op(eng, _replay(rec.call), dma)
        deps = []
        for b in reads:
            deps.extend(b.writers)
            if b.excl:
                deps.extend(r for r in b.readers if r.eng != eng)
        for b in writes:
            for r in b.readers:
                if r.is_dma or dma or r.eng != eng or eng != "pe":
                    deps.append(r)
            for w in b.writers:
                if w.is_dma or dma or w.eng != eng or eng != "pe":
                    deps.append(w)
        seen = set()
        for d in deps:
            if id(d) not in seen and d is not o:
                seen.add(id(d))
                o.deps.append(d)
                d.sig = True
        for b in reads:
            b.readers.append(o)
        for b in writes:
            if b.readers:
                b.readers = [r for r in b.readers if r is o]
                b.writers = [o]
            else:
                b.writers.append(o)
        self.ops[eng].append(o)
        return o

    def barrier(self):
        lasts = []
        for e in ENG_NAMES:
            nd = [o for o in self.ops[e] if not o.is_dma]
            if nd:
                lasts.append(nd[-1])
            dl = [o for o in self.ops[e] if o.is_dma]
            lasts.extend(dl[-self.n_dma_sems:])
        for e in ENG_NAMES:
            o = Op(e, lambda eng: eng.nop(), False)
            for d in lasts:
                if d.is_dma or d.eng != e:
                    o.deps.append(d)
                    d.sig = True
            self.ops[e].append(o)

    def emit(self):
        nc = self.nc
        with contextlib.ExitStack() as es:
            eng_sems = {}
            for e in ENG_NAMES:
                n = 0
                for o in self.ops[e]:
                    if not o.is_dma and o.sig:
                        n += 1
                        o.seq = n
                nep = max((n + EPOCH - 1) // EPOCH, 1)
                eng_sems[e] = [es.enter_context(nc.semaphore(f"s_{e}_{k}")) for k in range(nep)]
            for e in ENG_NAMES:
                dl = [o for o in self.ops[e] if o.is_dma]
                if not dl:
                    continue
                pool = [es.enter_context(nc.semaphore(f"d_{e}_{k}")) for k in range(self.n_dma_sems)]
                cnt = [0] * len(pool)
                for k, o in enumerate(dl):
                    j = k % len(pool)
                    cnt[j] += 1
                    o.dsem = pool[j]
                    o.dval = 16 * cnt[j]
            block = es.enter_context(nc.Block())

            def run_engine(ename, engobj):
                waited = {}

                def need(sem, val):
                    if waited.get(sem.num, 0) >= val:
                        return
                    waited[sem.num] = val
                    engobj.wait_ge(sem, val)

                for o in self.ops[ename]:
                    pend = []

                    def need2(sem, val):
                        if waited.get(sem.num, 0) >= val:
                            return
                        waited[sem.num] = val
                        pend.append((sem, val))

                    for d in o.deps:
                        if d.is_dma:
                            need2(d.dsem, d.dval)
                        else:
                            k = (d.seq - 1) // EPOCH
                            need2(eng_sems[d.eng][k], d.seq - k * EPOCH)
                    if o.is_dma and o.dval > 16:
                        need2(o.dsem, o.dval - 16)
                    inline = None
                    if pend and INLINE_WAIT:
                        inline = pend.pop()
                    for sem, val in pend:
                        engobj.wait_ge(sem, val)
                    ins = o.fn(engobj)
                    if inline is not None:
                        ins._wait_ge(inline[0], inline[1])
                    if o.is_dma:
                        ins.then_inc(o.dsem, 16)
                    elif o.sig:
                        k = (o.seq - 1) // EPOCH
                        ins.then_inc(eng_sems[ename][k], 1)
                last = {}
                for o in self.ops[ename]:
                    if o.is_dma:
                        last[o.dsem.num] = (o.dsem, o.dval)
                for sem, val in last.values():
                    need(sem, val)

            block.tensor(lambda eng: run_engine("pe", eng))
            block.scalar(lambda eng: run_engine("act", eng))
            block.vector(lambda eng: run_engine("dve", eng))
            block.gpsimd(lambda eng: run_engine("pool", eng))
            block.sync(lambda eng: run_engine("sp", eng))


class T:
    def __init__(self, t, name):
        self.t = t
        self.b = Buf(name)

    def __getitem__(self, k):
        return self.t[k]


def _bufs(xs):
    out = []
    for x in xs:
        if x is None:
            continue
        out.append(x.b if isinstance(x, T) else x)
    return out


class Ctx:
    def __init__(self, nc):
        self.nc = nc
        self.S = Sched(nc)
        self.n = 0
        self.stack = [contextlib.ExitStack()]
        self.rguards = []

    def sb(self, shape, dt, name=None):
        self.n += 1
        name = f"{name or 'sb'}_{self.n}"
        t = self.stack[-1].enter_context(self.nc.sbuf_tensor(name, list(shape), dt))
        return T(t, name)

    def sbr(self, shape, dt, name=None):
        self.n += 1
        name = f"{name or 'sbr'}_{self.n}"
        g = self.nc.sbuf_tensor(name, list(shape), dt, side="right")
        t = g.__enter__()
        self.rguards.append(g)
        return T(t, name)

    def rpop(self, n):
        for _ in range(n):
            self.rguards.pop().__exit__(None, None, None)

    @contextlib.contextmanager
    def scope(self):
        es = contextlib.ExitStack()
        self.stack.append(es)
        try:
            yield
        finally:
            self.S.barrier()
            self.stack.pop()
            es.close()

    def ps(self, shape, dt, name=None):
        self.n += 1
        name = f"{name or 'ps'}_{self.n}"
        t = T(self.nc.alloc_psum_tensor(name, list(shape), dt), name)
        t.b.excl = True
        return t

    def op(self, eng, fn, r=(), w=()):
        return self.S.op(eng, fn, _bufs(r), _bufs(w))

    def dma(self, q, out, in_, r=(), w=()):
        return self.S.op(q, lambda e: e.dma_start(out=out, in_=in_), _bufs(r), _bufs(w), dma=True)


def build(layers, debug=False):
    nc = bass.Bass("TRN2", target_bir_lowering=False)
    C = Ctx(nc)
    dbg_names = []

    def din(name, shape, dt=F32):
        return nc.dram_tensor(name, list(shape), dt, kind="ExternalInput").ap()

    def dscr(name, shape, dt=F32):
        if debug:
            dbg_names.append(name)
            return nc.dram_tensor(name, list(shape), dt, kind="ExternalOutput").ap()
        return nc.dram_tensor(name, list(shape), dt, kind="Internal").ap()

    def dump(name, tt, shape, dt=F32):
        if not debug:
            return
        dbg_names.append(name)
        o = nc.dram_tensor(name, list(shape), dt, kind="ExternalOutput").ap()
        C.dma("sp", o, tt[:], r=[tt])

    x_in = din("x", [SEQ, DM])
    out_d = nc.dram_tensor("out", [SEQ, DM], F32, kind="ExternalOutput").ap()
    pos_d = din("pos", [SEQ], I32)
    posT_d = din("posT", [128, NT], I32)
    p_d = {l: din(f"p{l}", [SEQ, 256]) for l in layers}
    lng_d = {l: din(f"ln_g{l}", [DM]) for l in layers}
    lnb_d = {l: din(f"ln_b{l}", [DM]) for l in layers}
    wout_d = {l: din(f"w_out{l}", [DM, DM]) for l in layers}
    gate_d = {l: din(f"gate{l}", [DM, DM]) for l in layers}
    proj_d = {l: din(f"proj{l}", [256, DM]) for l in layers}
    if 0 in layers:
        w_in0_d = din("w_in0", [DM, 3328])
        convw_d = din("conv_w", [3, 512])
        sink_d = din("sink", [8])
        anorm_d = din("a_norm", [512])
        bnorm_d = din("b_norm", [512])
    if 1 in layers:
        w_in1_d = din("w_in1", [DM, 2464])
        w_uq_d = din("w_uq", [256, 768])
        w_ukv_d = din("w_ukv", [128, 1024])
        w_sT_d = din("w_sT", [128, 4, 128])
        b_sT_d = din("b_sT", [128, 4])
        qn_d = din("q_norm", [256])
        kvn_d = din("kv_norm", [128])
        vlg_d = din("v_ln_g", [512])
        vlb_d = din("v_ln_b", [512])
        cn_d = din("c_norm", [512])
        dn_d = din("d_norm", [512])
        invf_d = din("inv_freq", [16])
    x_mid = dscr("x_mid", [SEQ, DM]) if len(layers) == 2 else None

    yg_s = dscr("yg_s", [SEQ, DM], BF16)
    ygB = [Buf(f"yg{i}") for i in range(NT)]
    xmidB = [Buf(f"xm{i}") for i in range(NT)]

    ident_f = C.sb([128, 128], F32, "identf")
    ident = C.sb([128, 128], BF16, "ident")
    eps_t = C.sb([128, 1], F32, "eps")
    posk_i = C.sb([128, NT], I32, "poski")
    posk = C.sb([128, NT], F32, "posk")
    junk = C.sb([128, DM], BF16, "junk")
    PT = [C.ps([128, 1024], BF16, "pt") for _ in range(2)]
    PF = [C.ps([128, 512], F32, "pf") for _ in range(6)]
    st = {"pt": 0}

    def next_pt():
        st["pt"] ^= 1
        return PT[st["pt"]]

    C.op("pool", lambda e: e.memset(ident_f[:], 0.0), w=[ident_f])
    C.op("pool", lambda e: e.affine_select(out=ident_f[:], in_=ident_f[:], pattern=[[-1, 128]],
                                           compare_op=ALU.not_equal, fill=1.0, base=0,
                                           channel_multiplier=1), r=[ident_f], w=[ident_f])
    C.op("dve", lambda e: e.tensor_copy(out=ident[:], in_=ident_f[:]), r=[ident_f], w=[ident])
    mlo = C.sb([128, 128], F32, "mlo")
    mhi = C.sb([128, 128], F32, "mhi")
    C.op("pool", lambda e: e.memset(mlo[:], 0.0), w=[mlo])
    C.op("pool", lambda e: e.memset(mhi[:], 0.0), w=[mhi])
    C.op("pool", lambda e: e.affine_select(out=mlo[:], in_=mlo[:], pattern=[[-1, 128]], compare_op=ALU.is_ge,
                                           fill=BIG, base=0, channel_multiplier=1), r=[mlo], w=[mlo])
    C.op("pool", lambda e: e.affine_select(out=mhi[:], in_=mhi[:], pattern=[[1, 128]], compare_op=ALU.is_ge,
                                           fill=BIG, base=0, channel_multiplier=-1), r=[mhi], w=[mhi])
    C.op("pool", lambda e: e.memset(eps_t[:], EPS), w=[eps_t])
    negh = C.sb([128, 1], F32, "negh")
    C.op("pool", lambda e: e.memset(negh[:], -0.5), w=[negh])
    C.dma("sp", posk_i[:], posT_d, w=[posk_i])
    C.op("dve", lambda e: e.tensor_copy(out=posk[:], in_=posk_i[:]), r=[posk_i], w=[posk])

    def rstd_from_sum(ssum, n, out):
        C.op("dve", lambda e: e.tensor_scalar(out=out[:], in0=ssum[:], scalar1=1.0 / n, scalar2=EPS, op0=ALU.mult,
                                              op1=ALU.add), r=[ssum], w=[out])
        C.op("pool", lambda e: e.tensor_tensor(out=out[:], in0=out[:], in1=negh[:], op=ALU.pow), r=[out, negh], w=[out])

    def transpose_to(src, nchunk, width, dst_ap_fn, dst, eng="act", rows=128):
        pt = next_pt()
        ptv = pt.t[:].rearrange("p (c t) -> p c t", t=128)
        for c in range(nchunk):
            C.op("pe", lambda e, c=c: e.transpose(out=ptv[0:width, c, :], in_=src[:, c * width:(c + 1) * width],
                                                  identity=ident[:]), r=[src, ident], w=[pt])
        if eng == "act":
            C.op("act", lambda e: e.copy(out=dst_ap_fn(), in_=ptv[0:width, 0:nchunk, :]), r=[pt], w=[dst])
        else:
            C.op(eng, lambda e: e.tensor_copy(out=dst_ap_fn(), in_=ptv[0:width, 0:nchunk, :]), r=[pt], w=[dst])

    def load_weight(dst_ap_fn, src_ap, nchunks, dstT, thunks=None):
        for c in range(nchunks):
            def go(c=c):
                C.dma("pool", dst_ap_fn(c), src_ap[c * 128:(c + 1) * 128, :], w=[dstT])
            if thunks is None:
                go()
            else:
                thunks.append(go)

    def alloc_post_weights(right):
        return (C.sbr if right else C.sb)([128, 8, 3072], BF16, "wpost")

    def load_post_weights(l, wbig, thunks=None):
        load_weight(lambda c: wbig.t[:, c, 0:1024], wout_d[l], 8, wbig, thunks)
        load_weight(lambda c: wbig.t[:, c, 1024:2048], gate_d[l], 8, wbig, thunks)
        load_weight(lambda c: wbig.t[:, c, 2048:3072], proj_d[l], 2, wbig, thunks)

    def post_stage(l, xsrc, xsrcB, dst, dstB, wts=None, thunks=None):
      with C.scope():
        if wts is None:
            wbig = alloc_post_weights(False)
            load_post_weights(l, wbig)
        else:
            wbig = wts
        lng = C.sb([128, DM], F32, "lng")
        lnb = C.sb([128, DM], F32, "lnb")
        C.dma("sp", lng[:], lng_d[l].partition_broadcast(128), w=[lng])
        C.dma("sp", lnb[:], lnb_d[l].partition_broadcast(128), w=[lnb])
        ygb = [C.sb([128, DM], BF16, "ygb") for _ in range(2)]
        ygT = [C.sb([128, 8, 128], BF16, "ygT") for _ in range(2)]
        xr = [C.sb([128, DM], F32, "xr") for _ in range(2)]
        pb = [C.sb([128, 256], BF16, "pb") for _ in range(2)]
        pT = [C.sb([128, 2, 128], BF16, "pT") for _ in range(3)]
        junk2 = C.sb([128, DM], BF16, "junk2")
        s = [C.sb([128, DM], F32, "s") for _ in range(2)]
        h2 = [C.sb([128, DM], F32, "h") for _ in range(2)]
        hb2 = [C.sb([128, DM], BF16, "hb") for _ in range(2)]
        hT2 = [C.sb([128, 8, 128], BF16, "hT") for _ in range(2)]
        sg = [C.sb([128, DM], F32, "sg") for _ in range(2)]
        msum = C.sb([128, 1], F32, "msum")
        nmean = C.sb([128, 1], F32, "nmean")
        vsum = C.sb([128, 1], F32, "vsum")
        rstd = C.sb([128, 1], F32, "rstd")
        py = (PF[0], PF[1])
        pg = (PF[2], PF[3])
        pp = (PF[4], PF[5])

        def loads(i):
            k = i % 2
            C.dma("sp", ygb[k][:], yg_s[i * 128:(i + 1) * 128, :], r=[ygB[i]], w=[ygb[k]])
            C.dma("sp", xr[k][:], xsrc[i * 128:(i + 1) * 128, :], r=([xsrcB[i]] if xsrcB else []), w=[xr[k]])
            C.dma("pool", pb[k][:], p_d[l][i * 128:(i + 1) * 128, :], w=[pb[k]])

        def phase1(i):
            k = i % 2
            transpose_to(ygb[k], 8, 128, lambda: ygT[k].t[:], ygT[k], eng="act")
            transpose_to(pb[k], 2, 128, lambda: pT[i % 3].t[:], pT[i % 3], eng="dve")
            for hf in range(2):
                for c in range(8):
                    C.op("pe", lambda e: e.matmul(py[hf].t[:], lhsT=ygT[k].t[:, c, :],
                                                  rhs=wbig.t[:, c, hf * 512:(hf + 1) * 512],
                                                  start=(c == 0), stop=(c == 7)),
                         r=[ygT[k], wbig], w=[py[hf]])
            for hf in range(2):
                C.op("dve", lambda e: e.scalar_tensor_tensor(
                    out=s[k].t[:, hf * 512:(hf + 1) * 512], in0=xr[k].t[:, hf * 512:(hf + 1) * 512], scalar=ALPHA,
                    in1=py[hf].t[:], op0=ALU.mult, op1=ALU.add), r=[xr[k], py[hf]], w=[s[k]])

        def phase2a(i):
            k = i % 2
            sk = s[k]
            h, hb = h2[k], hb2[k]
            C.op("act", lambda e: e.activation(out=junk[:], in_=sk[:], func=AF.Copy, accum_out=msum[:]),
                 r=[sk], w=[junk, msum])
            yield
            C.op("dve", lambda e: e.tensor_scalar(out=nmean[:], in0=msum[:], scalar1=-1.0 / DM, scalar2=None,
                                                  op0=ALU.mult), r=[msum], w=[nmean])
            C.op("act", lambda e: e.activation(out=sk[:], in_=sk[:], func=AF.Identity, bias=nmean[:, 0:1], scale=1.0),
                 r=[sk, nmean], w=[sk])
            yield
            C.op("act", lambda e: e.activation(out=junk2[:], in_=sk[:], func=AF.Square, accum_out=vsum[:]),
                 r=[sk], w=[junk2, vsum])
            yield
            rstd_from_sum(vsum, DM, rstd)
            yield
            C.op("dve", lambda e: e.scalar_tensor_tensor(out=h[:], in0=sk[:], scalar=rstd[:, 0:1], in1=lng[:],
                                                         op0=ALU.mult, op1=ALU.mult), r=[sk, rstd, lng], w=[h])
            yield
            C.op("dve", lambda e: e.tensor_tensor(out=hb[:], in0=h[:], in1=lnb[:], op=ALU.add), r=[h, lnb], w=[hb])
            C.op("dve", lambda e: e.tensor_tensor(out=h[:], in0=h[:], in1=lnb[:], op=ALU.add), r=[h, lnb], w=[h])
            yield

        def phase2b(i):
            k = i % 2
            h, hb, hT = h2[k], hb2[k], hT2[k]
            pTk = pT[i % 3]
            transpose_to(hb, 8, 128, lambda: hT.t[:], hT, eng="act")
            yield
            for hf in range(2):
                for c in range(8):
                    C.op("pe", lambda e: e.matmul(pg[hf].t[:], lhsT=hT.t[:, c, :],
                                                  rhs=wbig.t[:, c, 1024 + hf * 512:1024 + (hf + 1) * 512],
                                                  start=(c == 0), stop=(c == 7)),
                         r=[hT, wbig], w=[pg[hf]])
                for c in range(2):
                    C.op("pe", lambda e: e.matmul(pp[hf].t[:], lhsT=pTk.t[:, c, :],
                                                  rhs=wbig.t[:, c, 2048 + hf * 512:2048 + (hf + 1) * 512],
                                                  start=(c == 0), stop=(c == 1)),
                         r=[pTk, wbig], w=[pp[hf]])
                yield
            for hf in range(2):
                sl = slice(hf * 512, (hf + 1) * 512)
                C.op("act", lambda e: e.activation(out=sg[k].t[:, sl], in_=pg[hf].t[:], func=AF.Sigmoid),
                     r=[pg[hf]], w=[sg[k]])
                yield
                C.op("dve", lambda e: e.tensor_tensor(out=sg[k].t[:, sl], in0=sg[k].t[:, sl], in1=pp[hf].t[:],
                                                      op=ALU.mult), r=[sg[k], pp[hf]], w=[sg[k]])
                yield
            C.op("pool", lambda e: e.tensor_tensor(out=sg[k].t[:], in0=sg[k].t[:], in1=h[:], op=ALU.add),
                 r=[sg[k], h], w=[sg[k]])
            C.dma("sp", dst[i * 128:(i + 1) * 128, :], sg[k].t[:], r=[sg[k]], w=([dstB[i]] if dstB else []))
            yield

        def interleave(*gens):
            gens = [g for g in gens if g is not None]
            while gens:
                for g in list(gens):
                    try:
                        next(g)
                    except StopIteration:
                        gens.remove(g)

        loads(0)
        loads(1)
        phase1(0)
        for i in range(NT + 1):
            if i + 2 < NT:
                loads(i + 2)
            if thunks:
                thunks.pop(0)()
            interleave(phase2a(i) if i < NT else None, phase2b(i - 1) if i >= 1 else None)
            if i + 1 < NT:
                phase1(i + 1)

    def layer0(xsrc, xsrcB, dst, dstB, hooks=None):
      hooks = hooks or {}
      with C.scope():
        q_s = dscr("q_s", [SEQ, 512], BF16)
        zp_s = dscr("zp_s", [SEQ + 2, 512])
        bg_s = dscr("bg_s", [SEQ, 512])
        sz_s = dscr("sz_s", [SEQ, DM])
        qB = [Buf() for _ in range(NT)]
        zpB = [Buf() for _ in range(NT)]
        zpadB = Buf()
        bgB = [Buf() for _ in range(NT)]
        szB = [Buf() for _ in range(NT)]

        kT = C.sb([128, SEQ], BF16, "kT")
        vall = C.sb([128, NT, 2, 65], BF16, "vall")
        kTB = [Buf() for _ in range(NT)]
        vB = [Buf() for _ in range(NT)]
        C.op("pool", lambda e: e.memset(vall[:], 1.0), w=[vall] + vB)
        zero = C.sb([128, 512], F32, "zero")
        C.op("pool", lambda e: e.memset(zero[:], 0.0), w=[zero])
        C.dma("sp", zp_s[0:1, :], zero.t[0:1, :], r=[zero], w=[zpadB])
        C.dma("sp", zp_s[SEQ + 1:SEQ + 2, :], zero.t[0:1, :], r=[zero], w=[zpadB])
        cw = [C.sb([128, 512], F32, "cw") for _ in range(3)]
        for k3 in range(3):
            C.dma("sp", cw[k3][:], convw_d[k3].partition_broadcast(128), w=[cw[k3]])
        anb = C.sb([128, 512], F32, "anb")
        bnb = C.sb([128, 512], F32, "bnb")
        C.dma("sp", anb[:], anorm_d.partition_broadcast(128), w=[anb])
        C.dma("sp", bnb[:], bnorm_d.partition_broadcast(128), w=[bnb])
        esink = C.sb([128, 8], F32, "esink")
        C.dma("sp", esink[:], sink_d.partition_broadcast(128), w=[esink])
        C.op("act", lambda e: e.activation(out=esink[:], in_=esink[:], func=AF.Exp), r=[esink], w=[esink])
        nsI = C.sb([128, 8, 128], BF16, "nsI")
        for hh in range(8):
            C.op("dve", lambda e, hh=hh: e.tensor_scalar(out=nsI.t[:, hh, :], in0=ident_f[:],
                                                         scalar1=-8.0 * 2.0 ** (-(hh + 1)), scalar2=None,
                                                         op0=ALU.mult), r=[ident_f], w=[nsI])

        with C.scope():
            wbig = C.sb([128, 8, 3328], BF16, "w_in0")
            wcB = [Buf(f"w_in0_c{c}") for c in range(8)]
            for c in range(8):
                C.dma("pool", wbig.t[:, c, :], w_in0_d[c * 128:(c + 1) * 128, :], w=[wcB[c]])
            xb = [C.sb([128, DM], BF16, "xb") for _ in range(2)]
            xT = [C.sb([128, 8, 128], BF16, "xT") for _ in range(2)]
            qb = [C.sb([128, 512], BF16, "qb") for _ in range(2)]
            kb_ = C.sb([128, 128], BF16, "kb")
            cgs = C.sb([128, 512], F32, "cgs")
            zpt = [C.sb([128, 512], F32, "zpt") for _ in range(2)]
            bgt = [C.sb([128, 512], F32, "bgt") for _ in range(2)]
            szt = [C.sb([128, DM], F32, "szt") for _ in range(2)]
            groups = [(0, 512), (512, 256), (768, 512), (1280, 512), (1792, 512), (2304, 512), (2816, 512)]

            def ld1(i):
                C.dma("pool", xb[i % 2][:], xsrc[i * 128:(i + 1) * 128, :], r=([xsrcB[i]] if xsrcB else []), w=[xb[i % 2]])

            ld1(0)
            for i in range(NT):
                k = i % 2
                if i + 1 < NT:
                    ld1(i + 1)
                transpose_to(xb[k], 8, 128, lambda k=k: xT[k].t[:], xT[k], eng="act")
                banks = []
                for gi, (n0, wd) in enumerate(groups):
                    bank = PF[gi % 6]
                    banks.append(bank)
                    for c in range(8):
                        C.op("pe", lambda e, c=c, n0=n0, wd=wd, bank=bank, k=k: e.matmul(
                            bank.t[:, 0:wd], lhsT=xT[k].t[:, c, :], rhs=wbig.t[:, c, n0:n0 + wd],
                            start=(c == 0), stop=(c == 7)), r=[xT[k], wcB[c]], w=[bank])
                    if gi == 0:
                        C.op("act", lambda e, bank=bank, k=k: e.copy(out=qb[k][:], in_=bank.t[:]), r=[bank], w=[qb[k]])
                        C.dma("sp", q_s[i * 128:(i + 1) * 128, :], qb[k][:], r=[qb[k]], w=[qB[i]])
                    elif gi == 1:
                        C.op("dve", lambda e, bank=bank: e.tensor_copy(out=kb_[:], in_=bank.t[:, 0:128]), r=[bank], w=[kb_])
                        C.op("dve", lambda e, bank=bank, i=i: e.tensor_copy(
                            out=vall.t[:, i, :, 0:64], in_=bank.t[:, 128:256].rearrange("p (h d) -> p h d", h=2)),
                            r=[bank], w=[vB[i]])
                        transpose_to(kb_, 1, 128, lambda i=i: kT.t[:, i * 128:(i + 1) * 128].unsqueeze(1), kTB[i], eng="dve")
                    elif gi == 2:
                        C.op("act", lambda e, bank=bank, k=k: e.copy(out=bgt[k][:], in_=bank.t[:]), r=[bank], w=[bgt[k]])
                        C.dma("sp", bg_s[i * 128:(i + 1) * 128, :], bgt[k][:], r=[bgt[k]], w=[bgB[i]])
                    elif gi == 3:
                        C.op("act", lambda e, bank=bank: e.copy(out=cgs[:], in_=bank.t[:]), r=[bank], w=[cgs])
                    elif gi == 4:
                        C.op("dve", lambda e, bank=bank, k=k: e.tensor_tensor(out=zpt[k][:], in0=bank.t[:], in1=cgs[:],
                                                                             op=ALU.mult), r=[bank, cgs], w=[zpt[k]])
                        C.dma("sp", zp_s[1 + i * 128:1 + (i + 1) * 128, :], zpt[k][:], r=[zpt[k]], w=[zpB[i]])
                    else:
                        hf = gi - 5
                        C.op("act", lambda e, bank=bank, k=k, hf=hf: e.activation(
                            out=szt[k].t[:, hf * 512:(hf + 1) * 512], in_=bank.t[:], func=AF.Silu), r=[bank], w=[szt[k]])
                        if hf == 1:
                            C.dma("sp", sz_s[i * 128:(i + 1) * 128, :], szt[k][:], r=[szt[k]], w=[szB[i]])

        if "pre_s2" in hooks:
            hooks["pre_s2"]()
        with C.scope():
            q2 = [C.sb([128, 512], BF16, "q2") for _ in range(2)]
            qT = [C.sb([128, 4, 128], BF16, "qT") for _ in range(2)]
            zw = [[C.sb([128, 512], F32, "zw") for _ in range(3)] for _ in range(2)]
            bg2 = [C.sb([128, 512], F32, "bg2") for _ in range(2)]
            sz2 = [C.sb([128, DM], F32, "sz2") for _ in range(2)]
            pq_i = [C.sb([128, 128], I32, "pqi") for _ in range(2)]
            pq = [C.sb([128, 128], F32, "pq") for _ in range(2)]
            dmf = [C.sb([128, 3, 128], F32, "dmf") for _ in range(2)]
            Dm = [C.sb([128, 3, 128], BF16, "Dm") for _ in range(2)]
            pTt = [C.sb([128, 3, 128], BF16, "pTt") for _ in range(4)]
            ya = C.sb([128, 8, 64], F32, "ya")
            den = C.sb([128, 8], F32, "den")
            ssa = C.sb([128, 1], F32, "ssa")
            ra = C.sb([128, 1], F32, "ra")
            ssb = C.sb([128, 1], F32, "ssb")
            rb = C.sb([128, 1], F32, "rb")
            yg = [C.sb([128, DM], F32, "yg") for _ in range(2)]
            ygo = [C.sb([128, DM], BF16, "ygo") for _ in range(2)]
            po = (PF[4], PF[5])

            def ld2(j):
                k = j % 2
                C.dma("sp", q2[k][:], q_s[j * 128:(j + 1) * 128, :], r=[qB[j]], w=[q2[k]])
                for d3 in range(3):
                    rd = [zpB[j]]
                    if d3 == 0:
                        rd.append(zpB[j - 1] if j > 0 else zpadB)
                    if d3 == 2:
                        rd.append(zpB[j + 1] if j + 1 < NT else zpadB)
                    C.dma("sp", zw[k][d3][:], zp_s[j * 128 + d3:j * 128 + d3 + 128, :], r=rd, w=[zw[k][d3]])
                C.dma("sp", bg2[k][:], bg_s[j * 128:(j + 1) * 128, :], r=[bgB[j]], w=[bg2[k]])
                C.dma("sp", sz2[k][:], sz_s[j * 128:(j + 1) * 128, :], r=[szB[j]], w=[sz2[k]])
                C.dma("sp", pq_i[k][:], pos_d[j * 128:(j + 1) * 128].partition_broadcast(128), w=[pq_i[k]])

            def kbs_of(j):
                return [kb for kb in (j - 1, j, j + 1) if 0 <= kb < NT]

            def prep_attn(j):
                k = j % 2
                transpose_to(q2[k], 4, 128, lambda: qT[k].t[:], qT[k], eng="dve")
                C.op("dve", lambda e: e.tensor_copy(out=pq[k][:], in_=pq_i[k][:]), r=[pq_i[k]], w=[pq[k]])
                kbs = kbs_of(j)
                s0 = 0 if j > 0 else 1
                for kb in kbs:
                    s = kb - j + 1
                    C.op("act", lambda e: e.activation(out=dmf[k].t[:, s, :], in_=pq[k][:], func=AF.Abs,
                                                       bias=posk.t[:, kb:kb + 1], scale=-1.0),
                         r=[pq[k], posk], w=[dmf[k]])
                    if s == 0:
                        C.op("pool", lambda e: e.tensor_tensor(out=dmf[k].t[:, 0, :], in0=dmf[k].t[:, 0, :],
                                                               in1=mlo[:], op=ALU.add), r=[dmf[k], mlo], w=[dmf[k]])
                    elif s == 2:
                        C.op("pool", lambda e: e.tensor_tensor(out=dmf[k].t[:, 2, :], in0=dmf[k].t[:, 2, :],
                                                               in1=mhi[:], op=ALU.add), r=[dmf[k], mhi], w=[dmf[k]])
                C.op("pool", lambda e: e.tensor_copy(out=Dm[k].t[:, s0:s0 + len(kbs), :],
                                                     in_=dmf[k].t[:, s0:s0 + len(kbs), :]), r=[dmf[k]], w=[Dm[k]])

            def conv_branch(j):
                k = j % 2
                z_m1, z_0, z_p1 = zw[k]
                C.op("pool", lambda e: e.tensor_tensor(out=z_m1[:], in0=z_m1[:], in1=cw[0][:], op=ALU.mult),
                     r=[z_m1, cw[0]], w=[z_m1])
                C.op("pool", lambda e: e.tensor_tensor(out=z_0[:], in0=z_0[:], in1=cw[1][:], op=ALU.mult),
                     r=[z_0, cw[1]], w=[z_0])
                C.op("pool", lambda e: e.tensor_tensor(out=z_p1[:], in0=z_p1[:], in1=cw[2][:], op=ALU.mult),
                     r=[z_p1, cw[2]], w=[z_p1])
                C.op("dve", lambda e: e.tensor_tensor(out=z_0[:], in0=z_0[:], in1=z_m1[:], op=ALU.add),
                     r=[z_0, z_m1], w=[z_0])
                C.op("dve", lambda e: e.tensor_tensor(out=z_0[:], in0=z_0[:], in1=z_p1[:], op=ALU.add),
                     r=[z_0, z_p1], w=[z_0])
                C.op("dve", lambda e: e.tensor_tensor(out=z_0[:], in0=z_0[:], in1=bg2[k][:], op=ALU.mult),
                     r=[z_0, bg2[k]], w=[z_0])
                C.op("act", lambda e: e.activation(out=junk.t[:, 512:1024], in_=z_0[:], func=AF.Square,
                                                   accum_out=ssb[:]), r=[z_0], w=[junk, ssb])
                rstd_from_sum(ssb, 512, rb)
                C.op("dve", lambda e: e.scalar_tensor_tensor(out=yg[k].t[:, 512:1024], in0=z_0[:], scalar=rb[:, 0:1],
                                                             in1=bnb[:], op0=ALU.mult, op1=ALU.mult),
                     r=[z_0, rb, bnb], w=[yg[k]])

            hc = {"n": 0}

            def heads(j, mid=None):
                k = j % 2
                kbs = kbs_of(j)
                ns = len(kbs)
                s0 = 0 if j > 0 else 1
                slots = {}

                def front(hh):
                    hk, gq = hh // 4, hh % 4
                    n = hc["n"]
                    hc["n"] += 1
                    stb = PF[n % 4]
                    ptt = pTt[n % 4]
                    slots[hh] = ptt
                    stv = stb.t[:, 0:384].rearrange("p (s q) -> p s q", s=3)
                    for kb in kbs:
                        s = kb - j + 1
                        C.op("pe", lambda e: e.matmul(
                            stv[:, s, :], lhsT=kT.t[hk * 64:(hk + 1) * 64, kb * 128:(kb + 1) * 128],
                            rhs=qT[k].t[hk * 64:(hk + 1) * 64, gq, :], start=True, stop=False),
                            r=[kTB[kb], qT[k]], w=[stb])
                        C.op("pe", lambda e: e.matmul(
                            stv[:, s, :], lhsT=nsI.t[:, hh, :], rhs=Dm[k].t[:, s, :], start=False, stop=True),
                            r=[nsI, Dm[k]], w=[stb])
                    C.op("act", lambda e: e.activation(
                        out=ptt.t[:, s0:s0 + ns, :], in_=stv[:, s0:s0 + ns, :], func=AF.Exp, scale=0.125),
                        r=[stb], w=[ptt])

                def back(hh):
                    hk = hh // 4
                    ptt = slots[hh]
                    pob = po[hh // 4]
                    pov = pob.t[:, 0:260].rearrange("p (h d) -> p h d", h=4)
                    for n_, kb in enumerate(kbs):
                        s = kb - j + 1
                        C.op("pe", lambda e: e.matmul(
                            pov[:, hh % 4, :], lhsT=ptt.t[:, s, :], rhs=vall.t[:, kb, hk, :],
                            start=(n_ == 0), stop=(n_ == ns - 1)), r=[ptt, vB[kb]], w=[pob])

                LOOK = 3
                for hh in range(8 + LOOK):
                    if hh < 8:
                        front(hh)
                    if hh == 5 and mid is not None:
                        mid()
                    if hh >= LOOK:
                        back(hh - LOOK)

            def tail(j):
                k = j % 2
                for g2 in range(2):
                    pov = po[g2].t[:, 0:260].rearrange("p (h d) -> p h d", h=4)
                    C.op("dve", lambda e: e.tensor_tensor(
                        out=den.t[:, g2 * 4:(g2 + 1) * 4], in0=pov[:, :, 64], in1=esink.t[:, g2 * 4:(g2 + 1) * 4],
                        op=ALU.add), r=[po[g2], esink], w=[den])
                C.op("dve", lambda e: e.reciprocal(out=den[:], in_=den[:]), r=[den], w=[den])
                for g2 in range(2):
                    pov = po[g2].t[:, 0:260].rearrange("p (h d) -> p h d", h=4)
                    C.op("dve", lambda e: e.tensor_tensor(
                        out=ya.t[:, g2 * 4:(g2 + 1) * 4, :], in0=pov[:, :, 0:64],
                        in1=den.t[:, g2 * 4:(g2 + 1) * 4].unsqueeze(2).to_broadcast([128, 4, 64]), op=ALU.mult),
                        r=[po[g2], den], w=[ya])
                yaf = ya.t[:].rearrange("p h d -> p (h d)")
                C.op("act", lambda e: e.activation(out=junk.t[:, 0:512], in_=yaf, func=AF.Square, accum_out=ssa[:]),
                     r=[ya], w=[junk, ssa])
                rstd_from_sum(ssa, 512, ra)
                C.op("dve", lambda e: e.scalar_tensor_tensor(out=yg[k].t[:, 0:512], in0=yaf, scalar=ra[:, 0:1],
                                                             in1=anb[:], op0=ALU.mult, op1=ALU.mult),
                     r=[ya, ra, anb], w=[yg[k]])
                C.op("dve", lambda e: e.tensor_tensor(out=ygo[k][:], in0=yg[k][:], in1=sz2[k][:], op=ALU.mult),
                     r=[yg[k], sz2[k]], w=[ygo[k]])
                C.dma("sp", yg_s[j * 128:(j + 1) * 128, :], ygo[k][:], r=[ygo[k]], w=[ygB[j]])

            ld2(0)
            prep_attn(0)
            conv_branch(0)
            for j in range(NT):
                if j + 1 < NT:
                    ld2(j + 1)
                if hooks.get("s2_thunks"):
                    hooks["s2_thunks"].pop(0)()
                heads(j, mid=(lambda: prep_attn(j + 1)) if j + 1 < NT else None)
                tail(j)
                if j + 1 < NT:
                    conv_branch(j + 1)

      if "pre_p" in hooks:
          hooks["pre_p"]()
      post_stage(0, xsrc, xsrcB, dst, dstB, wts=hooks.get("wpost"), thunks=hooks.get("p_thunks"))

    def alloc_l1_weights(right):
        al = C.sbr if right else C.sb
        return (al([128, 8, 2464], BF16, "w_in1"), al([128, 2, 768], BF16, "w_uq"), al([128, 1024], BF16, "w_ukv"),
                al([128, 4, 128], BF16, "w_sT"))

    def load_l1_weights(ws, thunks=None):
        w1, wuq, wukv, wsT = ws
        load_weight(lambda c: w1.t[:, c, :], w_in1_d, 8, w1, thunks)
        load_weight(lambda c: wuq.t[:, c, :], w_uq_d, 2, wuq, thunks)
        g1 = lambda: C.dma("pool", wukv[:], w_ukv_d, w=[wukv])
        g2 = lambda: C.dma("pool", wsT[:], w_sT_d, w=[wsT])
        for g in (g1, g2):
            if thunks is None:
                g()
            else:
                thunks.append(g)

    def layer1(xsrc, xsrcB, dst, dstB, hooks=None):
      hooks = hooks or {}
      import math
      TWO_PI = 2.0 * math.pi
      C1 = 6.28125
      C2 = TWO_PI - C1
      PI_S = 3.1415925
      SCALE = 96.0 ** -0.5
      with C.scope():
        qT_s = dscr("qT_s", [96, 8, SEQ], BF16)
        szc_s = dscr("szc_s", [SEQ, 512])
        qTB = [Buf() for _ in range(NT)]
        szcB = [Buf() for _ in range(NT)]
        KT = C.sb([128, 8, SEQ], BF16, "KT")
        vall = C.sb([128, NT, 8, 65], BF16, "vall1")
        KTB = [Buf() for _ in range(NT)]
        vB = [Buf() for _ in range(NT)]
        C.op("pool", lambda e: e.memset(vall[:], 1.0), w=[vall] + vB)
        KTpad = Buf("KTpad")

        with C.scope():
            if "w_s1" in hooks:
                w1, wuq, wukv, wsT = hooks["w_s1"]
            else:
                w1, wuq, wukv, wsT = ws_ = alloc_l1_weights(False)
                load_l1_weights(ws_)
            bsT = C.sb([128, 4], F32, "bsT")
            C.dma("sp", bsT[:], b_sT_d, w=[bsT])

            def bc(src, n, name):
                t = C.sb([128, n], F32, name)
                C.dma("sp", t[:], src.partition_broadcast(128), w=[t])
                return t

            qnb = bc(qn_d, 256, "qnb")
            kvnb = bc(kvn_d, 128, "kvnb")
            vlgb = bc(vlg_d, 512, "vlgb")
            vlbb = bc(vlb_d, 512, "vlbb")
            dnb = bc(dn_d, 512, "dnb")
            invb = bc(invf_d, 16, "invb")
            negpi = C.sb([128, 1], F32, "negpi")
            sc = C.sb([128, NT, 32], F32, "sc")
            with C.scope():
                ang = C.sb([128, NT, 32], F32, "ang")
                kf = C.sb([128, NT, 32], F32, "kf")
                ki = C.sb([128, NT, 32], I32, "ki")
                C.op("dve", lambda e: e.tensor_tensor(out=ang.t[:, :, 0:16],
                                                      in0=posk.t[:].unsqueeze(2).to_broadcast([128, NT, 16]),
                                                      in1=invb.t[:].unsqueeze(1).to_broadcast([128, NT, 16]),
                                                      op=ALU.mult), r=[posk, invb], w=[ang])
                C.op("dve", lambda e: e.tensor_scalar(out=ang.t[:, :, 16:32], in0=ang.t[:, :, 0:16], scalar1=0.5 * math.pi,
                                                      scalar2=None, op0=ALU.add), r=[ang], w=[ang])
                C.op("dve", lambda e: e.tensor_scalar(out=kf[:], in0=ang[:], scalar1=1.0 / TWO_PI, scalar2=None,
                                                      op0=ALU.mult), r=[ang], w=[kf])
                C.op("dve", lambda e: e.tensor_copy(out=ki[:], in_=kf[:]), r=[kf], w=[ki])
                C.op("dve", lambda e: e.tensor_copy(out=kf[:], in_=ki[:]), r=[ki], w=[kf])
                C.op("dve", lambda e: e.scalar_tensor_tensor(out=ang[:], in0=kf[:], scalar=-C1, in1=ang[:],
                                                             op0=ALU.mult, op1=ALU.add), r=[kf, ang], w=[ang])
                C.op("dve", lambda e: e.scalar_tensor_tensor(out=ang[:], in0=kf[:], scalar=-C2, in1=ang[:],
                                                             op0=ALU.mult, op1=ALU.add), r=[kf, ang], w=[ang])
                C.op("dve", lambda e: e.tensor_scalar(out=ang[:], in0=ang[:], scalar1=-PI_S, scalar2=PI_S,
                                                      op0=ALU.max, op1=ALU.min), r=[ang], w=[ang])
                C.op("act", lambda e: e.activation(out=sc[:], in_=ang[:], func=AF.Sin), r=[ang], w=[sc])

            xb = [C.sb([128, DM], BF16, "xb") for _ in range(2)]
            xT = [C.sb([128, 8, 128], BF16, "xT") for _ in range(2)]
            cqn = [C.sb([128, 256], BF16, "cqn") for _ in range(2)]
            cqnT = C.sb([128, 2, 128], BF16, "cqnT")
            cn = [C.sb([128, 128], BF16, "cn") for _ in range(2)]
            cnT = C.sb([128, 128], BF16, "cnT")
            qfull = C.sb([128, 8 * 96], BF16, "qfull")
            kfull = C.sb([128, 8 * 96], BF16, "kfull")
            qTt = [C.sb([128, 8, 128], BF16, "qTt") for _ in range(2)]
            kro = [C.sb([128, 32], F32, "kro") for _ in range(2)]
            tq = [C.sb([128, 4, 16], F32, "tq") for _ in range(4)]
            tk = [C.sb([128, 16], F32, "tk") for _ in range(4)]
            gu = [C.sb([128, 512], F32, "gu") for _ in range(2)]
            gtmp2 = [C.sb([128, 512], F32, "gtmp") for _ in range(2)]
            xstg = [C.sb([128, 512], F32, "xstg") for _ in range(3)]
            gv = [C.sb([128, 512], F32, "gv") for _ in range(2)]
            vn = C.sb([128, 512], BF16, "vn")
            szd = [C.sb([128, 512], F32, "szd") for _ in range(2)]
            szc = [C.sb([128, 512], F32, "szc") for _ in range(2)]
            yd = C.sb([128, 512], F32, "yd")
            ygd = [C.sb([128, 512], BF16, "ygd") for _ in range(2)]
            st1 = {n: C.sb([128, 1], F32, n) for n in ("ssq", "rq", "ssk", "rk", "ms", "nm", "vs", "rv", "ssd", "rd")}
            groups = [(0, 416), (416, 512), (928, 512), (1440, 512), (1952, 512)]
            gbank = [PF[0], PF[1], PF[2], PF[0], PF[1]]

            def ld1(i):
                C.dma("pool", xb[i % 2][:], xsrc[i * 128:(i + 1) * 128, :], r=([xsrcB[i]] if xsrcB else []),
                      w=[xb[i % 2]])

            def silu_from(bank_, out_):
                zs = xstg[2]
                C.op("act", lambda e: e.activation(out=out_[:], in_=bank_.t[:], func=AF.Sigmoid), r=[bank_], w=[out_])
                C.op("act", lambda e: e.copy(out=zs[:], in_=bank_.t[:]), r=[bank_], w=[zs])
                C.op("pool", lambda e: e.tensor_tensor(out=out_[:], in0=out_[:], in1=zs[:], op=ALU.mult),
                     r=[out_, zs], w=[out_])

            def gelu_from(bank_, g_, which):
                gtmp = gtmp2[which]
                xs = xstg[which]
                C.op("act", lambda e: e.activation(out=gtmp[:], in_=bank_.t[:], func=AF.Square), r=[bank_], w=[gtmp])
                C.op("act", lambda e: e.copy(out=xs[:], in_=bank_.t[:]), r=[bank_], w=[xs])
                C.op("dve", lambda e: e.tensor_scalar(out=gtmp[:], in0=gtmp[:], scalar1=0.044715, scalar2=1.0,
                                                      op0=ALU.mult, op1=ALU.add), r=[gtmp], w=[gtmp])
                C.op("pool", lambda e: e.tensor_tensor(out=gtmp[:], in0=gtmp[:], in1=xs[:], op=ALU.mult),
                     r=[gtmp, xs], w=[gtmp])
                C.op("act", lambda e: e.activation(out=gtmp[:], in_=gtmp[:], func=AF.Sigmoid,
                                                   scale=1.5957691216057308), r=[gtmp], w=[gtmp])
                C.op("pool", lambda e: e.tensor_tensor(out=g_[:], in0=gtmp[:], in1=xs[:], op=ALU.mult),
                     r=[gtmp, xs], w=[g_])

            def mm_group(i, gi):
                k = i % 2
                n0, wd = groups[gi]
                bank = gbank[gi]
                for c in range(8):
                    C.op("pe", lambda e: e.matmul(bank.t[:, 0:wd], lhsT=xT[k].t[:, c, :], rhs=w1.t[:, c, n0:n0 + wd],
                                                  start=(c == 0), stop=(c == 7)), r=[xT[k], w1], w=[bank])

            def A1(i):
                k = i % 2
                if i < 8:
                    C.op("pool", lambda e: e.memset(KT.t[96:128, i, :], 0.0), w=[KTpad])
                transpose_to(xb[k], 8, 128, lambda: xT[k].t[:], xT[k], eng="act")
                mm_group(i, 0)
                mm_group(i, 1)
                mm_group(i, 2)

            def A2(i):
                k = i % 2
                g0 = gbank[0]
                C.op("act", lambda e: e.activation(out=junk.t[:, 0:256], in_=g0.t[:, 0:256], func=AF.Square,
                                                   accum_out=st1["ssq"][:]), r=[g0], w=[junk, st1["ssq"]])
                C.op("act", lambda e: e.activation(out=junk.t[:, 256:384], in_=g0.t[:, 256:384], func=AF.Square,
                                                   accum_out=st1["ssk"][:]), r=[g0], w=[junk, st1["ssk"]])
                sinb = sc.t[:, i, 0:16]
                cosb = sc.t[:, i, 16:32]
                x1 = g0.t[:, 384:400]
                x2 = g0.t[:, 400:416]
                C.op("dve", lambda e: e.tensor_tensor(out=tk[0][:], in0=x1, in1=cosb, op=ALU.mult), r=[g0, sc], w=[tk[0]])
                C.op("dve", lambda e: e.tensor_tensor(out=tk[1][:], in0=x2, in1=sinb, op=ALU.mult), r=[g0, sc], w=[tk[1]])
                C.op("dve", lambda e: e.tensor_tensor(out=tk[2][:], in0=x1, in1=sinb, op=ALU.mult), r=[g0, sc], w=[tk[2]])
                C.op("dve", lambda e: e.tensor_tensor(out=tk[3][:], in0=x2, in1=cosb, op=ALU.mult), r=[g0, sc], w=[tk[3]])
                rstd_from_sum(st1["ssq"], 256, st1["rq"])
                rstd_from_sum(st1["ssk"], 128, st1["rk"])
                C.op("dve", lambda e: e.tensor_tensor(out=kro[k].t[:, 0:16], in0=tk[0][:], in1=tk[1][:],
                                                      op=ALU.subtract), r=[tk[0], tk[1]], w=[kro[k]])
                C.op("dve", lambda e: e.tensor_tensor(out=kro[k].t[:, 16:32], in0=tk[2][:], in1=tk[3][:], op=ALU.add),
                     r=[tk[2], tk[3]], w=[kro[k]])
                C.op("dve", lambda e: e.scalar_tensor_tensor(out=cqn[k][:], in0=g0.t[:, 0:256],
                                                             scalar=st1["rq"].t[:, 0:1], in1=qnb[:], op0=ALU.mult,
                                                             op1=ALU.mult), r=[g0, st1["rq"], qnb], w=[cqn[k]])
                C.op("dve", lambda e: e.scalar_tensor_tensor(out=cn[k][:], in0=g0.t[:, 256:384],
                                                             scalar=st1["rk"].t[:, 0:1], in1=kvnb[:], op0=ALU.mult,
                                                             op1=ALU.mult), r=[g0, st1["rk"], kvnb], w=[cn[k]])

            def A2b(i):
                k = i % 2
                mm_group(i, 3)
                gelu_from(gbank[1], gu[k], 0)
                mm_group(i, 4)
                gelu_from(gbank[2], gv[k], 1)
                silu_from(gbank[3], szc[k])
                C.dma("sp", szc_s[i * 128:(i + 1) * 128, :], szc[k][:], r=[szc[k]], w=[szcB[i]])
                silu_from(gbank[4], szd[k])

            qb_ = (PF[3], PF[4])
            kvb = (PF[5], PF[3])

            def B1(i):
                k = i % 2
                transpose_to(cqn[k], 2, 128, lambda: cqnT.t[:], cqnT, eng="dve")
                transpose_to(cn[k], 1, 128, lambda: cnT.t[:].unsqueeze(1), cnT, eng="dve")
                for b2 in range(2):
                    for c in range(2):
                        C.op("pe", lambda e: e.matmul(qb_[b2].t[:, 0:384], lhsT=cqnT.t[:, c, :],
                                                      rhs=wuq.t[:, c, b2 * 384:(b2 + 1) * 384], start=(c == 0),
                                                      stop=(c == 1)), r=[cqnT, wuq], w=[qb_[b2]])
                C.op("pe", lambda e: e.matmul(kvb[0].t[:], lhsT=cnT.t[:], rhs=wukv.t[:, 0:512],
                                              start=True, stop=True), r=[cnT, wukv], w=[kvb[0]])

            def B2(i):
                k = i % 2
                qf3 = qfull.t[:].rearrange("p (h d) -> p h d", h=8)
                kf3 = kfull.t[:].rearrange("p (h d) -> p h d", h=8)
                cos4 = sc.t[:, i, 16:32].unsqueeze(1).to_broadcast([128, 4, 16])
                sin4 = sc.t[:, i, 0:16].unsqueeze(1).to_broadcast([128, 4, 16])
                gvk = gv[k]
                C.op("act", lambda e: e.activation(out=junk.t[:, 0:512], in_=gvk[:], func=AF.Copy,
                                                   accum_out=st1["ms"][:]), r=[gvk], w=[junk, st1["ms"]])
                for b2 in range(2):
                    hs = slice(b2 * 4, b2 * 4 + 4)
                    qv = qb_[b2].t[:, 0:384].rearrange("p (h d) -> p h d", h=4)
                    C.op("act", lambda e: e.copy(out=qf3[:, hs, 0:64], in_=qv[:, :, 0:64]), r=[qb_[b2]], w=[qfull])
                    C.op("dve", lambda e: e.tensor_tensor(out=tq[0][:], in0=qv[:, :, 64:80], in1=cos4, op=ALU.mult),
                         r=[qb_[b2], sc], w=[tq[0]])
                    C.op("dve", lambda e: e.tensor_tensor(out=tq[1][:], in0=qv[:, :, 80:96], in1=sin4, op=ALU.mult),
                         r=[qb_[b2], sc], w=[tq[1]])
                    C.op("dve", lambda e: e.tensor_tensor(out=tq[2][:], in0=qv[:, :, 64:80], in1=sin4, op=ALU.mult),
                         r=[qb_[b2], sc], w=[tq[2]])
                    C.op("dve", lambda e: e.tensor_tensor(out=tq[3][:], in0=qv[:, :, 80:96], in1=cos4, op=ALU.mult),
                         r=[qb_[b2], sc], w=[tq[3]])
                    C.op("dve", lambda e: e.tensor_tensor(out=qf3[:, hs, 64:80], in0=tq[0][:], in1=tq[1][:],
                                                          op=ALU.subtract), r=[tq[0], tq[1]], w=[qfull])
                    C.op("dve", lambda e: e.tensor_tensor(out=qf3[:, hs, 80:96], in0=tq[2][:], in1=tq[3][:],
                                                          op=ALU.add), r=[tq[2], tq[3]], w=[qfull])
                    if b2 == 0:
                        C.op("pe", lambda e: e.matmul(kvb[1].t[:], lhsT=cnT.t[:], rhs=wukv.t[:, 512:1024],
                                                      start=True, stop=True), r=[cnT, wukv], w=[kvb[1]])
                        C.op("dve", lambda e: e.tensor_scalar(out=st1["nm"][:], in0=st1["ms"][:], scalar1=-1.0 / 512,
                                                              scalar2=None, op0=ALU.mult), r=[st1["ms"]], w=[st1["nm"]])
                        C.op("act", lambda e: e.activation(out=gvk[:], in_=gvk[:], func=AF.Identity,
                                                           bias=st1["nm"].t[:, 0:1], scale=1.0),
                             r=[gvk, st1["nm"]], w=[gvk])
                        C.op("act", lambda e: e.activation(out=junk.t[:, 512:1024], in_=gvk[:], func=AF.Square,
                                                           accum_out=st1["vs"][:]), r=[gvk], w=[junk, st1["vs"]])
                for b2 in range(2):
                    hs = slice(b2 * 4, b2 * 4 + 4)
                    kvv = kvb[b2].t[:].rearrange("p (h d) -> p h d", h=4)
                    C.op("act", lambda e: e.copy(out=kf3[:, hs, 0:64], in_=kvv[:, :, 0:64]), r=[kvb[b2]], w=[kfull])
                    C.op("act", lambda e: e.copy(out=vall.t[:, i, hs, 0:64], in_=kvv[:, :, 64:128]), r=[kvb[b2]],
                         w=[vB[i]])
                C.op("dve", lambda e: e.tensor_copy(out=kf3[:, :, 64:96],
                                                    in_=kro[k].t[:].unsqueeze(1).to_broadcast([128, 8, 32])),
                     r=[kro[k]], w=[kfull])
                rstd_from_sum(st1["vs"], 512, st1["rv"])
                C.op("dve", lambda e: e.scalar_tensor_tensor(out=gvk[:], in0=gvk[:], scalar=st1["rv"].t[:, 0:1],
                                                             in1=vlgb[:], op0=ALU.mult, op1=ALU.mult),
                     r=[gvk, st1["rv"], vlgb], w=[gvk])
                C.op("dve", lambda e: e.tensor_tensor(out=vn[:], in0=gvk[:], in1=vlbb[:], op=ALU.add),
                     r=[gvk, vlbb], w=[vn])

            def B3(i):
                k = i % 2
                guk = gu[k]
                transpose_to(qfull, 8, 96, lambda: qTt[k].t[0:96, :, :], qTt[k], eng="act")
                C.dma("sp", qT_s[:, :, i * 128:(i + 1) * 128], qTt[k].t[0:96, :, :], r=[qTt[k]], w=[qTB[i]])
                transpose_to(kfull, 8, 96, lambda: KT.t[0:96, :, i * 128:(i + 1) * 128], KTB[i], eng="dve")
                mixb = PF[4]
                for g4 in range(4):
                    C.op("pe", lambda e: e.matmul(mixb.t[:, g4 * 128:(g4 + 1) * 128], lhsT=wsT.t[:, g4, :],
                                                  rhs=vn.t[:, g4 * 128:(g4 + 1) * 128], start=True, stop=True),
                         r=[wsT, vn], w=[mixb])
                for g4 in range(4):
                    sl = slice(g4 * 128, (g4 + 1) * 128)
                    C.op("dve", lambda e: e.scalar_tensor_tensor(out=yd.t[:, sl], in0=mixb.t[:, sl],
                                                                 scalar=bsT.t[:, g4:g4 + 1], in1=guk.t[:, sl],
                                                                 op0=ALU.add, op1=ALU.mult),
                         r=[mixb, bsT, guk], w=[yd])
                C.op("act", lambda e: e.activation(out=junk.t[:, 0:512], in_=yd[:], func=AF.Square,
                                                   accum_out=st1["ssd"][:]), r=[yd], w=[junk, st1["ssd"]])
                rstd_from_sum(st1["ssd"], 512, st1["rd"])
                C.op("dve", lambda e: e.scalar_tensor_tensor(out=yd[:], in0=yd[:], scalar=st1["rd"].t[:, 0:1],
                                                             in1=dnb[:], op0=ALU.mult, op1=ALU.mult),
                     r=[yd, st1["rd"], dnb], w=[yd])
                C.op("dve", lambda e: e.tensor_tensor(out=ygd[k][:], in0=yd[:], in1=szd[k][:], op=ALU.mult),
                     r=[yd, szd[k]], w=[ygd[k]])
                C.dma("sp", yg_s[i * 128:(i + 1) * 128, 512:1024], ygd[k][:], r=[ygd[k]], w=[ygB[i]])

            ld1(0)
            ld1(1)
            A1(0)
            A2(0)
            A2b(0)
            for i in range(NT):
                if i + 2 < NT:
                    ld1(i + 2)
                B1(i)
                if i + 1 < NT:
                    A1(i + 1)
                if i + 1 < NT:
                    A2(i + 1)
                B2(i)
                if i + 1 < NT:
                    A2b(i + 1)
                B3(i)

        if "pre_s2" in hooks:
            hooks["pre_s2"]()
        with C.scope():
            cnb = C.sb([128, 512], F32, "cnb")
            C.dma("sp", cnb[:], cn_d.partition_broadcast(128), w=[cnb])
            qTg = [C.sb([128, 8, 512], BF16, "qTg") for _ in range(2)]
            ptt = [C.sb([128, 512], BF16, "ptt") for _ in range(4)]
            ycg = C.sb([128, 4, 512], F32, "ycg")
            szc2 = [C.sb([128, 512], F32, "szc2") for _ in range(2)]
            ygc = [C.sb([128, 512], BF16, "ygc") for _ in range(2)]
            rden = C.sb([128, 4], F32, "rden")
            ssc = C.sb([128, 1], F32, "ssc")
            rc = C.sb([128, 1], F32, "rc")
            NG = SEQ // 512

            def ldq(g8):
                C.dma("sp", qTg[g8 % 2].t[0:96, :, :], qT_s[:, :, g8 * 512:(g8 + 1) * 512],
                      r=[qTB[g8 * 4 + t4] for t4 in range(4)], w=[qTg[g8 % 2]])

            for qg_ in qTg:
                C.op("pool", lambda e: e.memset(qg_[:], 0.0), w=[qg_])
            ldq(0)
            for g_ in hooks.get("s2_thunks", []):
                g_()
            iters = [(g8, hh, kt) for g8 in range(NG) for hh in range(8) for kt in range(NT)]
            LOOK = 2

            def front(n):
                g8, hh, kt = iters[n]
                if hh == 0 and kt == 0 and g8 + 1 < NG:
                    ldq(g8 + 1)
                qg = qTg[g8 % 2]
                stb = PF[n % 4]
                pt_ = ptt[n % 4]
                C.op("pe", lambda e: e.matmul(stb.t[:], lhsT=KT.t[:, hh, kt * 128:(kt + 1) * 128],
                                              rhs=qg.t[:, hh, :], start=True, stop=True),
                     r=[KTB[kt], KTpad, qg], w=[stb])
                C.op("act", lambda e: e.activation(out=pt_[:], in_=stb.t[:], func=AF.Exp, scale=SCALE),
                     r=[stb], w=[pt_])

            def back(n):
                g8, hh, kt = iters[n]
                pt_ = ptt[n % 4]
                pob = PF[4 + hh % 2]
                pov = pob.t[:, 0:260].rearrange("p (q d) -> p q d", q=4)
                for qi in range(4):
                    C.op("pe", lambda e: e.matmul(pov[:, qi, :], lhsT=pt_.t[:, qi * 128:(qi + 1) * 128],
                                                  rhs=vall.t[:, kt, hh, :], start=(kt == 0 and qi == 0),
                                                  stop=(kt == NT - 1), skip_group_check=True),
                         r=[pt_, vB[kt]], w=[pob])
                if kt != NT - 1:
                    return
                C.op("dve", lambda e: e.reciprocal(out=rden[:], in_=pov[:, :, 64]), r=[pob], w=[rden])
                C.op("dve", lambda e: e.tensor_tensor(out=ycg.t[:, :, hh * 64:(hh + 1) * 64], in0=pov[:, :, 0:64],
                                                      in1=rden.t[:].unsqueeze(2).to_broadcast([128, 4, 64]),
                                                      op=ALU.mult), r=[pob, rden], w=[ycg])
                if hh != 7:
                    return
                for qi in range(4):
                    ti = g8 * 4 + qi
                    k = ti % 2
                    C.dma("sp", szc2[k][:], szc_s[ti * 128:(ti + 1) * 128, :], r=[szcB[ti]], w=[szc2[k]])
                    C.op("act", lambda e: e.activation(out=junk.t[:, 0:512], in_=ycg.t[:, qi, :], func=AF.Square,
                                                       accum_out=ssc[:]), r=[ycg], w=[junk, ssc])
                    rstd_from_sum(ssc, 512, rc)
                    C.op("dve", lambda e: e.scalar_tensor_tensor(out=ycg.t[:, qi, :], in0=ycg.t[:, qi, :],
                                                                 scalar=rc.t[:, 0:1], in1=cnb[:], op0=ALU.mult,
                                                                 op1=ALU.mult), r=[ycg, rc, cnb], w=[ycg])
                    C.op("dve", lambda e: e.tensor_tensor(out=ygc[k][:], in0=ycg.t[:, qi, :], in1=szc2[k][:],
                                                          op=ALU.mult), r=[ycg, szc2[k]], w=[ygc[k]])
                    C.dma("sp", yg_s[ti * 128:(ti + 1) * 128, 0:512], ygc[k][:], r=[ygc[k]], w=[ygB[ti]])

            for n in range(len(iters) + LOOK):
                if n < len(iters):
                    front(n)
                if n >= LOOK:
                    back(n - LOOK)
      post_stage(1, xsrc, xsrcB, dst, dstB, wts=hooks.get("wpost"))

    if layers == (0,):
        layer0(x_in, None, out_d, None)
    elif layers == (1,):
        layer1(x_in, None, out_d, None)
    else:
        pre = {}

        def l0_pre_s2():
            pre["w_s1"] = alloc_l1_weights(True)
            pre["wpost0"] = alloc_post_weights(True)
            h0["wpost"] = pre["wpost0"]
            h0["s2_thunks"] = []
            load_post_weights(0, pre["wpost0"], h0["s2_thunks"])

        def l0_pre_p():
            while h0["s2_thunks"]:
                h0["s2_thunks"].pop(0)()
            h0["p_thunks"] = []
            load_l1_weights(pre["w_s1"], h0["p_thunks"])

        h0 = {"pre_s2": l0_pre_s2, "pre_p": l0_pre_p}
        layer0(x_in, None, x_mid, xmidB, hooks=h0)
        while h0.get("p_thunks"):
            h0["p_thunks"].pop(0)()
        C.rpop(1)
        h1 = {"w_s1": pre["w_s1"]}

        def l1_pre_s2():
            C.rpop(4)
            pre["wpost1"] = alloc_post_weights(True)
            h1["wpost"] = pre["wpost1"]
            h1["s2_thunks"] = []
            load_post_weights(1, pre["wpost1"], h1["s2_thunks"])

        h1["pre_s2"] = l1_pre_s2
        layer1(x_mid, xmidB, out_d, None, hooks=h1)
        C.rpop(1)
    C.S.emit()
    nc.dbg_names = dbg_names
    return nc


def _host_inputs(inputs, layers):
    f = lambda a: np.ascontiguousarray(np.asarray(a))
    x = f(inputs["x"])
    p = f(inputs["p"])
    pos = f(inputs["positions"]).astype(np.int32)
    shared = {}
    for l in layers:
        shared[f"ln_g{l}"] = f(inputs["post_ln_g"][l])
        shared[f"ln_b{l}"] = f(inputs["post_ln_b"][l])
        shared[f"proj{l}"] = f(inputs["ple_proj"][l])
        shared[f"gate{l}"] = f(inputs["ple_gate"][l])
    if 0 in layers:
        w = f(inputs["ev_w_in"][0])
        qcols = w[:, 0:512].reshape(1024, 8, 64)
        order = [0, 4, 1, 5, 2, 6, 3, 7]
        wq = qcols[:, order, :].reshape(1024, 512)
        shared["w_in0"] = f(np.concatenate([wq, w[:, 512:]], axis=1))
        shared["conv_w"] = f(inputs["ev_conv_w"][0])
        shared["sink"] = f(inputs["ev_sink"][0])
        shared["a_norm"] = f(inputs["ev_a_norm"][0])
        shared["b_norm"] = f(inputs["ev_b_norm"][0])
        shared["w_out0"] = f(inputs["ev_w_out"][0])
    if 1 in layers:
        shared["w_in1"] = f(inputs["od_w_in"][0])
        shared["w_uq"] = f(inputs["od_w_uq"][0])
        shared["w_ukv"] = f(inputs["od_w_ukv"][0])
        shared["w_sT"] = f(np.transpose(np.asarray(inputs["od_w_s"][0]), (2, 0, 1)))
        shared["b_sT"] = f(np.transpose(np.asarray(inputs["od_b_s"][0]), (1, 0)))
        shared["q_norm"] = f(inputs["od_q_norm"][0])
        shared["kv_norm"] = f(inputs["od_kv_norm"][0])
        shared["v_ln_g"] = f(inputs["od_v_ln_g"][0])
        shared["v_ln_b"] = f(inputs["od_v_ln_b"][0])
        shared["c_norm"] = f(inputs["od_c_norm"][0])
        shared["d_norm"] = f(inputs["od_d_norm"][0])
        shared["w_out1"] = f(inputs["od_w_out"][0])
        half = 16
        shared["inv_freq"] = (10000.0 ** (-np.arange(half, dtype=np.float32) / half)).astype(np.float32)
    maps = []
    for b in range(x.shape[0]):
        m = dict(shared)
        m["x"] = f(x[b])
        for l in layers:
            m[f"p{l}"] = f(p[l, b])
        m["pos"] = f(pos[b])
        m["posT"] = f(pos[b].reshape(NT, 128).T)
        maps.append(m)
    return maps


_CACHE = {}


def _run(layers, inputs, ncores=NCORES, debug=False):
    key = (layers, debug)
    if key not in _CACHE:
        _CACHE[key] = build(layers, debug)
    nc = _CACHE[key]
    maps = _host_inputs(inputs, layers)[:ncores]
    res = run_bass_kernel_spmd(nc, maps, core_ids=list(range(len(maps))))
    if debug:
        return res.results
    return np.stack([r["out"] for r in res.results], axis=0)


FUSED = True


def kernel(**inputs):
    if FUSED:
        out = _run((0, 1), inputs)
    else:
        x1 = _run((0,), inputs)
        inputs1 = dict(inputs)
        inputs1["x"] = x1
        out = _run((1,), inputs1)
    return np.ascontiguousarray(out, dtype=np.float32)
```

```python
import contextlib
import numpy as np
import concourse.bass as bass
import concourse.mybir as mybir
from concourse.bass_utils import run_bass_kernel_spmd

F32 = mybir.dt.float32
BF16 = mybir.dt.bfloat16
I32 = mybir.dt.int32
AF = mybir.ActivationFunctionType
ALU = mybir.AluOpType

NCORES = 8
SEQ = 4096
DM = 1024
NT = SEQ // 128
ALPHA = 4.0 ** 0.25
EPS = 1e-6
BIG = 30000.0

ENG_NAMES = ("pe", "act", "dve", "pool", "sp")
EPOCH = 30000
INLINE_WAIT = True


class Buf:
    __slots__ = ("name", "writers", "readers", "excl")

    def __init__(self, name="", excl=False):
        self.name = name
        self.writers = []
        self.readers = []
        self.excl = excl


class Op:
    __slots__ = ("eng", "fn", "deps", "is_dma", "sig", "seq", "dsem", "dval")

    def __init__(self, eng, fn, is_dma):
        self.eng = eng
        self.fn = fn
        self.deps = []
        self.is_dma = is_dma
        self.sig = False
        self.seq = 0
        self.dsem = None
        self.dval = 0


class _Rec:
    def __init__(self):
        self.call = None

    def __getattr__(self, name):
        def f(*a, **k):
            self.call = (name, a, k)
        return f


def _replay(call):
    name, a, k = call
    return lambda e: getattr(e, name)(*a, **k)


class Sched:
    def __init__(self, nc, n_dma_sems=24):
        self.nc = nc
        self.ops = {e: [] for e in ENG_NAMES}
        self.n_dma_sems = n_dma_sems

    def op(self, eng, fn, reads=(), writes=(), dma=False):
        rec = _Rec()
        fn(rec)
        o = Op(eng, _replay(rec.call), dma)
        deps = []
        for b in reads:
            deps.extend(b.writers)
            if b.excl:
                deps.extend(r for r in b.readers if r.eng != eng)
        for b in writes:
            for r in b.readers:
                if r.is_dma or dma or r.eng != eng or eng != "pe":
                    deps.append(r)
            for w in b.writers:
                if w.is_dma or dma or w.eng != eng or eng != "pe":
                    deps.append(w)
        seen = set()
        for d in deps:
            if id(d) not in seen and d is not o:
                seen.add(id(d))
                o.deps.append(d)
                d.sig = True
        for b in reads:
            b.readers.append(o)
        for b in writes:
            if b.readers:
                b.readers = [r for r in b.readers if r is o]
                b.writers = [o]
            else:
                b.writers.append(o)
        self.ops[eng].append(o)
        return o

    def barrier(self):
        lasts = []
        for e in ENG_NAMES:
            nd = [o for o in self.ops[e] if not o.is_dma]
            if nd:
                lasts.append(nd[-1])
            dl = [o for o in self.ops[e] if o.is_dma]
            lasts.extend(dl[-self.n_dma_sems:])
        for e in ENG_NAMES:
            o = Op(e, lambda eng: eng.nop(), False)
            for d in lasts:
                if d.is_dma or d.eng != e:
                    o.deps.append(d)
                    d.sig = True
            self.ops[e].append(o)

    def emit(self):
        nc = self.nc
        with contextlib.ExitStack() as es:
            eng_sems = {}
            for e in ENG_NAMES:
                n = 0
                for o in self.ops[e]:
                    if not o.is_dma and o.sig:
                        n += 1
                        o.seq = n
                nep = max((n + EPOCH - 1) // EPOCH, 1)
                eng_sems[e] = [es.enter_context(nc.semaphore(f"s_{e}_{k}")) for k in range(nep)]
            for e in ENG_NAMES:
                dl = [o for o in self.ops[e] if o.is_dma]
                if not dl:
                    continue
                pool = [es.enter_context(nc.semaphore(f"d_{e}_{k}")) for k in range(self.n_dma_sems)]
                cnt = [0] * len(pool)
                for k, o in enumerate(dl):
                    j = k % len(pool)
                    cnt[j] += 1
                    o.dsem = pool[j]
                    o.dval = 16 * cnt[j]
            block = es.enter_context(nc.Block())

            def run_engine(ename, engobj):
                waited = {}

                def need(sem, val):
                    if waited.get(sem.num, 0) >= val:
                        return
                    waited[sem.num] = val
                    engobj.wait_ge(sem, val)

                for o in self.ops[ename]:
                    pend = []

                    def need2(sem, val):
                        if waited.get(sem.num, 0) >= val:
                            return
                        waited[sem.num] = val
                        pend.append((sem, val))

                    for d in o.deps:
                        if d.is_dma:
                            need2(d.dsem, d.dval)
                        else:
                            k = (d.seq - 1) // EPOCH
                            need2(eng_sems[d.eng][k], d.seq - k * EPOCH)
                    if o.is_dma and o.dval > 16:
                        need2(o.dsem, o.dval - 16)
                    inline = None
                    if pend and INLINE_WAIT and not o.is_dma:
                        inline = pend.pop()
                    for sem, val in pend:
                        engobj.wait_ge(sem, val)
                    ins = o.fn(engobj)
                    if inline is not None:
                        ins._wait_ge(inline[0], inline[1])
                    if o.is_dma:
                        ins.then_inc(o.dsem, 16)
                    elif o.sig:
                        k = (o.seq - 1) // EPOCH
                        ins.then_inc(eng_sems[ename][k], 1)
                last = {}
                for o in self.ops[ename]:
                    if o.is_dma:
                        last[o.dsem.num] = (o.dsem, o.dval)
                for sem, val in last.values():
                    need(sem, val)

            block.tensor(lambda eng: run_engine("pe", eng))
            block.scalar(lambda eng: run_engine("act", eng))
            block.vector(lambda eng: run_engine("dve", eng))
            block.gpsimd(lambda eng: run_engine("pool", eng))
            block.sync(lambda eng: run_engine("sp", eng))


class T:
    def __init__(self, t, name):
        self.t = t
        self.b = Buf(name)

    def __getitem__(self, k):
        return self.t[k]


def _bufs(xs):
    out = []
    for x in xs:
        if x is None:
            continue
        out.append(x.b if isinstance(x, T) else x)
    return out


class Ctx:
    def __init__(self, nc):
        self.nc = nc
        self.S = Sched(nc)
        self.n = 0
        self.stack = [contextlib.ExitStack()]
        self.rguards = []

    def sb(self, shape, dt, name=None):
        self.n += 1
        name = f"{name or 'sb'}_{self.n}"
        t = self.stack[-1].enter_context(self.nc.sbuf_tensor(name, list(shape), dt))
        return T(t, name)

    def sbr(self, shape, dt, name=None):
        self.n += 1
        name = f"{name or 'sbr'}_{self.n}"
        g = self.nc.sbuf_tensor(name, list(shape), dt, side="right")
        t = g.__enter__()
        self.rguards.append(g)
        return T(t, name)

    def rpop(self, n):
        for _ in range(n):
            self.rguards.pop().__exit__(None, None, None)

    @contextlib.contextmanager
    def scope(self):
        es = contextlib.ExitStack()
        self.stack.append(es)
        try:
            yield
        finally:
            self.S.barrier()
            self.stack.pop()
            es.close()

    def ps(self, shape, dt, name=None):
        self.n += 1
        name = f"{name or 'ps'}_{self.n}"
        t = T(self.nc.alloc_psum_tensor(name, list(shape), dt), name)
        t.b.excl = True
        return t

    def op(self, eng, fn, r=(), w=()):
        return self.S.op(eng, fn, _bufs(r), _bufs(w))

    def dma(self, q, out, in_, r=(), w=()):
        return self.S.op(q, lambda e: e.dma_start(out=out, in_=in_), _bufs(r), _bufs(w), dma=True)


def build(layers, debug=False):
    nc = bass.Bass("TRN2", target_bir_lowering=False)
    C = Ctx(nc)
    dbg_names = []

    def din(name, shape, dt=F32):
        return nc.dram_tensor(name, list(shape), dt, kind="ExternalInput").ap()

    def dscr(name, shape, dt=F32):
        if debug:
            dbg_names.append(name)
            return nc.dram_tensor(name, list(shape), dt, kind="ExternalOutput").ap()
        return nc.dram_tensor(name, list(shape), dt, kind="Internal").ap()

    def dump(name, tt, shape, dt=F32):
        if not debug:
            return
        dbg_names.append(name)
        o = nc.dram_tensor(name, list(shape), dt, kind="ExternalOutput").ap()
        C.dma("sp", o, tt[:], r=[tt])

    x_in = din("x", [SEQ, DM])
    out_d = nc.dram_tensor("out", [SEQ, DM], F32, kind="ExternalOutput").ap()
    pos_d = din("pos", [SEQ], I32)
    posT_d = din("posT", [128, NT], I32)
    p_d = {l: din(f"p{l}", [SEQ, 256]) for l in layers}
    lng_d = {l: din(f"ln_g{l}", [DM]) for l in layers}
    lnb_d = {l: din(f"ln_b{l}", [DM]) for l in layers}
    wout_d = {l: din(f"w_out{l}", [DM, DM]) for l in layers}
    gate_d = {l: din(f"gate{l}", [DM, DM]) for l in layers}
    proj_d = {l: din(f"proj{l}", [256, DM]) for l in layers}
    if 0 in layers:
        w_in0_d = din("w_in0", [DM, 3328])
        convw_d = din("conv_w", [3, 512])
        sink_d = din("sink", [8])
        anorm_d = din("a_norm", [512])
        bnorm_d = din("b_norm", [512])
    if 1 in layers:
        w_in1_d = din("w_in1", [DM, 2464])
        w_uq_d = din("w_uq", [256, 768])
        w_ukv_d = din("w_ukv", [128, 1024])
        w_sT_d = din("w_sT", [128, 4, 128])
        b_sT_d = din("b_sT", [128, 4])
        qn_d = din("q_norm", [256])
        kvn_d = din("kv_norm", [128])
        vlg_d = din("v_ln_g", [512])
        vlb_d = din("v_ln_b", [512])
        cn_d = din("c_norm", [512])
        dn_d = din("d_norm", [512])
        invf_d = din("inv_freq", [16])
    x_mid = dscr("x_mid", [SEQ, DM]) if len(layers) == 2 else None

    yg_s = dscr("yg_s", [SEQ, DM], BF16)
    ygB = [Buf(f"yg{i}") for i in range(NT)]
    xmidB = [Buf(f"xm{i}") for i in range(NT)]

    ident_f = C.sb([128, 128], F32, "identf")
    ident = C.sb([128, 128], BF16, "ident")
    eps_t = C.sb([128, 1], F32, "eps")
    posk_i = C.sb([128, NT], I32, "poski")
    posk = C.sb([128, NT], F32, "posk")
    junk = C.sb([128, DM], BF16, "junk")
    PT = [C.ps([128, 1024], BF16, "pt") for _ in range(2)]
    PF = [C.ps([128, 512], F32, "pf") for _ in range(6)]
    st = {"pt": 0}

    def next_pt():
        st["pt"] ^= 1
        return PT[st["pt"]]

    C.op("pool", lambda e: e.memset(ident_f[:], 0.0), w=[ident_f])
    C.op("pool", lambda e: e.affine_select(out=ident_f[:], in_=ident_f[:], pattern=[[-1, 128]],
                                           compare_op=ALU.not_equal, fill=1.0, base=0,
                                           channel_multiplier=1), r=[ident_f], w=[ident_f])
    C.op("dve", lambda e: e.tensor_copy(out=ident[:], in_=ident_f[:]), r=[ident_f], w=[ident])
    mlo = C.sb([128, 128], F32, "mlo")
    mhi = C.sb([128, 128], F32, "mhi")
    C.op("pool", lambda e: e.memset(mlo[:], 0.0), w=[mlo])
    C.op("pool", lambda e: e.memset(mhi[:], 0.0), w=[mhi])
    C.op("pool", lambda e: e.affine_select(out=mlo[:], in_=mlo[:], pattern=[[-1, 128]], compare_op=ALU.is_ge,
                                           fill=BIG, base=0, channel_multiplier=1), r=[mlo], w=[mlo])
    C.op("pool", lambda e: e.affine_select(out=mhi[:], in_=mhi[:], pattern=[[1, 128]], compare_op=ALU.is_ge,
                                           fill=BIG, base=0, channel_multiplier=-1), r=[mhi], w=[mhi])
    C.op("pool", lambda e: e.memset(eps_t[:], EPS), w=[eps_t])
    negh = C.sb([128, 1], F32, "negh")
    C.op("pool", lambda e: e.memset(negh[:], -0.5), w=[negh])
    C.dma("sp", posk_i[:], posT_d, w=[posk_i])
    C.op("dve", lambda e: e.tensor_copy(out=posk[:], in_=posk_i[:]), r=[posk_i], w=[posk])

    def rstd_from_sum(ssum, n, out):
        C.op("dve", lambda e: e.tensor_scalar(out=out[:], in0=ssum[:], scalar1=1.0 / n, scalar2=EPS, op0=ALU.mult,
                                              op1=ALU.add), r=[ssum], w=[out])
        C.op("pool", lambda e: e.tensor_tensor(out=out[:], in0=out[:], in1=negh[:], op=ALU.pow), r=[out, negh], w=[out])

    def transpose_to(src, nchunk, width, dst_ap_fn, dst, eng="act", rows=128):
        pt = next_pt()
        ptv = pt.t[:].rearrange("p (c t) -> p c t", t=128)
        for c in range(nchunk):
            C.op("pe", lambda e, c=c: e.transpose(out=ptv[0:width, c, :], in_=src[:, c * width:(c + 1) * width],
                                                  identity=ident[:]), r=[src, ident], w=[pt])
        if eng == "act":
            C.op("act", lambda e: e.copy(out=dst_ap_fn(), in_=ptv[0:width, 0:nchunk, :]), r=[pt], w=[dst])
        else:
            C.op(eng, lambda e: e.tensor_copy(out=dst_ap_fn(), in_=ptv[0:width, 0:nchunk, :]), r=[pt], w=[dst])

    def load_weight(dst_ap_fn, src_ap, nchunks, dstT, thunks=None):
        for c in range(nchunks):
            def go(c=c):
                C.dma("pool", dst_ap_fn(c), src_ap[c * 128:(c + 1) * 128, :], w=[dstT])
            if thunks is None:
                go()
            else:
                thunks.append(go)

    def alloc_post_weights(right):
        return (C.sbr if right else C.sb)([128, 8, 3072], BF16, "wpost")

    def load_post_weights(l, wbig, thunks=None):
        load_weight(lambda c: wbig.t[:, c, 0:1024], wout_d[l], 8, wbig, thunks)
        load_weight(lambda c: wbig.t[:, c, 1024:2048], gate_d[l], 8, wbig, thunks)
        load_weight(lambda c: wbig.t[:, c, 2048:3072], proj_d[l], 2, wbig, thunks)

    def post_stage(l, xsrc, xsrcB, dst, dstB, wts=None, thunks=None):
      with C.scope():
        if wts is None:
            wbig = alloc_post_weights(False)
            load_post_weights(l, wbig)
        else:
            wbig = wts
        lng = C.sb([128, DM], F32, "lng")
        lnb = C.sb([128, DM], F32, "lnb")
        C.dma("sp", lng[:], lng_d[l].partition_broadcast(128), w=[lng])
        C.dma("sp", lnb[:], lnb_d[l].partition_broadcast(128), w=[lnb])
        ygb = [C.sb([128, DM], BF16, "ygb") for _ in range(2)]
        ygT = [C.sb([128, 8, 128], BF16, "ygT") for _ in range(2)]
        xr = [C.sb([128, DM], F32, "xr") for _ in range(2)]
        pb = [C.sb([128, 256], BF16, "pb") for _ in range(2)]
        pT = [C.sb([128, 2, 128], BF16, "pT") for _ in range(3)]
        junk2 = C.sb([128, DM], BF16, "junk2")
        s = [C.sb([128, DM], F32, "s") for _ in range(2)]
        h2 = [C.sb([128, DM], F32, "h") for _ in range(2)]
        hb2 = [C.sb([128, DM], BF16, "hb") for _ in range(2)]
        hT2 = [C.sb([128, 8, 128], BF16, "hT") for _ in range(2)]
        sg = [C.sb([128, DM], F32, "sg") for _ in range(2)]
        msum = C.sb([128, 1], F32, "msum")
        nmean = C.sb([128, 1], F32, "nmean")
        vsum = C.sb([128, 1], F32, "vsum")
        rstd = C.sb([128, 1], F32, "rstd")
        py = (PF[0], PF[1])
        pg = (PF[2], PF[3])
        pp = (PF[4], PF[5])

        def loads(i):
            k = i % 2
            C.dma("sp", ygb[k][:], yg_s[i * 128:(i + 1) * 128, :], r=[ygB[i]], w=[ygb[k]])
            C.dma("sp", xr[k][:], xsrc[i * 128:(i + 1) * 128, :], r=([xsrcB[i]] if xsrcB else []), w=[xr[k]])
            C.dma("pool", pb[k][:], p_d[l][i * 128:(i + 1) * 128, :], w=[pb[k]])

        def phase1(i):
            k = i % 2
            transpose_to(ygb[k], 8, 128, lambda: ygT[k].t[:], ygT[k], eng="act")
            transpose_to(pb[k], 2, 128, lambda: pT[i % 3].t[:], pT[i % 3], eng="dve")
            for hf in range(2):
                for c in range(8):
                    C.op("pe", lambda e: e.matmul(py[hf].t[:], lhsT=ygT[k].t[:, c, :],
                                                  rhs=wbig.t[:, c, hf * 512:(hf + 1) * 512],
                                                  start=(c == 0), stop=(c == 7)),
                         r=[ygT[k], wbig], w=[py[hf]])
            for hf in range(2):
                C.op("dve", lambda e: e.scalar_tensor_tensor(
                    out=s[k].t[:, hf * 512:(hf + 1) * 512], in0=xr[k].t[:, hf * 512:(hf + 1) * 512], scalar=ALPHA,
                    in1=py[hf].t[:], op0=ALU.mult, op1=ALU.add), r=[xr[k], py[hf]], w=[s[k]])

        def phase2a(i):
            k = i % 2
            sk = s[k]
            h, hb = h2[k], hb2[k]
            C.op("act", lambda e: e.activation(out=junk[:], in_=sk[:], func=AF.Copy, accum_out=msum[:]),
                 r=[sk], w=[junk, msum])
            yield
            C.op("dve", lambda e: e.tensor_scalar(out=nmean[:], in0=msum[:], scalar1=-1.0 / DM, scalar2=None,
                                                  op0=ALU.mult), r=[msum], w=[nmean])
            C.op("act", lambda e: e.activation(out=sk[:], in_=sk[:], func=AF.Identity, bias=nmean[:, 0:1], scale=1.0),
                 r=[sk, nmean], w=[sk])
            yield
            C.op("act", lambda e: e.activation(out=junk2[:], in_=sk[:], func=AF.Square, accum_out=vsum[:]),
                 r=[sk], w=[junk2, vsum])
            yield
            rstd_from_sum(vsum, DM, rstd)
            yield
            C.op("dve", lambda e: e.scalar_tensor_tensor(out=h[:], in0=sk[:], scalar=rstd[:, 0:1], in1=lng[:],
                                                         op0=ALU.mult, op1=ALU.mult), r=[sk, rstd, lng], w=[h])
            yield
            C.op("dve", lambda e: e.tensor_tensor(out=hb[:], in0=h[:], in1=lnb[:], op=ALU.add), r=[h, lnb], w=[hb])
            C.op("dve", lambda e: e.tensor_tensor(out=h[:], in0=h[:], in1=lnb[:], op=ALU.add), r=[h, lnb], w=[h])
            yield

        def phase2b(i):
            k = i % 2
            h, hb, hT = h2[k], hb2[k], hT2[k]
            pTk = pT[i % 3]
            transpose_to(hb, 8, 128, lambda: hT.t[:], hT, eng="act")
            yield
            for hf in range(2):
                for c in range(8):
                    C.op("pe", lambda e: e.matmul(pg[hf].t[:], lhsT=hT.t[:, c, :],
                                                  rhs=wbig.t[:, c, 1024 + hf * 512:1024 + (hf + 1) * 512],
                                                  start=(c == 0), stop=(c == 7)),
                         r=[hT, wbig], w=[pg[hf]])
                for c in range(2):
                    C.op("pe", lambda e: e.matmul(pp[hf].t[:], lhsT=pTk.t[:, c, :],
                                                  rhs=wbig.t[:, c, 2048 + hf * 512:2048 + (hf + 1) * 512],
                                                  start=(c == 0), stop=(c == 1)),
                         r=[pTk, wbig], w=[pp[hf]])
                yield
            for hf in range(2):
                sl = slice(hf * 512, (hf + 1) * 512)
                C.op("act", lambda e: e.activation(out=sg[k].t[:, sl], in_=pg[hf].t[:], func=AF.Sigmoid),
                     r=[pg[hf]], w=[sg[k]])
                yield
                C.op("dve", lambda e: e.tensor_tensor(out=sg[k].t[:, sl], in0=sg[k].t[:, sl], in1=pp[hf].t[:],
                                                      op=ALU.mult), r=[sg[k], pp[hf]], w=[sg[k]])
                yield
            C.op("pool", lambda e: e.tensor_tensor(out=sg[k].t[:], in0=sg[k].t[:], in1=h[:], op=ALU.add),
                 r=[sg[k], h], w=[sg[k]])
            C.dma("sp", dst[i * 128:(i + 1) * 128, :], sg[k].t[:], r=[sg[k]], w=([dstB[i]] if dstB else []))
            yield

        def interleave(*gens):
            gens = [g for g in gens if g is not None]
            while gens:
                for g in list(gens):
                    try:
                        next(g)
                    except StopIteration:
                        gens.remove(g)

        loads(0)
        loads(1)
        phase1(0)
        for i in range(NT + 1):
            if i + 2 < NT:
                loads(i + 2)
            if thunks:
                thunks.pop(0)()
            interleave(phase2a(i) if i < NT else None, phase2b(i - 1) if i >= 1 else None)
            if i + 1 < NT:
                phase1(i + 1)

    def layer0(xsrc, xsrcB, dst, dstB, hooks=None):
      hooks = hooks or {}
      with C.scope():
        q_s = dscr("q_s", [SEQ, 512], BF16)
        zp_s = dscr("zp_s", [SEQ + 2, 512])
        bg_s = dscr("bg_s", [SEQ, 512])
        sz_s = dscr("sz_s", [SEQ, DM])
        qB = [Buf() for _ in range(NT)]
        zpB = [Buf() for _ in range(NT)]
        zpadB = Buf()
        bgB = [Buf() for _ in range(NT)]
        szB = [Buf() for _ in range(NT)]

        kT = C.sb([128, SEQ], BF16, "kT")
        vall = C.sb([128, NT, 2, 65], BF16, "vall")
        kTB = [Buf() for _ in range(NT)]
        vB = [Buf() for _ in range(NT)]
        C.op("pool", lambda e: e.memset(vall[:], 1.0), w=[vall] + vB)
        zero = C.sb([128, 512], F32, "zero")
        C.op("pool", lambda e: e.memset(zero[:], 0.0), w=[zero])
        C.dma("sp", zp_s[0:1, :], zero.t[0:1, :], r=[zero], w=[zpadB])
        C.dma("sp", zp_s[SEQ + 1:SEQ + 2, :], zero.t[0:1, :], r=[zero], w=[zpadB])
        cw = [C.sb([128, 512], F32, "cw") for _ in range(3)]
        for k3 in range(3):
            C.dma("sp", cw[k3][:], convw_d[k3].partition_broadcast(128), w=[cw[k3]])
        anb = C.sb([128, 512], F32, "anb")
        bnb = C.sb([128, 512], F32, "bnb")
        C.dma("sp", anb[:], anorm_d.partition_broadcast(128), w=[anb])
        C.dma("sp", bnb[:], bnorm_d.partition_broadcast(128), w=[bnb])
        esink = C.sb([128, 8], F32, "esink")
        C.dma("sp", esink[:], sink_d.partition_broadcast(128), w=[esink])
        C.op("act", lambda e: e.activation(out=esink[:], in_=esink[:], func=AF.Exp), r=[esink], w=[esink])
        nsI = C.sb([128, 8, 128], BF16, "nsI")
        for hh in range(8):
            C.op("dve", lambda e, hh=hh: e.tensor_scalar(out=nsI.t[:, hh, :], in0=ident_f[:],
                                                         scalar1=-8.0 * 2.0 ** (-(hh + 1)), scalar2=None,
                                                         op0=ALU.mult), r=[ident_f], w=[nsI])

        with C.scope():
            wbig = C.sb([128, 8, 3328], BF16, "w_in0")
            wcB = [Buf(f"w_in0_c{c}") for c in range(8)]
            for c in range(8):
                C.dma("pool", wbig.t[:, c, :], w_in0_d[c * 128:(c + 1) * 128, :], w=[wcB[c]])
            xb = [C.sb([128, DM], BF16, "xb") for _ in range(2)]
            xT = [C.sb([128, 8, 128], BF16, "xT") for _ in range(2)]
            qb = [C.sb([128, 512], BF16, "qb") for _ in range(2)]
            kb_ = C.sb([128, 128], BF16, "kb")
            cgs = C.sb([128, 512], F32, "cgs")
            zpt = [C.sb([128, 512], F32, "zpt") for _ in range(2)]
            bgt = [C.sb([128, 512], F32, "bgt") for _ in range(2)]
            szt = [C.sb([128, DM], F32, "szt") for _ in range(2)]
            groups = [(0, 512), (512, 256), (768, 512), (1280, 512), (1792, 512), (2304, 512), (2816, 512)]

            def ld1(i):
                C.dma("pool", xb[i % 2][:], xsrc[i * 128:(i + 1) * 128, :], r=([xsrcB[i]] if xsrcB else []), w=[xb[i % 2]])

            ld1(0)
            for i in range(NT):
                k = i % 2
                if i + 1 < NT:
                    ld1(i + 1)
                transpose_to(xb[k], 8, 128, lambda k=k: xT[k].t[:], xT[k], eng="act")
                banks = []
                for gi, (n0, wd) in enumerate(groups):
                    bank = PF[gi % 6]
                    banks.append(bank)
                    for c in range(8):
                        C.op("pe", lambda e, c=c, n0=n0, wd=wd, bank=bank, k=k: e.matmul(
                            bank.t[:, 0:wd], lhsT=xT[k].t[:, c, :], rhs=wbig.t[:, c, n0:n0 + wd],
                            start=(c == 0), stop=(c == 7)), r=[xT[k], wcB[c]], w=[bank])
                    if gi == 0:
                        C.op("act", lambda e, bank=bank, k=k: e.copy(out=qb[k][:], in_=bank.t[:]), r=[bank], w=[qb[k]])
                        C.dma("sp", q_s[i * 128:(i + 1) * 128, :], qb[k][:], r=[qb[k]], w=[qB[i]])
                    elif gi == 1:
                        C.op("dve", lambda e, bank=bank: e.tensor_copy(out=kb_[:], in_=bank.t[:, 0:128]), r=[bank], w=[kb_])
                        C.op("dve", lambda e, bank=bank, i=i: e.tensor_copy(
                            out=vall.t[:, i, :, 0:64], in_=bank.t[:, 128:256].rearrange("p (h d) -> p h d", h=2)),
                            r=[bank], w=[vB[i]])
                        transpose_to(kb_, 1, 128, lambda i=i: kT.t[:, i * 128:(i + 1) * 128].unsqueeze(1), kTB[i], eng="dve")
                    elif gi == 2:
                        C.op("act", lambda e, bank=bank, k=k: e.copy(out=bgt[k][:], in_=bank.t[:]), r=[bank], w=[bgt[k]])
                        C.dma("sp", bg_s[i * 128:(i + 1) * 128, :], bgt[k][:], r=[bgt[k]], w=[bgB[i]])
                    elif gi == 3:
                        C.op("act", lambda e, bank=bank: e.copy(out=cgs[:], in_=bank.t[:]), r=[bank], w=[cgs])
                    elif gi == 4:
                        C.op("dve", lambda e, bank=bank, k=k: e.tensor_tensor(out=zpt[k][:], in0=bank.t[:], in1=cgs[:],
                                                                             op=ALU.mult), r=[bank, cgs], w=[zpt[k]])
                        C.dma("sp", zp_s[1 + i * 128:1 + (i + 1) * 128, :], zpt[k][:], r=[zpt[k]], w=[zpB[i]])
                    else:
                        hf = gi - 5
                        C.op("act", lambda e, bank=bank, k=k, hf=hf: e.activation(
                            out=szt[k].t[:, hf * 512:(hf + 1) * 512], in_=bank.t[:], func=AF.Silu), r=[bank], w=[szt[k]])
                        if hf == 1:
                            C.dma("sp", sz_s[i * 128:(i + 1) * 128, :], szt[k][:], r=[szt[k]], w=[szB[i]])

        if "pre_s2" in hooks:
            hooks["pre_s2"]()
        with C.scope():
            q2 = [C.sb([128, 512], BF16, "q2") for _ in range(2)]
            qT = [C.sb([128, 4, 128], BF16, "qT") for _ in range(2)]
            zw = [[C.sb([128, 512], F32, "zw") for _ in range(3)] for _ in range(2)]
            bg2 = [C.sb([128, 512], F32, "bg2") for _ in range(2)]
            sz2 = [C.sb([128, DM], F32, "sz2") for _ in range(2)]
            pq_i = [C.sb([128, 128], I32, "pqi") for _ in range(2)]
            pq = [C.sb([128, 128], F32, "pq") for _ in range(2)]
            dmf = [C.sb([128, 3, 128], F32, "dmf") for _ in range(2)]
            Dm = [C.sb([128, 3, 128], BF16, "Dm") for _ in range(2)]
            pTt = [C.sb([128, 3, 128], BF16, "pTt") for _ in range(4)]
            ya = C.sb([128, 8, 64], F32, "ya")
            den = C.sb([128, 8], F32, "den")
            ssa = C.sb([128, 1], F32, "ssa")
            ra = C.sb([128, 1], F32, "ra")
            ssb = C.sb([128, 1], F32, "ssb")
            rb = C.sb([128, 1], F32, "rb")
            yg = [C.sb([128, DM], F32, "yg") for _ in range(2)]
            ygo = [C.sb([128, DM], BF16, "ygo") for _ in range(2)]
            po = (PF[4], PF[5])

            def ld2(j):
                k = j % 2
                C.dma("sp", q2[k][:], q_s[j * 128:(j + 1) * 128, :], r=[qB[j]], w=[q2[k]])
                for d3 in range(3):
                    rd = [zpB[j]]
                    if d3 == 0:
                        rd.append(zpB[j - 1] if j > 0 else zpadB)
                    if d3 == 2:
                        rd.append(zpB[j + 1] if j + 1 < NT else zpadB)
                    C.dma("sp", zw[k][d3][:], zp_s[j * 128 + d3:j * 128 + d3 + 128, :], r=rd, w=[zw[k][d3]])
                C.dma("sp", bg2[k][:], bg_s[j * 128:(j + 1) * 128, :], r=[bgB[j]], w=[bg2[k]])
                C.dma("sp", sz2[k][:], sz_s[j * 128:(j + 1) * 128, :], r=[szB[j]], w=[sz2[k]])
                C.dma("sp", pq_i[k][:], pos_d[j * 128:(j + 1) * 128].partition_broadcast(128), w=[pq_i[k]])

            def kbs_of(j):
                return [kb for kb in (j - 1, j, j + 1) if 0 <= kb < NT]

            def prep_attn(j):
                k = j % 2
                transpose_to(q2[k], 4, 128, lambda: qT[k].t[:], qT[k], eng="dve")
                C.op("dve", lambda e: e.tensor_copy(out=pq[k][:], in_=pq_i[k][:]), r=[pq_i[k]], w=[pq[k]])
                kbs = kbs_of(j)
                s0 = 0 if j > 0 else 1
                for kb in kbs:
                    s = kb - j + 1
                    C.op("act", lambda e: e.activation(out=dmf[k].t[:, s, :], in_=pq[k][:], func=AF.Abs,
                                                       bias=posk.t[:, kb:kb + 1], scale=-1.0),
                         r=[pq[k], posk], w=[dmf[k]])
                    if s == 0:
                        C.op("pool", lambda e: e.tensor_tensor(out=dmf[k].t[:, 0, :], in0=dmf[k].t[:, 0, :],
                                                               in1=mlo[:], op=ALU.add), r=[dmf[k], mlo], w=[dmf[k]])
                    elif s == 2:
                        C.op("pool", lambda e: e.tensor_tensor(out=dmf[k].t[:, 2, :], in0=dmf[k].t[:, 2, :],
                                                               in1=mhi[:], op=ALU.add), r=[dmf[k], mhi], w=[dmf[k]])
                C.op("pool", lambda e: e.tensor_copy(out=Dm[k].t[:, s0:s0 + len(kbs), :],
                                                     in_=dmf[k].t[:, s0:s0 + len(kbs), :]), r=[dmf[k]], w=[Dm[k]])

            def conv_branch(j):
                k = j % 2
                z_m1, z_0, z_p1 = zw[k]
                C.op("pool", lambda e: e.tensor_tensor(out=z_m1[:], in0=z_m1[:], in1=cw[0][:], op=ALU.mult),
                     r=[z_m1, cw[0]], w=[z_m1])
                C.op("pool", lambda e: e.tensor_tensor(out=z_0[:], in0=z_0[:], in1=cw[1][:], op=ALU.mult),
                     r=[z_0, cw[1]], w=[z_0])
                C.op("pool", lambda e: e.tensor_tensor(out=z_p1[:], in0=z_p1[:], in1=cw[2][:], op=ALU.mult),
                     r=[z_p1, cw[2]], w=[z_p1])
                C.op("dve", lambda e: e.tensor_tensor(out=z_0[:], in0=z_0[:], in1=z_m1[:], op=ALU.add),
                     r=[z_0, z_m1], w=[z_0])
                C.op("dve", lambda e: e.tensor_tensor(out=z_0[:], in0=z_0[:], in1=z_p1[:], op=ALU.add),
                     r=[z_0, z_p1], w=[z_0])
                C.op("dve", lambda e: e.tensor_tensor(out=z_0[:], in0=z_0[:], in1=bg2[k][:], op=ALU.mult),
                     r=[z_0, bg2[k]], w=[z_0])
                C.op("act", lambda e: e.activation(out=junk.t[:, 512:1024], in_=z_0[:], func=AF.Square,
                                                   accum_out=ssb[:]), r=[z_0], w=[junk, ssb])
                rstd_from_sum(ssb, 512, rb)
                C.op("dve", lambda e: e.scalar_tensor_tensor(out=yg[k].t[:, 512:1024], in0=z_0[:], scalar=rb[:, 0:1],
                                                             in1=bnb[:], op0=ALU.mult, op1=ALU.mult),
                     r=[z_0, rb, bnb], w=[yg[k]])

            hc = {"n": 0}

            def heads(j, mid=None):
                k = j % 2
                kbs = kbs_of(j)
                ns = len(kbs)
                s0 = 0 if j > 0 else 1
                slots = {}

                def front(hh):
                    hk, gq = hh // 4, hh % 4
                    n = hc["n"]
                    hc["n"] += 1
                    stb = PF[n % 4]
                    ptt = pTt[n % 4]
                    slots[hh] = ptt
                    stv = stb.t[:, 0:384].rearrange("p (s q) -> p s q", s=3)
                    for kb in kbs:
                        s = kb - j + 1
                        C.op("pe", lambda e: e.matmul(
                            stv[:, s, :], lhsT=kT.t[hk * 64:(hk + 1) * 64, kb * 128:(kb + 1) * 128],
                            rhs=qT[k].t[hk * 64:(hk + 1) * 64, gq, :], start=True, stop=False),
                            r=[kTB[kb], qT[k]], w=[stb])
                        C.op("pe", lambda e: e.matmul(
                            stv[:, s, :], lhsT=nsI.t[:, hh, :], rhs=Dm[k].t[:, s, :], start=False, stop=True),
                            r=[nsI, Dm[k]], w=[stb])
                    C.op("act", lambda e: e.activation(
                        out=ptt.t[:, s0:s0 + ns, :], in_=stv[:, s0:s0 + ns, :], func=AF.Exp, scale=0.125),
                        r=[stb], w=[ptt])

                def back(hh):
                    hk = hh // 4
                    ptt = slots[hh]
                    pob = po[hh // 4]
                    pov = pob.t[:, 0:260].rearrange("p (h d) -> p h d", h=4)
                    for n_, kb in enumerate(kbs):
                        s = kb - j + 1
                        C.op("pe", lambda e: e.matmul(
                            pov[:, hh % 4, :], lhsT=ptt.t[:, s, :], rhs=vall.t[:, kb, hk, :],
                            start=(n_ == 0), stop=(n_ == ns - 1)), r=[ptt, vB[kb]], w=[pob])

                LOOK = 3
                for hh in range(8 + LOOK):
                    if hh < 8:
                        front(hh)
                    if hh == 5 and mid is not None:
                        mid()
                    if hh >= LOOK:
                        back(hh - LOOK)

            def tail(j):
                k = j % 2
                for g2 in range(2):
                    pov = po[g2].t[:, 0:260].rearrange("p (h d) -> p h d", h=4)
                    C.op("dve", lambda e: e.tensor_tensor(
                        out=den.t[:, g2 * 4:(g2 + 1) * 4], in0=pov[:, :, 64], in1=esink.t[:, g2 * 4:(g2 + 1) * 4],
                        op=ALU.add), r=[po[g2], esink], w=[den])
                C.op("dve", lambda e: e.reciprocal(out=den[:], in_=den[:]), r=[den], w=[den])
                for g2 in range(2):
                    pov = po[g2].t[:, 0:260].rearrange("p (h d) -> p h d", h=4)
                    C.op("dve", lambda e: e.tensor_tensor(
                        out=ya.t[:, g2 * 4:(g2 + 1) * 4, :], in0=pov[:, :, 0:64],
                        in1=den.t[:, g2 * 4:(g2 + 1) * 4].unsqueeze(2).to_broadcast([128, 4, 64]), op=ALU.mult),
                        r=[po[g2], den], w=[ya])
                yaf = ya.t[:].rearrange("p h d -> p (h d)")
                C.op("act", lambda e: e.activation(out=junk.t[:, 0:512], in_=yaf, func=AF.Square, accum_out=ssa[:]),
                     r=[ya], w=[junk, ssa])
                rstd_from_sum(ssa, 512, ra)
                C.op("dve", lambda e: e.scalar_tensor_tensor(out=yg[k].t[:, 0:512], in0=yaf, scalar=ra[:, 0:1],
                                                             in1=anb[:], op0=ALU.mult, op1=ALU.mult),
                     r=[ya, ra, anb], w=[yg[k]])
                C.op("dve", lambda e: e.tensor_tensor(out=ygo[k][:], in0=yg[k][:], in1=sz2[k][:], op=ALU.mult),
                     r=[yg[k], sz2[k]], w=[ygo[k]])
                C.dma("sp", yg_s[j * 128:(j + 1) * 128, :], ygo[k][:], r=[ygo[k]], w=[ygB[j]])

            ld2(0)
            prep_attn(0)
            conv_branch(0)
            for j in range(NT):
                if j + 1 < NT:
                    ld2(j + 1)
                if hooks.get("s2_thunks"):
                    hooks["s2_thunks"].pop(0)()
                heads(j, mid=(lambda: prep_attn(j + 1)) if j + 1 < NT else None)
                tail(j)
                if j + 1 < NT:
                    conv_branch(j + 1)

      if "pre_p" in hooks:
          hooks["pre_p"]()
      post_stage(0, xsrc, xsrcB, dst, dstB, wts=hooks.get("wpost"), thunks=hooks.get("p_thunks"))

    def alloc_l1_weights(right):
        al = C.sbr if right else C.sb
        return (al([128, 8, 2464], BF16, "w_in1"), al([128, 2, 768], BF16, "w_uq"), al([128, 1024], BF16, "w_ukv"),
                al([128, 4, 128], BF16, "w_sT"))

    def load_l1_weights(ws, thunks=None):
        w1, wuq, wukv, wsT = ws
        load_weight(lambda c: w1.t[:, c, :], w_in1_d, 8, w1, thunks)
        load_weight(lambda c: wuq.t[:, c, :], w_uq_d, 2, wuq, thunks)
        g1 = lambda: C.dma("pool", wukv[:], w_ukv_d, w=[wukv])
        g2 = lambda: C.dma("pool", wsT[:], w_sT_d, w=[wsT])
        for g in (g1, g2):
            if thunks is None:
                g()
            else:
                thunks.append(g)

    def layer1(xsrc, xsrcB, dst, dstB, hooks=None):
      hooks = hooks or {}
      import math
      TWO_PI = 2.0 * math.pi
      C1 = 6.28125
      C2 = TWO_PI - C1
      PI_S = 3.1415925
      SCALE = 96.0 ** -0.5
      with C.scope():
        qT_s = dscr("qT_s", [96, 8, SEQ], BF16)
        szc_s = dscr("szc_s", [SEQ, 512])
        qTB = [Buf() for _ in range(NT)]
        szcB = [Buf() for _ in range(NT)]
        KT = C.sb([128, 8, SEQ], BF16, "KT")
        vall = C.sb([128, NT, 8, 65], BF16, "vall1")
        KTB = [Buf() for _ in range(NT)]
        vB = [Buf() for _ in range(NT)]
        C.op("pool", lambda e: e.memset(vall[:], 1.0), w=[vall] + vB)
        KTpad = Buf("KTpad")

        with C.scope():
            if "w_s1" in hooks:
                w1, wuq, wukv, wsT = hooks["w_s1"]
            else:
                w1, wuq, wukv, wsT = ws_ = alloc_l1_weights(False)
                load_l1_weights(ws_)
            bsT = C.sb([128, 4], F32, "bsT")
            C.dma("sp", bsT[:], b_sT_d, w=[bsT])

            def bc(src, n, name):
                t = C.sb([128, n], F32, name)
                C.dma("sp", t[:], src.partition_broadcast(128), w=[t])
                return t

            qnb = bc(qn_d, 256, "qnb")
            kvnb = bc(kvn_d, 128, "kvnb")
            vlgb = bc(vlg_d, 512, "vlgb")
            vlbb = bc(vlb_d, 512, "vlbb")
            dnb = bc(dn_d, 512, "dnb")
            invb = bc(invf_d, 16, "invb")
            negpi = C.sb([128, 1], F32, "negpi")
            sc = C.sb([128, NT, 32], F32, "sc")
            with C.scope():
                ang = C.sb([128, NT, 32], F32, "ang")
                kf = C.sb([128, NT, 32], F32, "kf")
                ki = C.sb([128, NT, 32], I32, "ki")
                C.op("dve", lambda e: e.tensor_tensor(out=ang.t[:, :, 0:16],
                                                      in0=posk.t[:].unsqueeze(2).to_broadcast([128, NT, 16]),
                                                      in1=invb.t[:].unsqueeze(1).to_broadcast([128, NT, 16]),
                                                      op=ALU.mult), r=[posk, invb], w=[ang])
                C.op("dve", lambda e: e.tensor_scalar(out=ang.t[:, :, 16:32], in0=ang.t[:, :, 0:16], scalar1=0.5 * math.pi,
                                                      scalar2=None, op0=ALU.add), r=[ang], w=[ang])
                C.op("dve", lambda e: e.tensor_scalar(out=kf[:], in0=ang[:], scalar1=1.0 / TWO_PI, scalar2=None,
                                                      op0=ALU.mult), r=[ang], w=[kf])
                C.op("dve", lambda e: e.tensor_copy(out=ki[:], in_=kf[:]), r=[kf], w=[ki])
                C.op("dve", lambda e: e.tensor_copy(out=kf[:], in_=ki[:]), r=[ki], w=[kf])
                C.op("dve", lambda e: e.scalar_tensor_tensor(out=ang[:], in0=kf[:], scalar=-C1, in1=ang[:],
                                                             op0=ALU.mult, op1=ALU.add), r=[kf, ang], w=[ang])
                C.op("dve", lambda e: e.scalar_tensor_tensor(out=ang[:], in0=kf[:], scalar=-C2, in1=ang[:],
                                                             op0=ALU.mult, op1=ALU.add), r=[kf, ang], w=[ang])
                C.op("dve", lambda e: e.tensor_scalar(out=ang[:], in0=ang[:], scalar1=-PI_S, scalar2=PI_S,
                                                      op0=ALU.max, op1=ALU.min), r=[ang], w=[ang])
                C.op("act", lambda e: e.activation(out=sc[:], in_=ang[:], func=AF.Sin), r=[ang], w=[sc])

            xb = [C.sb([128, DM], BF16, "xb") for _ in range(2)]
            xT = [C.sb([128, 8, 128], BF16, "xT") for _ in range(2)]
            cqn = [C.sb([128, 256], BF16, "cqn") for _ in range(2)]
            cqnT = C.sb([128, 2, 128], BF16, "cqnT")
            cn = [C.sb([128, 128], BF16, "cn") for _ in range(2)]
            cnT = C.sb([128, 128], BF16, "cnT")
            qfull = C.sb([128, 8 * 96], BF16, "qfull")
            kfull = C.sb([128, 8 * 96], BF16, "kfull")
            qTt = [C.sb([128, 8, 128], BF16, "qTt") for _ in range(2)]
            kro = [C.sb([128, 32], F32, "kro") for _ in range(2)]
            tq = [C.sb([128, 4, 16], F32, "tq") for _ in range(4)]
            tk = [C.sb([128, 16], F32, "tk") for _ in range(4)]
            gu = [C.sb([128, 512], F32, "gu") for _ in range(2)]
            gtmp2 = [C.sb([128, 512], F32, "gtmp") for _ in range(2)]
            xstg = [C.sb([128, 512], F32, "xstg") for _ in range(3)]
            gv = [C.sb([128, 512], F32, "gv") for _ in range(2)]
            vn = C.sb([128, 512], BF16, "vn")
            szd = [C.sb([128, 512], F32, "szd") for _ in range(2)]
            szc = [C.sb([128, 512], F32, "szc") for _ in range(2)]
            yd = C.sb([128, 512], F32, "yd")
            ygd = [C.sb([128, 512], BF16, "ygd") for _ in range(2)]
            st1 = {n: C.sb([128, 1], F32, n) for n in ("ssq", "rq", "ssk", "rk", "ms", "nm", "vs", "rv", "ssd", "rd")}
            groups = [(0, 416), (416, 512), (928, 512), (1440, 512), (1952, 512)]
            gbank = [PF[0], PF[1], PF[2], PF[0], PF[1]]

            def ld1(i):
                C.dma("pool", xb[i % 2][:], xsrc[i * 128:(i + 1) * 128, :], r=([xsrcB[i]] if xsrcB else []),
                      w=[xb[i % 2]])

            def silu_from(bank_, out_):
                zs = xstg[2]
                C.op("act", lambda e: e.activation(out=out_[:], in_=bank_.t[:], func=AF.Sigmoid), r=[bank_], w=[out_])
                C.op("act", lambda e: e.copy(out=zs[:], in_=bank_.t[:]), r=[bank_], w=[zs])
                C.op("pool", lambda e: e.tensor_tensor(out=out_[:], in0=out_[:], in1=zs[:], op=ALU.mult),
                     r=[out_, zs], w=[out_])

            def gelu_from(bank_, g_, which):
                gtmp = gtmp2[which]
                xs = xstg[which]
                C.op("act", lambda e: e.activation(out=gtmp[:], in_=bank_.t[:], func=AF.Square), r=[bank_], w=[gtmp])
                C.op("act", lambda e: e.copy(out=xs[:], in_=bank_.t[:]), r=[bank_], w=[xs])
                C.op("dve", lambda e: e.tensor_scalar(out=gtmp[:], in0=gtmp[:], scalar1=0.044715, scalar2=1.0,
                                                      op0=ALU.mult, op1=ALU.add), r=[gtmp], w=[gtmp])
                C.op("pool", lambda e: e.tensor_tensor(out=gtmp[:], in0=gtmp[:], in1=xs[:], op=ALU.mult),
                     r=[gtmp, xs], w=[gtmp])
                C.op("act", lambda e: e.activation(out=gtmp[:], in_=gtmp[:], func=AF.Sigmoid,
                                                   scale=1.5957691216057308), r=[gtmp], w=[gtmp])
                C.op("pool", lambda e: e.tensor_tensor(out=g_[:], in0=gtmp[:], in1=xs[:], op=ALU.mult),
                     r=[gtmp, xs], w=[g_])

            def mm_group(i, gi):
                k = i % 2
                n0, wd = groups[gi]
                bank = gbank[gi]
                for c in range(8):
                    C.op("pe", lambda e: e.matmul(bank.t[:, 0:wd], lhsT=xT[k].t[:, c, :], rhs=w1.t[:, c, n0:n0 + wd],
                                                  start=(c == 0), stop=(c == 7)), r=[xT[k], w1], w=[bank])

            def A1(i):
                k = i % 2
                if i < 8:
                    C.op("pool", lambda e: e.memset(KT.t[96:128, i, :], 0.0), w=[KTpad])
                transpose_to(xb[k], 8, 128, lambda: xT[k].t[:], xT[k], eng="act")
                mm_group(i, 0)
                mm_group(i, 1)
                mm_group(i, 2)

            def A2(i):
                k = i % 2
                g0 = gbank[0]
                C.op("act", lambda e: e.activation(out=junk.t[:, 0:256], in_=g0.t[:, 0:256], func=AF.Square,
                                                   accum_out=st1["ssq"][:]), r=[g0], w=[junk, st1["ssq"]])
                C.op("act", lambda e: e.activation(out=junk.t[:, 256:384], in_=g0.t[:, 256:384], func=AF.Square,
                                                   accum_out=st1["ssk"][:]), r=[g0], w=[junk, st1["ssk"]])
                sinb = sc.t[:, i, 0:16]
                cosb = sc.t[:, i, 16:32]
                x1 = g0.t[:, 384:400]
                x2 = g0.t[:, 400:416]
                C.op("dve", lambda e: e.tensor_tensor(out=tk[0][:], in0=x1, in1=cosb, op=ALU.mult), r=[g0, sc], w=[tk[0]])
                C.op("dve", lambda e: e.tensor_tensor(out=tk[1][:], in0=x2, in1=sinb, op=ALU.mult), r=[g0, sc], w=[tk[1]])
                C.op("dve", lambda e: e.tensor_tensor(out=tk[2][:], in0=x1, in1=sinb, op=ALU.mult), r=[g0, sc], w=[tk[2]])
                C.op("dve", lambda e: e.tensor_tensor(out=tk[3][:], in0=x2, in1=cosb, op=ALU.mult), r=[g0, sc], w=[tk[3]])
                rstd_from_sum(st1["ssq"], 256, st1["rq"])
                rstd_from_sum(st1["ssk"], 128, st1["rk"])
                C.op("dve", lambda e: e.tensor_tensor(out=kro[k].t[:, 0:16], in0=tk[0][:], in1=tk[1][:],
                                                      op=ALU.subtract), r=[tk[0], tk[1]], w=[kro[k]])
                C.op("dve", lambda e: e.tensor_tensor(out=kro[k].t[:, 16:32], in0=tk[2][:], in1=tk[3][:], op=ALU.add),
                     r=[tk[2], tk[3]], w=[kro[k]])
                C.op("dve", lambda e: e.scalar_tensor_tensor(out=cqn[k][:], in0=g0.t[:, 0:256],
                                                             scalar=st1["rq"].t[:, 0:1], in1=qnb[:], op0=ALU.mult,
                                                             op1=ALU.mult), r=[g0, st1["rq"], qnb], w=[cqn[k]])
                C.op("dve", lambda e: e.scalar_tensor_tensor(out=cn[k][:], in0=g0.t[:, 256:384],
                                                             scalar=st1["rk"].t[:, 0:1], in1=kvnb[:], op0=ALU.mult,
                                                             op1=ALU.mult), r=[g0, st1["rk"], kvnb], w=[cn[k]])

            def A2b(i):
                k = i % 2
                mm_group(i, 3)
                gelu_from(gbank[1], gu[k], 0)
                mm_group(i, 4)
                gelu_from(gbank[2], gv[k], 1)
                silu_from(gbank[3], szc[k])
                C.dma("sp", szc_s[i * 128:(i + 1) * 128, :], szc[k][:], r=[szc[k]], w=[szcB[i]])
                silu_from(gbank[4], szd[k])

            qb_ = (PF[3], PF[4])
            kvb = (PF[5], PF[3])

            def B1(i):
                k = i % 2
                transpose_to(cqn[k], 2, 128, lambda: cqnT.t[:], cqnT, eng="dve")
                transpose_to(cn[k], 1, 128, lambda: cnT.t[:].unsqueeze(1), cnT, eng="dve")
                for b2 in range(2):
                    for c in range(2):
                        C.op("pe", lambda e: e.matmul(qb_[b2].t[:, 0:384], lhsT=cqnT.t[:, c, :],
                                                      rhs=wuq.t[:, c, b2 * 384:(b2 + 1) * 384], start=(c == 0),
                                                      stop=(c == 1)), r=[cqnT, wuq], w=[qb_[b2]])
                C.op("pe", lambda e: e.matmul(kvb[0].t[:], lhsT=cnT.t[:], rhs=wukv.t[:, 0:512],
                                              start=True, stop=True), r=[cnT, wukv], w=[kvb[0]])

            def B2(i):
                k = i % 2
                qf3 = qfull.t[:].rearrange("p (h d) -> p h d", h=8)
                kf3 = kfull.t[:].rearrange("p (h d) -> p h d", h=8)
                cos4 = sc.t[:, i, 16:32].unsqueeze(1).to_broadcast([128, 4, 16])
                sin4 = sc.t[:, i, 0:16].unsqueeze(1).to_broadcast([128, 4, 16])
                gvk = gv[k]
                C.op("act", lambda e: e.activation(out=junk.t[:, 0:512], in_=gvk[:], func=AF.Copy,
                                                   accum_out=st1["ms"][:]), r=[gvk], w=[junk, st1["ms"]])
                for b2 in range(2):
                    hs = slice(b2 * 4, b2 * 4 + 4)
                    qv = qb_[b2].t[:, 0:384].rearrange("p (h d) -> p h d", h=4)
                    C.op("act", lambda e: e.copy(out=qf3[:, hs, 0:64], in_=qv[:, :, 0:64]), r=[qb_[b2]], w=[qfull])
                    C.op("dve", lambda e: e.tensor_tensor(out=tq[0][:], in0=qv[:, :, 64:80], in1=cos4, op=ALU.mult),
                         r=[qb_[b2], sc], w=[tq[0]])
                    C.op("dve", lambda e: e.tensor_tensor(out=tq[1][:], in0=qv[:, :, 80:96], in1=sin4, op=ALU.mult),
                         r=[qb_[b2], sc], w=[tq[1]])
                    C.op("dve", lambda e: e.tensor_tensor(out=tq[2][:], in0=qv[:, :, 64:80], in1=sin4, op=ALU.mult),
                         r=[qb_[b2], sc], w=[tq[2]])
                    C.op("dve", lambda e: e.tensor_tensor(out=tq[3][:], in0=qv[:, :, 80:96], in1=cos4, op=ALU.mult),
                         r=[qb_[b2], sc], w=[tq[3]])
                    C.op("dve", lambda e: e.tensor_tensor(out=qf3[:, hs, 64:80], in0=tq[0][:], in1=tq[1][:],
                                                          op=ALU.subtract), r=[tq[0], tq[1]], w=[qfull])
                    C.op("dve", lambda e: e.tensor_tensor(out=qf3[:, hs, 80:96], in0=tq[2][:], in1=tq[3][:],
                                                          op=ALU.add), r=[tq[2], tq[3]], w=[qfull])
                    if b2 == 0:
                        C.op("pe", lambda e: e.matmul(kvb[1].t[:], lhsT=cnT.t[:], rhs=wukv.t[:, 512:1024],
                                                      start=True, stop=True), r=[cnT, wukv], w=[kvb[1]])
                        C.op("dve", lambda e: e.tensor_scalar(out=st1["nm"][:], in0=st1["ms"][:], scalar1=-1.0 / 512,
                                                              scalar2=None, op0=ALU.mult), r=[st1["ms"]], w=[st1["nm"]])
                        C.op("act", lambda e: e.activation(out=gvk[:], in_=gvk[:], func=AF.Identity,
                                                           bias=st1["nm"].t[:, 0:1], scale=1.0),
                             r=[gvk, st1["nm"]], w=[gvk])
                        C.op("act", lambda e: e.activation(out=junk.t[:, 512:1024], in_=gvk[:], func=AF.Square,
                                                           accum_out=st1["vs"][:]), r=[gvk], w=[junk, st1["vs"]])
                for b2 in range(2):
                    hs = slice(b2 * 4, b2 * 4 + 4)
                    kvv = kvb[b2].t[:].rearrange("p (h d) -> p h d", h=4)
                    C.op("act", lambda e: e.copy(out=kf3[:, hs, 0:64], in_=kvv[:, :, 0:64]), r=[kvb[b2]], w=[kfull])
                    C.op("act", lambda e: e.copy(out=vall.t[:, i, hs, 0:64], in_=kvv[:, :, 64:128]), r=[kvb[b2]],
                         w=[vB[i]])
                C.op("dve", lambda e: e.tensor_copy(out=kf3[:, :, 64:96],
                                                    in_=kro[k].t[:].unsqueeze(1).to_broadcast([128, 8, 32])),
                     r=[kro[k]], w=[kfull])
                rstd_from_sum(st1["vs"], 512, st1["rv"])
                C.op("dve", lambda e: e.scalar_tensor_tensor(out=gvk[:], in0=gvk[:], scalar=st1["rv"].t[:, 0:1],
                                                             in1=vlgb[:], op0=ALU.mult, op1=ALU.mult),
                     r=[gvk, st1["rv"], vlgb], w=[gvk])
                C.op("dve", lambda e: e.tensor_tensor(out=vn[:], in0=gvk[:], in1=vlbb[:], op=ALU.add),
                     r=[gvk, vlbb], w=[vn])

            def B3(i):
                k = i % 2
                guk = gu[k]
                transpose_to(qfull, 8, 96, lambda: qTt[k].t[0:96, :, :], qTt[k], eng="act")
                C.dma("sp", qT_s[:, :, i * 128:(i + 1) * 128], qTt[k].t[0:96, :, :], r=[qTt[k]], w=[qTB[i]])
                transpose_to(kfull, 8, 96, lambda: KT.t[0:96, :, i * 128:(i + 1) * 128], KTB[i], eng="dve")
                mixb = PF[4]
                for g4 in range(4):
                    C.op("pe", lambda e: e.matmul(mixb.t[:, g4 * 128:(g4 + 1) * 128], lhsT=wsT.t[:, g4, :],
                                                  rhs=vn.t[:, g4 * 128:(g4 + 1) * 128], start=True, stop=True),
                         r=[wsT, vn], w=[mixb])
                for g4 in range(4):
                    sl = slice(g4 * 128, (g4 + 1) * 128)
                    C.op("dve", lambda e: e.scalar_tensor_tensor(out=yd.t[:, sl], in0=mixb.t[:, sl],
                                                                 scalar=bsT.t[:, g4:g4 + 1], in1=guk.t[:, sl],
                                                                 op0=ALU.add, op1=ALU.mult),
                         r=[mixb, bsT, guk], w=[yd])
                C.op("act", lambda e: e.activation(out=junk.t[:, 0:512], in_=yd[:], func=AF.Square,
                                                   accum_out=st1["ssd"][:]), r=[yd], w=[junk, st1["ssd"]])
                rstd_from_sum(st1["ssd"], 512, st1["rd"])
                C.op("dve", lambda e: e.scalar_tensor_tensor(out=yd[:], in0=yd[:], scalar=st1["rd"].t[:, 0:1],
                                                             in1=dnb[:], op0=ALU.mult, op1=ALU.mult),
                     r=[yd, st1["rd"], dnb], w=[yd])
                C.op("dve", lambda e: e.tensor_tensor(out=ygd[k][:], in0=yd[:], in1=szd[k][:], op=ALU.mult),
                     r=[yd, szd[k]], w=[ygd[k]])
                C.dma("sp", yg_s[i * 128:(i + 1) * 128, 512:1024], ygd[k][:], r=[ygd[k]], w=[ygB[i]])

            ld1(0)
            ld1(1)
            A1(0)
            A2(0)
            A2b(0)
            for i in range(NT):
                if i + 2 < NT:
                    ld1(i + 2)
                B1(i)
                if i + 1 < NT:
                    A1(i + 1)
                if i + 1 < NT:
                    A2(i + 1)
                B2(i)
                if i + 1 < NT:
                    A2b(i + 1)
                B3(i)

        if "pre_s2" in hooks:
            hooks["pre_s2"]()
        with C.scope():
            cnb = C.sb([128, 512], F32, "cnb")
            C.dma("sp", cnb[:], cn_d.partition_broadcast(128), w=[cnb])
            qTg = [C.sb([128, 8, 512], BF16, "qTg") for _ in range(2)]
            ptt = [C.sb([128, 512], BF16, "ptt") for _ in range(4)]
            ycg = C.sb([128, 4, 512], F32, "ycg")
            szc2 = [C.sb([128, 512], F32, "szc2") for _ in range(2)]
            ygc = [C.sb([128, 512], BF16, "ygc") for _ in range(2)]
            rden = C.sb([128, 4], F32, "rden")
            ssc = C.sb([128, 1], F32, "ssc")
            rc = C.sb([128, 1], F32, "rc")
            NG = SEQ // 512

            def ldq(g8):
                C.dma("sp", qTg[g8 % 2].t[0:96, :, :], qT_s[:, :, g8 * 512:(g8 + 1) * 512],
                      r=[qTB[g8 * 4 + t4] for t4 in range(4)], w=[qTg[g8 % 2]])

            for qg_ in qTg:
                C.op("pool", lambda e: e.memset(qg_[:], 0.0), w=[qg_])
            ldq(0)
            for g_ in hooks.get("s2_thunks", []):
                g_()
            iters = [(g8, hh, kt) for g8 in range(NG) for hh in range(8) for kt in range(NT)]
            LOOK = 3

            def front(n):
                g8, hh, kt = iters[n]
                if hh == 0 and kt == 0 and g8 + 1 < NG:
                    ldq(g8 + 1)
                qg = qTg[g8 % 2]
                stb = PF[n % 4]
                pt_ = ptt[n % 4]
                C.op("pe", lambda e: e.matmul(stb.t[:], lhsT=KT.t[:, hh, kt * 128:(kt + 1) * 128],
                                              rhs=qg.t[:, hh, :], start=True, stop=True),
                     r=[KTB[kt], KTpad, qg], w=[stb])
                C.op("act", lambda e: e.activation(out=pt_[:], in_=stb.t[:], func=AF.Exp, scale=SCALE),
                     r=[stb], w=[pt_])

            def back(n):
                g8, hh, kt = iters[n]
                pt_ = ptt[n % 4]
                pob = PF[4 + hh % 2]
                pov = pob.t[:, 0:260].rearrange("p (q d) -> p q d", q=4)
                for qi in range(4):
                    C.op("pe", lambda e: e.matmul(pov[:, qi, :], lhsT=pt_.t[:, qi * 128:(qi + 1) * 128],
                                                  rhs=vall.t[:, kt, hh, :], start=(kt == 0 and qi == 0),
                                                  stop=(kt == NT - 1), skip_group_check=True),
                         r=[pt_, vB[kt]], w=[pob])
                if kt != NT - 1:
                    return
                C.op("dve", lambda e: e.reciprocal(out=rden[:], in_=pov[:, :, 64]), r=[pob], w=[rden])
                C.op("dve", lambda e: e.tensor_tensor(out=ycg.t[:, :, hh * 64:(hh + 1) * 64], in0=pov[:, :, 0:64],
                                                      in1=rden.t[:].unsqueeze(2).to_broadcast([128, 4, 64]),
                                                      op=ALU.mult), r=[pob, rden], w=[ycg])
                if hh != 7:
                    return
                for qi in range(4):
                    ti = g8 * 4 + qi
                    k = ti % 2
                    C.dma("sp", szc2[k][:], szc_s[ti * 128:(ti + 1) * 128, :], r=[szcB[ti]], w=[szc2[k]])
                    C.op("act", lambda e: e.activation(out=junk.t[:, 0:512], in_=ycg.t[:, qi, :], func=AF.Square,
                                                       accum_out=ssc[:]), r=[ycg], w=[junk, ssc])
                    rstd_from_sum(ssc, 512, rc)
                    C.op("dve", lambda e: e.scalar_tensor_tensor(out=ycg.t[:, qi, :], in0=ycg.t[:, qi, :],
                                                                 scalar=rc.t[:, 0:1], in1=cnb[:], op0=ALU.mult,
                                                                 op1=ALU.mult), r=[ycg, rc, cnb], w=[ycg])
                    C.op("dve", lambda e: e.tensor_tensor(out=ygc[k][:], in0=ycg.t[:, qi, :], in1=szc2[k][:],
                                                          op=ALU.mult), r=[ycg, szc2[k]], w=[ygc[k]])
                    C.dma("sp", yg_s[ti * 128:(ti + 1) * 128, 0:512], ygc[k][:], r=[ygc[k]], w=[ygB[ti]])

            for n in range(len(iters) + LOOK):
                if n < len(iters):
                    front(n)
                if n >= LOOK:
                    back(n - LOOK)
      post_stage(1, xsrc, xsrcB, dst, dstB, wts=hooks.get("wpost"))

    if layers == (0,):
        layer0(x_in, None, out_d, None)
    elif layers == (1,):
        layer1(x_in, None, out_d, None)
    else:
        pre = {}

        def l0_pre_s2():
            pre["w_s1"] = alloc_l1_weights(True)
            pre["wpost0"] = alloc_post_weights(True)
            h0["wpost"] = pre["wpost0"]
            h0["s2_thunks"] = []
            load_post_weights(0, pre["wpost0"], h0["s2_thunks"])

        def l0_pre_p():
            while h0["s2_thunks"]:
                h0["s2_thunks"].pop(0)()
            h0["p_thunks"] = []
            load_l1_weights(pre["w_s1"], h0["p_thunks"])

        h0 = {"pre_s2": l0_pre_s2, "pre_p": l0_pre_p}
        layer0(x_in, None, x_mid, xmidB, hooks=h0)
        while h0.get("p_thunks"):
            h0["p_thunks"].pop(0)()
        C.rpop(1)
        h1 = {"w_s1": pre["w_s1"]}

        def l1_pre_s2():
            C.rpop(4)
            pre["wpost1"] = alloc_post_weights(True)
            h1["wpost"] = pre["wpost1"]
            h1["s2_thunks"] = []
            load_post_weights(1, pre["wpost1"], h1["s2_thunks"])

        h1["pre_s2"] = l1_pre_s2
        layer1(x_mid, xmidB, out_d, None, hooks=h1)
        C.rpop(1)
    C.S.emit()
    nc.dbg_names = dbg_names
    return nc


def _host_inputs(inputs, layers):
    f = lambda a: np.ascontiguousarray(np.asarray(a))
    x = f(inputs["x"])
    p = f(inputs["p"])
    pos = f(inputs["positions"]).astype(np.int32)
    shared = {}
    for l in layers:
        shared[f"ln_g{l}"] = f(inputs["post_ln_g"][l])
        shared[f"ln_b{l}"] = f(inputs["post_ln_b"][l])
        shared[f"proj{l}"] = f(inputs["ple_proj"][l])
        shared[f"gate{l}"] = f(inputs["ple_gate"][l])
    if 0 in layers:
        w = f(inputs["ev_w_in"][0])
        qcols = w[:, 0:512].reshape(1024, 8, 64)
        order = [0, 4, 1, 5, 2, 6, 3, 7]
        wq = qcols[:, order, :].reshape(1024, 512)
        shared["w_in0"] = f(np.concatenate([wq, w[:, 512:]], axis=1))
        shared["conv_w"] = f(inputs["ev_conv_w"][0])
        shared["sink"] = f(inputs["ev_sink"][0])
        shared["a_norm"] = f(inputs["ev_a_norm"][0])
        shared["b_norm"] = f(inputs["ev_b_norm"][0])
        shared["w_out0"] = f(inputs["ev_w_out"][0])
    if 1 in layers:
        shared["w_in1"] = f(inputs["od_w_in"][0])
        shared["w_uq"] = f(inputs["od_w_uq"][0])
        shared["w_ukv"] = f(inputs["od_w_ukv"][0])
        shared["w_sT"] = f(np.transpose(np.asarray(inputs["od_w_s"][0]), (2, 0, 1)))
        shared["b_sT"] = f(np.transpose(np.asarray(inputs["od_b_s"][0]), (1, 0)))
        shared["q_norm"] = f(inputs["od_q_norm"][0])
        shared["kv_norm"] = f(inputs["od_kv_norm"][0])
        shared["v_ln_g"] = f(inputs["od_v_ln_g"][0])
        shared["v_ln_b"] = f(inputs["od_v_ln_b"][0])
        shared["c_norm"] = f(inputs["od_c_norm"][0])
        shared["d_norm"] = f(inputs["od_d_norm"][0])
        shared["w_out1"] = f(inputs["od_w_out"][0])
        half = 16
        shared["inv_freq"] = (10000.0 ** (-np.arange(half, dtype=np.float32) / half)).astype(np.float32)
    maps = []
    for b in range(x.shape[0]):
        m = dict(shared)
        m["x"] = f(x[b])
        for l in layers:
            m[f"p{l}"] = f(p[l, b])
        m["pos"] = f(pos[b])
        m["posT"] = f(pos[b].reshape(NT, 128).T)
        maps.append(m)
    return maps


_CACHE = {}


def _run(layers, inputs, ncores=NCORES, debug=False):
    key = (layers, debug)
    if key not in _CACHE:
        _CACHE[key] = build(layers, debug)
    nc = _CACHE[key]
    maps = _host_inputs(inputs, layers)[:ncores]
    res = run_bass_kernel_spmd(nc, maps, core_ids=list(range(len(maps))))
    if debug:
        return res.results
    return np.stack([r["out"] for r in res.results], axis=0)


FUSED = True


def kernel(**inputs):
    if FUSED:
        out = _run((0, 1), inputs)
    else:
        x1 = _run((0,), inputs)
        inputs1 = dict(inputs)
        inputs1["x"] = x1
        out = _run((1,), inputs1)
    return np.ascontiguousarray(out, dtype=np.float32)
```

```python
import contextlib
import numpy as np
import concourse.bass as bass
import concourse.mybir as mybir
from concourse.bass_utils import run_bass_kernel_spmd

F32 = mybir.dt.float32
BF16 = mybir.dt.bfloat16
I32 = mybir.dt.int32
AF = mybir.ActivationFunctionType
ALU = mybir.AluOpType

NCORES = 8
SEQ = 4096
DM = 1024
NT = SEQ // 128
ALPHA = 4.0 ** 0.25
EPS = 1e-6
BIG = 30000.0

ENG_NAMES = ("pe", "act", "dve", "pool", "sp")
EPOCH = 30000
INLINE_WAIT = True


class Buf:
    __slots__ = ("name", "writers", "readers", "excl")

    def __init__(self, name="", excl=False):
        self.name = name
        self.writers = []
        self.readers = []
        self.excl = excl


class Op:
    __slots__ = ("eng", "fn", "deps", "is_dma", "sig", "seq", "dsem", "dval")

    def __init__(self, eng, fn, is_dma):
        self.eng = eng
        self.fn = fn
        self.deps = []
        self.is_dma = is_dma
        self.sig = False
        self.seq = 0
        self.dsem = None
        self.dval = 0


class _Rec:
    def __init__(self):
        self.call = None

    def __getattr__(self, name):
        def f(*a, **k):
            self.call = (name, a, k)
        return f


def _replay(call):
    name, a, k = call
    return lambda e: getattr(e, name)(*a, **k)


class Sched:
    def __init__(self, nc, n_dma_sems=24):
        self.nc = nc
        self.ops = {e: [] for e in ENG_NAMES}
        self.n_dma_sems = n_dma_sems

    def op(self, eng, fn, reads=(), writes=(), dma=False):
        rec = _Rec()
        fn(rec)
        o = Op(eng, _replay(rec.call), dma)
        deps = []
        for b in reads:
            deps.extend(b.writers)
            if b.excl:
                deps.extend(r for r in b.readers if r.eng != eng)
        for b in writes:
            for r in b.readers:
                if r.is_dma or dma or r.eng != eng or eng != "pe":
                    deps.append(r)
            for w in b.writers:
                if w.is_dma or dma or w.eng != eng or eng != "pe":
                    deps.append(w)
        seen = set()
        for d in deps:
            if id(d) not in seen and d is not o:
                seen.add(id(d))
                o.deps.append(d)
                d.sig = True
        for b in reads:
            b.readers.append(o)
        for b in writes:
            if b.readers:
                b.readers = [r for r in b.readers if r is o]
                b.writers = [o]
            else:
                b.writers.append(o)
        self.ops[eng].append(o)
        return o

    def barrier(self):
        lasts = []
        for e in ENG_NAMES:
            nd = [o for o in self.ops[e] if not o.is_dma]
            if nd:
                lasts.append(nd[-1])
            dl = [o for o in self.ops[e] if o.is_dma]
            lasts.extend(dl[-self.n_dma_sems:])
        for e in ENG_NAMES:
            o = Op(e, lambda eng: eng.nop(), False)
            for d in lasts:
                if d.is_dma or d.eng != e:
                    o.deps.append(d)
                    d.sig = True
            self.ops[e].append(o)

    def emit(self):
        nc = self.nc
        with contextlib.ExitStack() as es:
            eng_sems = {}
            for e in ENG_NAMES:
                n = 0
                for o in self.ops[e]:
                    if not o.is_dma and o.sig:
                        n += 1
                        o.seq = n
                nep = max((n + EPOCH - 1) // EPOCH, 1)
                eng_sems[e] = [es.enter_context(nc.semaphore(f"s_{e}_{k}")) for k in range(nep)]
            for e in ENG_NAMES:
                dl = [o for o in self.ops[e] if o.is_dma]
                if not dl:
                    continue
                pool = [es.enter_context(nc.semaphore(f"d_{e}_{k}")) for k in range(self.n_dma_sems)]
                cnt = [0] * len(pool)
                for k, o in enumerate(dl):
                    j = k % len(pool)
                    cnt[j] += 1
                    o.dsem = pool[j]
                    o.dval = 16 * cnt[j]
            block = es.enter_context(nc.Block())

            def run_engine(ename, engobj):
                waited = {}

                def need(sem, val):
                    if waited.get(sem.num, 0) >= val:
                        return
                    waited[sem.num] = val
                    engobj.wait_ge(sem, val)

                for o in self.ops[ename]:
                    pend = []

                    def need2(sem, val):
                        if waited.get(sem.num, 0) >= val:
                            return
                        waited[sem.num] = val
                        pend.append((sem, val))

                    for d in o.deps:
                        if d.is_dma:
                            need2(d.dsem, d.dval)
                        else:
                            k = (d.seq - 1) // EPOCH
                            need2(eng_sems[d.eng][k], d.seq - k * EPOCH)
                    if o.is_dma and o.dval > 16:
                        need2(o.dsem, o.dval - 16)
                    inline = None
                    if pend and INLINE_WAIT and not o.is_dma:
                        inline = pend.pop()
                    for sem, val in pend:
                        engobj.wait_ge(sem, val)
                    ins = o.fn(engobj)
                    if inline is not None:
                        ins._wait_ge(inline[0], inline[1])
                    if o.is_dma:
                        ins.then_inc(o.dsem, 16)
                    elif o.sig:
                        k = (o.seq - 1) // EPOCH
                        ins.then_inc(eng_sems[ename][k], 1)
                last = {}
                for o in self.ops[ename]:
                    if o.is_dma:
                        last[o.dsem.num] = (o.dsem, o.dval)
                for sem, val in last.values():
                    need(sem, val)

            block.tensor(lambda eng: run_engine("pe", eng))
            block.scalar(lambda eng: run_engine("act", eng))
            block.vector(lambda eng: run_engine("dve", eng))
            block.gpsimd(lambda eng: run_engine("pool", eng))
            block.sync(lambda eng: run_engine("sp", eng))


class T:
    def __init__(self, t, name):
        self.t = t
        self.b = Buf(name)

    def __getitem__(self, k):
        return self.t[k]


def _bufs(xs):
    out = []
    for x in xs:
        if x is None:
            continue
        out.append(x.b if isinstance(x, T) else x)
    return out


class Ctx:
    def __init__(self, nc):
        self.nc = nc
        self.S = Sched(nc)
        self.n = 0
        self.stack = [contextlib.ExitStack()]
        self.rguards = []

    def sb(self, shape, dt, name=None):
        self.n += 1
        name = f"{name or 'sb'}_{self.n}"
        t = self.stack[-1].enter_context(self.nc.sbuf_tensor(name, list(shape), dt))
        return T(t, name)

    def sbr(self, shape, dt, name=None):
        self.n += 1
        name = f"{name or 'sbr'}_{self.n}"
        g = self.nc.sbuf_tensor(name, list(shape), dt, side="right")
        t = g.__enter__()
        self.rguards.append(g)
        return T(t, name)

    def rpop(self, n):
        for _ in range(n):
            self.rguards.pop().__exit__(None, None, None)

    @contextlib.contextmanager
    def scope(self):
        es = contextlib.ExitStack()
        self.stack.append(es)
        try:
            yield
        finally:
            self.S.barrier()
            self.stack.pop()
            es.close()

    def ps(self, shape, dt, name=None):
        self.n += 1
        name = f"{name or 'ps'}_{self.n}"
        t = T(self.nc.alloc_psum_tensor(name, list(shape), dt), name)
        t.b.excl = True
        return t

    def op(self, eng, fn, r=(), w=()):
        return self.S.op(eng, fn, _bufs(r), _bufs(w))

    def dma(self, q, out, in_, r=(), w=()):
        return self.S.op(q, lambda e: e.dma_start(out=out, in_=in_), _bufs(r), _bufs(w), dma=True)


def build(layers, debug=False):
    nc = bass.Bass("TRN2", target_bir_lowering=False)
    C = Ctx(nc)
    dbg_names = []

    def din(name, shape, dt=F32):
        return nc.dram_tensor(name, list(shape), dt, kind="ExternalInput").ap()

    def dscr(name, shape, dt=F32):
        if debug:
            dbg_names.append(name)
            return nc.dram_tensor(name, list(shape), dt, kind="ExternalOutput").ap()
        return nc.dram_tensor(name, list(shape), dt, kind="Internal").ap()

    def dump(name, tt, shape, dt=F32):
        if not debug:
            return
        dbg_names.append(name)
        o = nc.dram_tensor(name, list(shape), dt, kind="ExternalOutput").ap()
        C.dma("sp", o, tt[:], r=[tt])

    x_in = din("x", [SEQ, DM])
    out_d = nc.dram_tensor("out", [SEQ, DM], F32, kind="ExternalOutput").ap()
    pos_d = din("pos", [SEQ], I32)
    posT_d = din("posT", [128, NT], I32)
    p_d = {l: din(f"p{l}", [SEQ, 256]) for l in layers}
    lng_d = {l: din(f"ln_g{l}", [DM]) for l in layers}
    lnb_d = {l: din(f"ln_b{l}", [DM]) for l in layers}
    wout_d = {l: din(f"w_out{l}", [DM, DM]) for l in layers}
    gate_d = {l: din(f"gate{l}", [DM, DM]) for l in layers}
    proj_d = {l: din(f"proj{l}", [256, DM]) for l in layers}
    if 0 in layers:
        w_in0_d = din("w_in0", [DM, 3328])
        convw_d = din("conv_w", [3, 512])
        sink_d = din("sink", [8])
        anorm_d = din("a_norm", [512])
        bnorm_d = din("b_norm", [512])
    if 1 in layers:
        w_in1_d = din("w_in1", [DM, 2464])
        w_uq_d = din("w_uq", [256, 768])
        w_ukv_d = din("w_ukv", [128, 1024])
        w_sT_d = din("w_sT", [128, 4, 128])
        b_sT_d = din("b_sT", [128, 4])
        qn_d = din("q_norm", [256])
        kvn_d = din("kv_norm", [128])
        vlg_d = din("v_ln_g", [512])
        vlb_d = din("v_ln_b", [512])
        cn_d = din("c_norm", [512])
        dn_d = din("d_norm", [512])
        invf_d = din("inv_freq", [16])
    x_mid = dscr("x_mid", [SEQ, DM]) if len(layers) == 2 else None

    yg_s = dscr("yg_s", [SEQ, DM], BF16)
    ygB = [Buf(f"yg{i}") for i in range(NT)]
    xmidB = [Buf(f"xm{i}") for i in range(NT)]

    ident_f = C.sb([128, 128], F32, "identf")
    ident = C.sb([128, 128], BF16, "ident")
    eps_t = C.sb([128, 1], F32, "eps")
    posk_i = C.sb([128, NT], I32, "poski")
    posk = C.sb([128, NT], F32, "posk")
    junk = C.sb([128, DM], BF16, "junk")
    PT = [C.ps([128, 1024], BF16, "pt") for _ in range(2)]
    PF = [C.ps([128, 512], F32, "pf") for _ in range(6)]
    st = {"pt": 0}

    def next_pt():
        st["pt"] ^= 1
        return PT[st["pt"]]

    C.op("pool", lambda e: e.memset(ident_f[:], 0.0), w=[ident_f])
    C.op("pool", lambda e: e.affine_select(out=ident_f[:], in_=ident_f[:], pattern=[[-1, 128]],
                                           compare_op=ALU.not_equal, fill=1.0, base=0,
                                           channel_multiplier=1), r=[ident_f], w=[ident_f])
    C.op("dve", lambda e: e.tensor_copy(out=ident[:], in_=ident_f[:]), r=[ident_f], w=[ident])
    mlo = C.sb([128, 128], F32, "mlo")
    mhi = C.sb([128, 128], F32, "mhi")
    C.op("pool", lambda e: e.memset(mlo[:], 0.0), w=[mlo])
    C.op("pool", lambda e: e.memset(mhi[:], 0.0), w=[mhi])
    C.op("pool", lambda e: e.affine_select(out=mlo[:], in_=mlo[:], pattern=[[-1, 128]], compare_op=ALU.is_ge,
                                           fill=BIG, base=0, channel_multiplier=1), r=[mlo], w=[mlo])
    C.op("pool", lambda e: e.affine_select(out=mhi[:], in_=mhi[:], pattern=[[1, 128]], compare_op=ALU.is_ge,
                                           fill=BIG, base=0, channel_multiplier=-1), r=[mhi], w=[mhi])
    C.op("pool", lambda e: e.memset(eps_t[:], EPS), w=[eps_t])
    negh = C.sb([128, 1], F32, "negh")
    C.op("pool", lambda e: e.memset(negh[:], -0.5), w=[negh])
    C.dma("sp", posk_i[:], posT_d, w=[posk_i])
    C.op("dve", lambda e: e.tensor_copy(out=posk[:], in_=posk_i[:]), r=[posk_i], w=[posk])

    def rstd_from_sum(ssum, n, out):
        C.op("dve", lambda e: e.tensor_scalar(out=out[:], in0=ssum[:], scalar1=1.0 / n, scalar2=EPS, op0=ALU.mult,
                                              op1=ALU.add), r=[ssum], w=[out])
        C.op("pool", lambda e: e.tensor_tensor(out=out[:], in0=out[:], in1=negh[:], op=ALU.pow), r=[out, negh], w=[out])

    def transpose_to(src, nchunk, width, dst_ap_fn, dst, eng="act", rows=128):
        pt = next_pt()
        ptv = pt.t[:].rearrange("p (c t) -> p c t", t=128)
        for c in range(nchunk):
            C.op("pe", lambda e, c=c: e.transpose(out=ptv[0:width, c, :], in_=src[:, c * width:(c + 1) * width],
                                                  identity=ident[:]), r=[src, ident], w=[pt])
        if eng == "act":
            C.op("act", lambda e: e.copy(out=dst_ap_fn(), in_=ptv[0:width, 0:nchunk, :]), r=[pt], w=[dst])
        else:
            C.op(eng, lambda e: e.tensor_copy(out=dst_ap_fn(), in_=ptv[0:width, 0:nchunk, :]), r=[pt], w=[dst])

    def load_weight(dst_ap_fn, src_ap, nchunks, dstT, thunks=None):
        for c in range(nchunks):
            def go(c=c):
                C.dma("pool", dst_ap_fn(c), src_ap[c * 128:(c + 1) * 128, :], w=[dstT])
            if thunks is None:
                go()
            else:
                thunks.append(go)

    def alloc_post_weights(right):
        return (C.sbr if right else C.sb)([128, 8, 3072], BF16, "wpost")

    def load_post_weights(l, wbig, thunks=None):
        load_weight(lambda c: wbig.t[:, c, 0:1024], wout_d[l], 8, wbig, thunks)
        load_weight(lambda c: wbig.t[:, c, 1024:2048], gate_d[l], 8, wbig, thunks)
        load_weight(lambda c: wbig.t[:, c, 2048:3072], proj_d[l], 2, wbig, thunks)

    def post_stage(l, xsrc, xsrcB, dst, dstB, wts=None, thunks=None):
      with C.scope():
        if wts is None:
            wbig = alloc_post_weights(False)
            load_post_weights(l, wbig)
        else:
            wbig = wts
        lng = C.sb([128, DM], F32, "lng")
        lnb = C.sb([128, DM], F32, "lnb")
        C.dma("sp", lng[:], lng_d[l].partition_broadcast(128), w=[lng])
        C.dma("sp", lnb[:], lnb_d[l].partition_broadcast(128), w=[lnb])
        ygb = [C.sb([128, DM], BF16, "ygb") for _ in range(2)]
        ygT = [C.sb([128, 8, 128], BF16, "ygT") for _ in range(2)]
        xr = [C.sb([128, DM], F32, "xr") for _ in range(2)]
        pb = [C.sb([128, 256], BF16, "pb") for _ in range(2)]
        pT = [C.sb([128, 2, 128], BF16, "pT") for _ in range(3)]
        junk2 = C.sb([128, DM], BF16, "junk2")
        s = [C.sb([128, DM], F32, "s") for _ in range(2)]
        h2 = [C.sb([128, DM], F32, "h") for _ in range(2)]
        hb2 = [C.sb([128, DM], BF16, "hb") for _ in range(2)]
        hT2 = [C.sb([128, 8, 128], BF16, "hT") for _ in range(2)]
        sg = [C.sb([128, DM], F32, "sg") for _ in range(2)]
        msum = C.sb([128, 1], F32, "msum")
        nmean = C.sb([128, 1], F32, "nmean")
        vsum = C.sb([128, 1], F32, "vsum")
        rstd = C.sb([128, 1], F32, "rstd")
        py = (PF[0], PF[1])
        pg = (PF[2], PF[3])
        pp = (PF[4], PF[5])

        def loads(i):
            k = i % 2
            C.dma("sp", ygb[k][:], yg_s[i * 128:(i + 1) * 128, :], r=[ygB[i]], w=[ygb[k]])
            C.dma("sp", xr[k][:], xsrc[i * 128:(i + 1) * 128, :], r=([xsrcB[i]] if xsrcB else []), w=[xr[k]])
            C.dma("pool", pb[k][:], p_d[l][i * 128:(i + 1) * 128, :], w=[pb[k]])

        def phase1(i):
            k = i % 2
            transpose_to(ygb[k], 8, 128, lambda: ygT[k].t[:], ygT[k], eng="act")
            transpose_to(pb[k], 2, 128, lambda: pT[i % 3].t[:], pT[i % 3], eng="dve")
            for hf in range(2):
                for c in range(8):
                    C.op("pe", lambda e: e.matmul(py[hf].t[:], lhsT=ygT[k].t[:, c, :],
                                                  rhs=wbig.t[:, c, hf * 512:(hf + 1) * 512],
                                                  start=(c == 0), stop=(c == 7)),
                         r=[ygT[k], wbig], w=[py[hf]])
            for hf in range(2):
                C.op("dve", lambda e: e.scalar_tensor_tensor(
                    out=s[k].t[:, hf * 512:(hf + 1) * 512], in0=xr[k].t[:, hf * 512:(hf + 1) * 512], scalar=ALPHA,
                    in1=py[hf].t[:], op0=ALU.mult, op1=ALU.add), r=[xr[k], py[hf]], w=[s[k]])

        def phase2a(i):
            k = i % 2
            sk = s[k]
            h, hb = h2[k], hb2[k]
            C.op("act", lambda e: e.activation(out=junk[:], in_=sk[:], func=AF.Copy, accum_out=msum[:]),
                 r=[sk], w=[junk, msum])
            yield
            C.op("dve", lambda e: e.tensor_scalar(out=nmean[:], in0=msum[:], scalar1=-1.0 / DM, scalar2=None,
                                                  op0=ALU.mult), r=[msum], w=[nmean])
            C.op("act", lambda e: e.activation(out=sk[:], in_=sk[:], func=AF.Identity, bias=nmean[:, 0:1], scale=1.0),
                 r=[sk, nmean], w=[sk])
            yield
            C.op("act", lambda e: e.activation(out=junk2[:], in_=sk[:], func=AF.Square, accum_out=vsum[:]),
                 r=[sk], w=[junk2, vsum])
            yield
            rstd_from_sum(vsum, DM, rstd)
            yield
            C.op("dve", lambda e: e.scalar_tensor_tensor(out=h[:], in0=sk[:], scalar=rstd[:, 0:1], in1=lng[:],
                                                         op0=ALU.mult, op1=ALU.mult), r=[sk, rstd, lng], w=[h])
            yield
            C.op("dve", lambda e: e.tensor_tensor(out=hb[:], in0=h[:], in1=lnb[:], op=ALU.add), r=[h, lnb], w=[hb])
            C.op("dve", lambda e: e.tensor_tensor(out=h[:], in0=h[:], in1=lnb[:], op=ALU.add), r=[h, lnb], w=[h])
            yield

        def phase2b(i):
            k = i % 2
            h, hb, hT = h2[k], hb2[k], hT2[k]
            pTk = pT[i % 3]
            transpose_to(hb, 8, 128, lambda: hT.t[:], hT, eng="act")
            yield
            for hf in range(2):
                for c in range(8):
                    C.op("pe", lambda e: e.matmul(pg[hf].t[:], lhsT=hT.t[:, c, :],
                                                  rhs=wbig.t[:, c, 1024 + hf * 512:1024 + (hf + 1) * 512],
                                                  start=(c == 0), stop=(c == 7)),
                         r=[hT, wbig], w=[pg[hf]])
                for c in range(2):
                    C.op("pe", lambda e: e.matmul(pp[hf].t[:], lhsT=pTk.t[:, c, :],
                                                  rhs=wbig.t[:, c, 2048 + hf * 512:2048 + (hf + 1) * 512],
                                                  start=(c == 0), stop=(c == 1)),
                         r=[pTk, wbig], w=[pp[hf]])
                yield
            for hf in range(2):
                sl = slice(hf * 512, (hf + 1) * 512)
                C.op("act", lambda e: e.activation(out=sg[k].t[:, sl], in_=pg[hf].t[:], func=AF.Sigmoid),
                     r=[pg[hf]], w=[sg[k]])
                yield
                C.op("dve", lambda e: e.tensor_tensor(out=sg[k].t[:, sl], in0=sg[k].t[:, sl], in1=pp[hf].t[:],
                                                      op=ALU.mult), r=[sg[k], pp[hf]], w=[sg[k]])
                yield
            C.op("dve", lambda e: e.tensor_tensor(out=sg[k].t[:], in0=sg[k].t[:], in1=h[:], op=ALU.add),
                 r=[sg[k], h], w=[sg[k]])
            C.dma("sp", dst[i * 128:(i + 1) * 128, :], sg[k].t[:], r=[sg[k]], w=([dstB[i]] if dstB else []))
            yield

        def interleave(*gens):
            gens = [g for g in gens if g is not None]
            while gens:
                for g in list(gens):
                    try:
                        next(g)
                    except StopIteration:
                        gens.remove(g)

        loads(0)
        loads(1)
        phase1(0)
        for i in range(NT + 1):
            if i + 2 < NT:
                loads(i + 2)
            if thunks:
                thunks.pop(0)()
            interleave(phase2a(i) if i < NT else None, phase2b(i - 1) if i >= 1 else None)
            if i + 1 < NT:
                phase1(i + 1)

    def layer0(xsrc, xsrcB, dst, dstB, hooks=None):
      hooks = hooks or {}
      with C.scope():
        q_s = dscr("q_s", [SEQ, 512], BF16)
        zp_s = dscr("zp_s", [SEQ + 2, 512])
        bg_s = dscr("bg_s", [SEQ, 512])
        sz_s = dscr("sz_s", [SEQ, DM])
        qB = [Buf() for _ in range(NT)]
        zpB = [Buf() for _ in range(NT)]
        zpadB = Buf()
        bgB = [Buf() for _ in range(NT)]
        szB = [Buf() for _ in range(NT)]

        kT = C.sb([128, SEQ], BF16, "kT")
        vall = C.sb([128, NT, 2, 65], BF16, "vall")
        kTB = [Buf() for _ in range(NT)]
        vB = [Buf() for _ in range(NT)]
        C.op("pool", lambda e: e.memset(vall[:], 1.0), w=[vall] + vB)
        zero = C.sb([128, 512], F32, "zero")
        C.op("pool", lambda e: e.memset(zero[:], 0.0), w=[zero])
        C.dma("sp", zp_s[0:1, :], zero.t[0:1, :], r=[zero], w=[zpadB])
        C.dma("sp", zp_s[SEQ + 1:SEQ + 2, :], zero.t[0:1, :], r=[zero], w=[zpadB])
        cw = [C.sb([128, 512], F32, "cw") for _ in range(3)]
        for k3 in range(3):
            C.dma("sp", cw[k3][:], convw_d[k3].partition_broadcast(128), w=[cw[k3]])
        anb = C.sb([128, 512], F32, "anb")
        bnb = C.sb([128, 512], F32, "bnb")
        C.dma("sp", anb[:], anorm_d.partition_broadcast(128), w=[anb])
        C.dma("sp", bnb[:], bnorm_d.partition_broadcast(128), w=[bnb])
        esink = C.sb([128, 8], F32, "esink")
        C.dma("sp", esink[:], sink_d.partition_broadcast(128), w=[esink])
        C.op("act", lambda e: e.activation(out=esink[:], in_=esink[:], func=AF.Exp), r=[esink], w=[esink])
        nsI = C.sb([128, 8, 128], BF16, "nsI")
        for hh in range(8):
            C.op("dve", lambda e, hh=hh: e.tensor_scalar(out=nsI.t[:, hh, :], in0=ident_f[:],
                                                         scalar1=-8.0 * 2.0 ** (-(hh + 1)), scalar2=None,
                                                         op0=ALU.mult), r=[ident_f], w=[nsI])

        with C.scope():
            wbig = C.sb([128, 8, 3328], BF16, "w_in0")
            wcB = [Buf(f"w_in0_c{c}") for c in range(8)]
            for c in range(8):
                C.dma("pool", wbig.t[:, c, :], w_in0_d[c * 128:(c + 1) * 128, :], w=[wcB[c]])
            xb = [C.sb([128, DM], BF16, "xb") for _ in range(2)]
            xT = [C.sb([128, 8, 128], BF16, "xT") for _ in range(2)]
            qb = [C.sb([128, 512], BF16, "qb") for _ in range(2)]
            kb_ = C.sb([128, 128], BF16, "kb")
            cgs = C.sb([128, 512], F32, "cgs")
            zpt = [C.sb([128, 512], F32, "zpt") for _ in range(2)]
            bgt = [C.sb([128, 512], F32, "bgt") for _ in range(2)]
            szt = [C.sb([128, DM], F32, "szt") for _ in range(2)]
            groups = [(0, 512), (512, 256), (768, 512), (1280, 512), (1792, 512), (2304, 512), (2816, 512)]

            def ld1(i):
                C.dma("pool", xb[i % 2][:], xsrc[i * 128:(i + 1) * 128, :], r=([xsrcB[i]] if xsrcB else []), w=[xb[i % 2]])

            ld1(0)
            for i in range(NT):
                k = i % 2
                if i + 1 < NT:
                    ld1(i + 1)
                transpose_to(xb[k], 8, 128, lambda k=k: xT[k].t[:], xT[k], eng="act")
                banks = []
                for gi, (n0, wd) in enumerate(groups):
                    bank = PF[gi % 6]
                    banks.append(bank)
                    for c in range(8):
                        C.op("pe", lambda e, c=c, n0=n0, wd=wd, bank=bank, k=k: e.matmul(
                            bank.t[:, 0:wd], lhsT=xT[k].t[:, c, :], rhs=wbig.t[:, c, n0:n0 + wd],
                            start=(c == 0), stop=(c == 7)), r=[xT[k], wcB[c]], w=[bank])
                    if gi == 0:
                        C.op("act", lambda e, bank=bank, k=k: e.copy(out=qb[k][:], in_=bank.t[:]), r=[bank], w=[qb[k]])
                        C.dma("sp", q_s[i * 128:(i + 1) * 128, :], qb[k][:], r=[qb[k]], w=[qB[i]])
                    elif gi == 1:
                        C.op("dve", lambda e, bank=bank: e.tensor_copy(out=kb_[:], in_=bank.t[:, 0:128]), r=[bank], w=[kb_])
                        C.op("dve", lambda e, bank=bank, i=i: e.tensor_copy(
                            out=vall.t[:, i, :, 0:64], in_=bank.t[:, 128:256].rearrange("p (h d) -> p h d", h=2)),
                            r=[bank], w=[vB[i]])
                        transpose_to(kb_, 1, 128, lambda i=i: kT.t[:, i * 128:(i + 1) * 128].unsqueeze(1), kTB[i], eng="dve")
                    elif gi == 2:
                        C.op("act", lambda e, bank=bank, k=k: e.copy(out=bgt[k][:], in_=bank.t[:]), r=[bank], w=[bgt[k]])
                        C.dma("sp", bg_s[i * 128:(i + 1) * 128, :], bgt[k][:], r=[bgt[k]], w=[bgB[i]])
                    elif gi == 3:
                        C.op("act", lambda e, bank=bank: e.copy(out=cgs[:], in_=bank.t[:]), r=[bank], w=[cgs])
                    elif gi == 4:
                        C.op("dve", lambda e, bank=bank, k=k: e.tensor_tensor(out=zpt[k][:], in0=bank.t[:], in1=cgs[:],
                                                                             op=ALU.mult), r=[bank, cgs], w=[zpt[k]])
                        C.dma("sp", zp_s[1 + i * 128:1 + (i + 1) * 128, :], zpt[k][:], r=[zpt[k]], w=[zpB[i]])
                    else:
                        hf = gi - 5
                        C.op("act", lambda e, bank=bank, k=k, hf=hf: e.activation(
                            out=szt[k].t[:, hf * 512:(hf + 1) * 512], in_=bank.t[:], func=AF.Silu), r=[bank], w=[szt[k]])
                        if hf == 1:
                            C.dma("sp", sz_s[i * 128:(i + 1) * 128, :], szt[k][:], r=[szt[k]], w=[szB[i]])

        if "pre_s2" in hooks:
            hooks["pre_s2"]()
        with C.scope():
            q2 = [C.sb([128, 512], BF16, "q2") for _ in range(2)]
            qT = [C.sb([128, 4, 128], BF16, "qT") for _ in range(2)]
            zw = [[C.sb([128, 512], F32, "zw") for _ in range(3)] for _ in range(2)]
            bg2 = [C.sb([128, 512], F32, "bg2") for _ in range(2)]
            sz2 = [C.sb([128, DM], F32, "sz2") for _ in range(2)]
            pq_i = [C.sb([128, 128], I32, "pqi") for _ in range(2)]
            pq = [C.sb([128, 128], F32, "pq") for _ in range(2)]
            dmf = [C.sb([128, 3, 128], F32, "dmf") for _ in range(2)]
            Dm = [C.sb([128, 3, 128], BF16, "Dm") for _ in range(2)]
            pTt = [C.sb([128, 3, 128], BF16, "pTt") for _ in range(4)]
            ya = C.sb([128, 8, 64], F32, "ya")
            den = C.sb([128, 8], F32, "den")
            ssa = C.sb([128, 1], F32, "ssa")
            ra = C.sb([128, 1], F32, "ra")
            ssb = C.sb([128, 1], F32, "ssb")
            rb = C.sb([128, 1], F32, "rb")
            yg = [C.sb([128, DM], F32, "yg") for _ in range(2)]
            ygo = [C.sb([128, DM], BF16, "ygo") for _ in range(2)]
            po = (PF[4], PF[5])

            def ld2(j):
                k = j % 2
                C.dma("sp", q2[k][:], q_s[j * 128:(j + 1) * 128, :], r=[qB[j]], w=[q2[k]])
                for d3 in range(3):
                    rd = [zpB[j]]
                    if d3 == 0:
                        rd.append(zpB[j - 1] if j > 0 else zpadB)
                    if d3 == 2:
                        rd.append(zpB[j + 1] if j + 1 < NT else zpadB)
                    C.dma("sp", zw[k][d3][:], zp_s[j * 128 + d3:j * 128 + d3 + 128, :], r=rd, w=[zw[k][d3]])
                C.dma("sp", bg2[k][:], bg_s[j * 128:(j + 1) * 128, :], r=[bgB[j]], w=[bg2[k]])
                C.dma("sp", sz2[k][:], sz_s[j * 128:(j + 1) * 128, :], r=[szB[j]], w=[sz2[k]])
                C.dma("sp", pq_i[k][:], pos_d[j * 128:(j + 1) * 128].partition_broadcast(128), w=[pq_i[k]])

            def kbs_of(j):
                return [kb for kb in (j - 1, j, j + 1) if 0 <= kb < NT]

            def prep_attn(j):
                k = j % 2
                transpose_to(q2[k], 4, 128, lambda: qT[k].t[:], qT[k], eng="dve")
                C.op("dve", lambda e: e.tensor_copy(out=pq[k][:], in_=pq_i[k][:]), r=[pq_i[k]], w=[pq[k]])
                kbs = kbs_of(j)
                s0 = 0 if j > 0 else 1
                for kb in kbs:
                    s = kb - j + 1
                    C.op("act", lambda e: e.activation(out=dmf[k].t[:, s, :], in_=pq[k][:], func=AF.Abs,
                                                       bias=posk.t[:, kb:kb + 1], scale=-1.0),
                         r=[pq[k], posk], w=[dmf[k]])
                    if s == 0:
                        C.op("pool", lambda e: e.tensor_tensor(out=dmf[k].t[:, 0, :], in0=dmf[k].t[:, 0, :],
                                                               in1=mlo[:], op=ALU.add), r=[dmf[k], mlo], w=[dmf[k]])
                    elif s == 2:
                        C.op("pool", lambda e: e.tensor_tensor(out=dmf[k].t[:, 2, :], in0=dmf[k].t[:, 2, :],
                                                               in1=mhi[:], op=ALU.add), r=[dmf[k], mhi], w=[dmf[k]])
                C.op("pool", lambda e: e.tensor_copy(out=Dm[k].t[:, s0:s0 + len(kbs), :],
                                                     in_=dmf[k].t[:, s0:s0 + len(kbs), :]), r=[dmf[k]], w=[Dm[k]])

            def conv_branch(j):
                k = j % 2
                z_m1, z_0, z_p1 = zw[k]
                C.op("pool", lambda e: e.tensor_tensor(out=z_m1[:], in0=z_m1[:], in1=cw[0][:], op=ALU.mult),
                     r=[z_m1, cw[0]], w=[z_m1])
                C.op("pool", lambda e: e.tensor_tensor(out=z_0[:], in0=z_0[:], in1=cw[1][:], op=ALU.mult),
                     r=[z_0, cw[1]], w=[z_0])
                C.op("pool", lambda e: e.tensor_tensor(out=z_p1[:], in0=z_p1[:], in1=cw[2][:], op=ALU.mult),
                     r=[z_p1, cw[2]], w=[z_p1])
                C.op("dve", lambda e: e.tensor_tensor(out=z_0[:], in0=z_0[:], in1=z_m1[:], op=ALU.add),
                     r=[z_0, z_m1], w=[z_0])
                C.op("dve", lambda e: e.tensor_tensor(out=z_0[:], in0=z_0[:], in1=z_p1[:], op=ALU.add),
                     r=[z_0, z_p1], w=[z_0])
                C.op("dve", lambda e: e.tensor_tensor(out=z_0[:], in0=z_0[:], in1=bg2[k][:], op=ALU.mult),
                     r=[z_0, bg2[k]], w=[z_0])
                C.op("act", lambda e: e.activation(out=junk.t[:, 512:1024], in_=z_0[:], func=AF.Square,
                                                   accum_out=ssb[:]), r=[z_0], w=[junk, ssb])
                rstd_from_sum(ssb, 512, rb)
                C.op("dve", lambda e: e.scalar_tensor_tensor(out=yg[k].t[:, 512:1024], in0=z_0[:], scalar=rb[:, 0:1],
                                                             in1=bnb[:], op0=ALU.mult, op1=ALU.mult),
                     r=[z_0, rb, bnb], w=[yg[k]])

            hc = {"n": 0}

            def heads(j, mid=None):
                k = j % 2
                kbs = kbs_of(j)
                ns = len(kbs)
                s0 = 0 if j > 0 else 1
                slots = {}

                def front(hh):
                    hk, gq = hh // 4, hh % 4
                    n = hc["n"]
                    hc["n"] += 1
                    stb = PF[n % 4]
                    ptt = pTt[n % 4]
                    slots[hh] = ptt
                    stv = stb.t[:, 0:384].rearrange("p (s q) -> p s q", s=3)
                    for kb in kbs:
                        s = kb - j + 1
                        C.op("pe", lambda e: e.matmul(
                            stv[:, s, :], lhsT=kT.t[hk * 64:(hk + 1) * 64, kb * 128:(kb + 1) * 128],
                            rhs=qT[k].t[hk * 64:(hk + 1) * 64, gq, :], start=True, stop=False),
                            r=[kTB[kb], qT[k]], w=[stb])
                        C.op("pe", lambda e: e.matmul(
                            stv[:, s, :], lhsT=nsI.t[:, hh, :], rhs=Dm[k].t[:, s, :], start=False, stop=True),
                            r=[nsI, Dm[k]], w=[stb])
                    C.op("act", lambda e: e.activation(
                        out=ptt.t[:, s0:s0 + ns, :], in_=stv[:, s0:s0 + ns, :], func=AF.Exp, scale=0.125),
                        r=[stb], w=[ptt])

                def back(hh):
                    hk = hh // 4
                    ptt = slots[hh]
                    pob = po[hh // 4]
                    pov = pob.t[:, 0:260].rearrange("p (h d) -> p h d", h=4)
                    for n_, kb in enumerate(kbs):
                        s = kb - j + 1
                        C.op("pe", lambda e: e.matmul(
                            pov[:, hh % 4, :], lhsT=ptt.t[:, s, :], rhs=vall.t[:, kb, hk, :],
                            start=(n_ == 0), stop=(n_ == ns - 1)), r=[ptt, vB[kb]], w=[pob])

                LOOK = 3
                for hh in range(8 + LOOK):
                    if hh < 8:
                        front(hh)
                    if hh == 5 and mid is not None:
                        mid()
                    if hh >= LOOK:
                        back(hh - LOOK)

            def tail(j):
                k = j % 2
                for g2 in range(2):
                    pov = po[g2].t[:, 0:260].rearrange("p (h d) -> p h d", h=4)
                    C.op("dve", lambda e: e.tensor_tensor(
                        out=den.t[:, g2 * 4:(g2 + 1) * 4], in0=pov[:, :, 64], in1=esink.t[:, g2 * 4:(g2 + 1) * 4],
                        op=ALU.add), r=[po[g2], esink], w=[den])
                C.op("dve", lambda e: e.reciprocal(out=den[:], in_=den[:]), r=[den], w=[den])
                for g2 in range(2):
                    pov = po[g2].t[:, 0:260].rearrange("p (h d) -> p h d", h=4)
                    C.op("dve", lambda e: e.tensor_tensor(
                        out=ya.t[:, g2 * 4:(g2 + 1) * 4, :], in0=pov[:, :, 0:64],
                        in1=den.t[:, g2 * 4:(g2 + 1) * 4].unsqueeze(2).to_broadcast([128, 4, 64]), op=ALU.mult),
                        r=[po[g2], den], w=[ya])
                yaf = ya.t[:].rearrange("p h d -> p (h d)")
                C.op("act", lambda e: e.activation(out=junk.t[:, 0:512], in_=yaf, func=AF.Square, accum_out=ssa[:]),
                     r=[ya], w=[junk, ssa])
                rstd_from_sum(ssa, 512, ra)
                C.op("dve", lambda e: e.scalar_tensor_tensor(out=yg[k].t[:, 0:512], in0=yaf, scalar=ra[:, 0:1],
                                                             in1=anb[:], op0=ALU.mult, op1=ALU.mult),
                     r=[ya, ra, anb], w=[yg[k]])
                C.op("dve", lambda e: e.tensor_tensor(out=ygo[k][:], in0=yg[k][:], in1=sz2[k][:], op=ALU.mult),
                     r=[yg[k], sz2[k]], w=[ygo[k]])
                C.dma("sp", yg_s[j * 128:(j + 1) * 128, :], ygo[k][:], r=[ygo[k]], w=[ygB[j]])

            ld2(0)
            prep_attn(0)
            conv_branch(0)
            for j in range(NT):
                if j + 1 < NT:
                    ld2(j + 1)
                if hooks.get("s2_thunks"):
                    hooks["s2_thunks"].pop(0)()
                heads(j, mid=(lambda: prep_attn(j + 1)) if j + 1 < NT else None)
                tail(j)
                if j + 1 < NT:
                    conv_branch(j + 1)

      if "pre_p" in hooks:
          hooks["pre_p"]()
      post_stage(0, xsrc, xsrcB, dst, dstB, wts=hooks.get("wpost"), thunks=hooks.get("p_thunks"))

    def alloc_l1_weights(right):
        al = C.sbr if right else C.sb
        return (al([128, 8, 2464], BF16, "w_in1"), al([128, 2, 768], BF16, "w_uq"), al([128, 1024], BF16, "w_ukv"),
                al([128, 4, 128], BF16, "w_sT"))

    def load_l1_weights(ws, thunks=None):
        w1, wuq, wukv, wsT = ws
        load_weight(lambda c: w1.t[:, c, :], w_in1_d, 8, w1, thunks)
        load_weight(lambda c: wuq.t[:, c, :], w_uq_d, 2, wuq, thunks)
        g1 = lambda: C.dma("pool", wukv[:], w_ukv_d, w=[wukv])
        g2 = lambda: C.dma("pool", wsT[:], w_sT_d, w=[wsT])
        for g in (g1, g2):
            if thunks is None:
                g()
            else:
                thunks.append(g)

    def layer1(xsrc, xsrcB, dst, dstB, hooks=None):
      hooks = hooks or {}
      import math
      TWO_PI = 2.0 * math.pi
      C1 = 6.28125
      C2 = TWO_PI - C1
      PI_S = 3.1415925
      SCALE = 96.0 ** -0.5
      with C.scope():
        qT_s = dscr("qT_s", [96, 8, SEQ], BF16)
        szc_s = dscr("szc_s", [SEQ, 512])
        qTB = [Buf() for _ in range(NT)]
        szcB = [Buf() for _ in range(NT)]
        KT = C.sb([128, 8, SEQ], BF16, "KT")
        vall = C.sb([128, NT, 8, 65], BF16, "vall1")
        KTB = [Buf() for _ in range(NT)]
        vB = [Buf() for _ in range(NT)]
        C.op("pool", lambda e: e.memset(vall[:], 1.0), w=[vall] + vB)
        KTpad = Buf("KTpad")

        with C.scope():
            if "w_s1" in hooks:
                w1, wuq, wukv, wsT = hooks["w_s1"]
            else:
                w1, wuq, wukv, wsT = ws_ = alloc_l1_weights(False)
                load_l1_weights(ws_)
            bsT = C.sb([128, 4], F32, "bsT")
            C.dma("sp", bsT[:], b_sT_d, w=[bsT])

            def bc(src, n, name):
                t = C.sb([128, n], F32, name)
                C.dma("sp", t[:], src.partition_broadcast(128), w=[t])
                return t

            qnb = bc(qn_d, 256, "qnb")
            kvnb = bc(kvn_d, 128, "kvnb")
            vlgb = bc(vlg_d, 512, "vlgb")
            vlbb = bc(vlb_d, 512, "vlbb")
            dnb = bc(dn_d, 512, "dnb")
            invb = bc(invf_d, 16, "invb")
            negpi = C.sb([128, 1], F32, "negpi")
            sc = C.sb([128, NT, 32], F32, "sc")
            with C.scope():
                ang = C.sb([128, NT, 32], F32, "ang")
                kf = C.sb([128, NT, 32], F32, "kf")
                ki = C.sb([128, NT, 32], I32, "ki")
                C.op("dve", lambda e: e.tensor_tensor(out=ang.t[:, :, 0:16],
                                                      in0=posk.t[:].unsqueeze(2).to_broadcast([128, NT, 16]),
                                                      in1=invb.t[:].unsqueeze(1).to_broadcast([128, NT, 16]),
                                                      op=ALU.mult), r=[posk, invb], w=[ang])
                C.op("dve", lambda e: e.tensor_scalar(out=ang.t[:, :, 16:32], in0=ang.t[:, :, 0:16], scalar1=0.5 * math.pi,
                                                      scalar2=None, op0=ALU.add), r=[ang], w=[ang])
                C.op("dve", lambda e: e.tensor_scalar(out=kf[:], in0=ang[:], scalar1=1.0 / TWO_PI, scalar2=None,
                                                      op0=ALU.mult), r=[ang], w=[kf])
                C.op("dve", lambda e: e.tensor_copy(out=ki[:], in_=kf[:]), r=[kf], w=[ki])
                C.op("dve", lambda e: e.tensor_copy(out=kf[:], in_=ki[:]), r=[ki], w=[kf])
                C.op("dve", lambda e: e.scalar_tensor_tensor(out=ang[:], in0=kf[:], scalar=-C1, in1=ang[:],
                                                             op0=ALU.mult, op1=ALU.add), r=[kf, ang], w=[ang])
                C.op("dve", lambda e: e.scalar_tensor_tensor(out=ang[:], in0=kf[:], scalar=-C2, in1=ang[:],
                                                             op0=ALU.mult, op1=ALU.add), r=[kf, ang], w=[ang])
                C.op("dve", lambda e: e.tensor_scalar(out=ang[:], in0=ang[:], scalar1=-PI_S, scalar2=PI_S,
                                                      op0=ALU.max, op1=ALU.min), r=[ang], w=[ang])
                C.op("act", lambda e: e.activation(out=sc[:], in_=ang[:], func=AF.Sin), r=[ang], w=[sc])

            xb = [C.sb([128, DM], BF16, "xb") for _ in range(2)]
            xT = [C.sb([128, 8, 128], BF16, "xT") for _ in range(2)]
            cqn = [C.sb([128, 256], BF16, "cqn") for _ in range(2)]
            cqnT = C.sb([128, 2, 128], BF16, "cqnT")
            cn = [C.sb([128, 128], BF16, "cn") for _ in range(2)]
            cnT = C.sb([128, 128], BF16, "cnT")
            qfull = C.sb([128, 8 * 96], BF16, "qfull")
            kfull = C.sb([128, 8 * 96], BF16, "kfull")
            qTt = [C.sb([128, 8, 128], BF16, "qTt") for _ in range(2)]
            kro = [C.sb([128, 32], F32, "kro") for _ in range(2)]
            tq = [C.sb([128, 4, 16], F32, "tq") for _ in range(4)]
            tk = [C.sb([128, 16], F32, "tk") for _ in range(4)]
            gu = [C.sb([128, 512], F32, "gu") for _ in range(2)]
            gtmp2 = [C.sb([128, 512], F32, "gtmp") for _ in range(2)]
            xstg = [C.sb([128, 512], F32, "xstg") for _ in range(3)]
            gv = [C.sb([128, 512], F32, "gv") for _ in range(2)]
            vn = C.sb([128, 512], BF16, "vn")
            szd = [C.sb([128, 512], F32, "szd") for _ in range(2)]
            szc = [C.sb([128, 512], F32, "szc") for _ in range(2)]
            yd = C.sb([128, 512], F32, "yd")
            ygd = [C.sb([128, 512], BF16, "ygd") for _ in range(2)]
            st1 = {n: C.sb([128, 1], F32, n) for n in ("ssq", "rq", "ssk", "rk", "ms", "nm", "vs", "rv", "ssd", "rd")}
            groups = [(0, 416), (416, 512), (928, 512), (1440, 512), (1952, 512)]
            gbank = [PF[0], PF[1], PF[2], PF[0], PF[1]]

            def ld1(i):
                C.dma("pool", xb[i % 2][:], xsrc[i * 128:(i + 1) * 128, :], r=([xsrcB[i]] if xsrcB else []),
                      w=[xb[i % 2]])

            def silu_from(bank_, out_):
                zs = xstg[2]
                C.op("act", lambda e: e.activation(out=out_[:], in_=bank_.t[:], func=AF.Sigmoid), r=[bank_], w=[out_])
                C.op("act", lambda e: e.copy(out=zs[:], in_=bank_.t[:]), r=[bank_], w=[zs])
                C.op("pool", lambda e: e.tensor_tensor(out=out_[:], in0=out_[:], in1=zs[:], op=ALU.mult),
                     r=[out_, zs], w=[out_])

            def gelu_from(bank_, g_, which):
                gtmp = gtmp2[which]
                xs = xstg[which]
                C.op("act", lambda e: e.activation(out=gtmp[:], in_=bank_.t[:], func=AF.Square), r=[bank_], w=[gtmp])
                C.op("act", lambda e: e.copy(out=xs[:], in_=bank_.t[:]), r=[bank_], w=[xs])
                C.op("dve", lambda e: e.tensor_scalar(out=gtmp[:], in0=gtmp[:], scalar1=0.044715, scalar2=1.0,
                                                      op0=ALU.mult, op1=ALU.add), r=[gtmp], w=[gtmp])
                C.op("pool", lambda e: e.tensor_tensor(out=gtmp[:], in0=gtmp[:], in1=xs[:], op=ALU.mult),
                     r=[gtmp, xs], w=[gtmp])
                C.op("act", lambda e: e.activation(out=gtmp[:], in_=gtmp[:], func=AF.Sigmoid,
                                                   scale=1.5957691216057308), r=[gtmp], w=[gtmp])
                C.op("pool", lambda e: e.tensor_tensor(out=g_[:], in0=gtmp[:], in1=xs[:], op=ALU.mult),
                     r=[gtmp, xs], w=[g_])

            def mm_group(i, gi):
                k = i % 2
                n0, wd = groups[gi]
                bank = gbank[gi]
                for c in range(8):
                    C.op("pe", lambda e: e.matmul(bank.t[:, 0:wd], lhsT=xT[k].t[:, c, :], rhs=w1.t[:, c, n0:n0 + wd],
                                                  start=(c == 0), stop=(c == 7)), r=[xT[k], w1], w=[bank])

            def A1(i):
                k = i % 2
                if i < 8:
                    C.op("pool", lambda e: e.memset(KT.t[96:128, i, :], 0.0), w=[KTpad])
                transpose_to(xb[k], 8, 128, lambda: xT[k].t[:], xT[k], eng="act")
                mm_group(i, 0)
                mm_group(i, 1)
                mm_group(i, 2)

            def A2(i):
                k = i % 2
                g0 = gbank[0]
                C.op("act", lambda e: e.activation(out=junk.t[:, 0:256], in_=g0.t[:, 0:256], func=AF.Square,
                                                   accum_out=st1["ssq"][:]), r=[g0], w=[junk, st1["ssq"]])
                C.op("act", lambda e: e.activation(out=junk.t[:, 256:384], in_=g0.t[:, 256:384], func=AF.Square,
                                                   accum_out=st1["ssk"][:]), r=[g0], w=[junk, st1["ssk"]])
                sinb = sc.t[:, i, 0:16]
                cosb = sc.t[:, i, 16:32]
                x1 = g0.t[:, 384:400]
                x2 = g0.t[:, 400:416]
                C.op("dve", lambda e: e.tensor_tensor(out=tk[0][:], in0=x1, in1=cosb, op=ALU.mult), r=[g0, sc], w=[tk[0]])
                C.op("dve", lambda e: e.tensor_tensor(out=tk[1][:], in0=x2, in1=sinb, op=ALU.mult), r=[g0, sc], w=[tk[1]])
                C.op("dve", lambda e: e.tensor_tensor(out=tk[2][:], in0=x1, in1=sinb, op=ALU.mult), r=[g0, sc], w=[tk[2]])
                C.op("dve", lambda e: e.tensor_tensor(out=tk[3][:], in0=x2, in1=cosb, op=ALU.mult), r=[g0, sc], w=[tk[3]])
                rstd_from_sum(st1["ssq"], 256, st1["rq"])
                rstd_from_sum(st1["ssk"], 128, st1["rk"])
                C.op("dve", lambda e: e.tensor_tensor(out=kro[k].t[:, 0:16], in0=tk[0][:], in1=tk[1][:],
                                                      op=ALU.subtract), r=[tk[0], tk[1]], w=[kro[k]])
                C.op("dve", lambda e: e.tensor_tensor(out=kro[k].t[:, 16:32], in0=tk[2][:], in1=tk[3][:], op=ALU.add),
                     r=[tk[2], tk[3]], w=[kro[k]])
                C.op("dve", lambda e: e.scalar_tensor_tensor(out=cqn[k][:], in0=g0.t[:, 0:256],
                                                             scalar=st1["rq"].t[:, 0:1], in1=qnb[:], op0=ALU.mult,
                                                             op1=ALU.mult), r=[g0, st1["rq"], qnb], w=[cqn[k]])
                C.op("dve", lambda e: e.scalar_tensor_tensor(out=cn[k][:], in0=g0.t[:, 256:384],
                                                             scalar=st1["rk"].t[:, 0:1], in1=kvnb[:], op0=ALU.mult,
                                                             op1=ALU.mult), r=[g0, st1["rk"], kvnb], w=[cn[k]])

            def A2b(i):
                k = i % 2
                mm_group(i, 3)
                gelu_from(gbank[1], gu[k], 0)
                mm_group(i, 4)
                gelu_from(gbank[2], gv[k], 1)
                silu_from(gbank[3], szc[k])
                C.dma("sp", szc_s[i * 128:(i + 1) * 128, :], szc[k][:], r=[szc[k]], w=[szcB[i]])
                silu_from(gbank[4], szd[k])

            qb_ = (PF[3], PF[4])
            kvb = (PF[5], PF[3])

            def B1(i):
                k = i % 2
                transpose_to(cqn[k], 2, 128, lambda: cqnT.t[:], cqnT, eng="dve")
                transpose_to(cn[k], 1, 128, lambda: cnT.t[:].unsqueeze(1), cnT, eng="dve")
                for b2 in range(2):
                    for c in range(2):
                        C.op("pe", lambda e: e.matmul(qb_[b2].t[:, 0:384], lhsT=cqnT.t[:, c, :],
                                                      rhs=wuq.t[:, c, b2 * 384:(b2 + 1) * 384], start=(c == 0),
                                                      stop=(c == 1)), r=[cqnT, wuq], w=[qb_[b2]])
                C.op("pe", lambda e: e.matmul(kvb[0].t[:], lhsT=cnT.t[:], rhs=wukv.t[:, 0:512],
                                              start=True, stop=True), r=[cnT, wukv], w=[kvb[0]])

            def B2(i):
                k = i % 2
                qf3 = qfull.t[:].rearrange("p (h d) -> p h d", h=8)
                kf3 = kfull.t[:].rearrange("p (h d) -> p h d", h=8)
                cos4 = sc.t[:, i, 16:32].unsqueeze(1).to_broadcast([128, 4, 16])
                sin4 = sc.t[:, i, 0:16].unsqueeze(1).to_broadcast([128, 4, 16])
                gvk = gv[k]
                C.op("act", lambda e: e.activation(out=junk.t[:, 0:512], in_=gvk[:], func=AF.Copy,
                                                   accum_out=st1["ms"][:]), r=[gvk], w=[junk, st1["ms"]])
                for b2 in range(2):
                    hs = slice(b2 * 4, b2 * 4 + 4)
                    qv = qb_[b2].t[:, 0:384].rearrange("p (h d) -> p h d", h=4)
                    C.op("act", lambda e: e.copy(out=qf3[:, hs, 0:64], in_=qv[:, :, 0:64]), r=[qb_[b2]], w=[qfull])
                    C.op("dve", lambda e: e.tensor_tensor(out=tq[0][:], in0=qv[:, :, 64:80], in1=cos4, op=ALU.mult),
                         r=[qb_[b2], sc], w=[tq[0]])
                    C.op("dve", lambda e: e.tensor_tensor(out=tq[1][:], in0=qv[:, :, 80:96], in1=sin4, op=ALU.mult),
                         r=[qb_[b2], sc], w=[tq[1]])
                    C.op("dve", lambda e: e.tensor_tensor(out=tq[2][:], in0=qv[:, :, 64:80], in1=sin4, op=ALU.mult),
                         r=[qb_[b2], sc], w=[tq[2]])
                    C.op("dve", lambda e: e.tensor_tensor(out=tq[3][:], in0=qv[:, :, 80:96], in1=cos4, op=ALU.mult),
                         r=[qb_[b2], sc], w=[tq[3]])
                    C.op("dve", lambda e: e.tensor_tensor(out=qf3[:, hs, 64:80], in0=tq[0][:], in1=tq[1][:],
                                                          op=ALU.subtract), r=[tq[0], tq[1]], w=[qfull])
                    C.op("dve", lambda e: e.tensor_tensor(out=qf3[:, hs, 80:96], in0=tq[2][:], in1=tq[3][:],
                                                          op=ALU.add), r=[tq[2], tq[3]], w=[qfull])
                    if b2 == 0:
                        C.op("pe", lambda e: e.matmul(kvb[1].t[:], lhsT=cnT.t[:], rhs=wukv.t[:, 512:1024],
                                                      start=True, stop=True), r=[cnT, wukv], w=[kvb[1]])
                        C.op("dve", lambda e: e.tensor_scalar(out=st1["nm"][:], in0=st1["ms"][:], scalar1=-1.0 / 512,
                                                              scalar2=None, op0=ALU.mult), r=[st1["ms"]], w=[st1["nm"]])
                        C.op("act", lambda e: e.activation(out=gvk[:], in_=gvk[:], func=AF.Identity,
                                                           bias=st1["nm"].t[:, 0:1], scale=1.0),
                             r=[gvk, st1["nm"]], w=[gvk])
                        C.op("act", lambda e: e.activation(out=junk.t[:, 512:1024], in_=gvk[:], func=AF.Square,
                                                           accum_out=st1["vs"][:]), r=[gvk], w=[junk, st1["vs"]])
                for b2 in range(2):
                    hs = slice(b2 * 4, b2 * 4 + 4)
                    kvv = kvb[b2].t[:].rearrange("p (h d) -> p h d", h=4)
                    C.op("act", lambda e: e.copy(out=kf3[:, hs, 0:64], in_=kvv[:, :, 0:64]), r=[kvb[b2]], w=[kfull])
                    C.op("act", lambda e: e.copy(out=vall.t[:, i, hs, 0:64], in_=kvv[:, :, 64:128]), r=[kvb[b2]],
                         w=[vB[i]])
                C.op("dve", lambda e: e.tensor_copy(out=kf3[:, :, 64:96],
                                                    in_=kro[k].t[:].unsqueeze(1).to_broadcast([128, 8, 32])),
                     r=[kro[k]], w=[kfull])
                rstd_from_sum(st1["vs"], 512, st1["rv"])
                C.op("dve", lambda e: e.scalar_tensor_tensor(out=gvk[:], in0=gvk[:], scalar=st1["rv"].t[:, 0:1],
                                                             in1=vlgb[:], op0=ALU.mult, op1=ALU.mult),
                     r=[gvk, st1["rv"], vlgb], w=[gvk])
                C.op("dve", lambda e: e.tensor_tensor(out=vn[:], in0=gvk[:], in1=vlbb[:], op=ALU.add),
                     r=[gvk, vlbb], w=[vn])

            def B3(i):
                k = i % 2
                guk = gu[k]
                transpose_to(qfull, 8, 96, lambda: qTt[k].t[0:96, :, :], qTt[k], eng="act")
                C.dma("sp", qT_s[:, :, i * 128:(i + 1) * 128], qTt[k].t[0:96, :, :], r=[qTt[k]], w=[qTB[i]])
                transpose_to(kfull, 8, 96, lambda: KT.t[0:96, :, i * 128:(i + 1) * 128], KTB[i], eng="dve")
                mixb = PF[4]
                for g4 in range(4):
                    C.op("pe", lambda e: e.matmul(mixb.t[:, g4 * 128:(g4 + 1) * 128], lhsT=wsT.t[:, g4, :],
                                                  rhs=vn.t[:, g4 * 128:(g4 + 1) * 128], start=True, stop=True),
                         r=[wsT, vn], w=[mixb])
                for g4 in range(4):
                    sl = slice(g4 * 128, (g4 + 1) * 128)
                    C.op("dve", lambda e: e.scalar_tensor_tensor(out=yd.t[:, sl], in0=mixb.t[:, sl],
                                                                 scalar=bsT.t[:, g4:g4 + 1], in1=guk.t[:, sl],
                                                                 op0=ALU.add, op1=ALU.mult),
                         r=[mixb, bsT, guk], w=[yd])
                C.op("act", lambda e: e.activation(out=junk.t[:, 0:512], in_=yd[:], func=AF.Square,
                                                   accum_out=st1["ssd"][:]), r=[yd], w=[junk, st1["ssd"]])
                rstd_from_sum(st1["ssd"], 512, st1["rd"])
                C.op("dve", lambda e: e.scalar_tensor_tensor(out=yd[:], in0=yd[:], scalar=st1["rd"].t[:, 0:1],
                                                             in1=dnb[:], op0=ALU.mult, op1=ALU.mult),
                     r=[yd, st1["rd"], dnb], w=[yd])
                C.op("dve", lambda e: e.tensor_tensor(out=ygd[k][:], in0=yd[:], in1=szd[k][:], op=ALU.mult),
                     r=[yd, szd[k]], w=[ygd[k]])
                C.dma("sp", yg_s[i * 128:(i + 1) * 128, 512:1024], ygd[k][:], r=[ygd[k]], w=[ygB[i]])

            ld1(0)
            ld1(1)
            A1(0)
            A2(0)
            A2b(0)
            for i in range(NT):
                if i + 2 < NT:
                    ld1(i + 2)
                B1(i)
                if i + 1 < NT:
                    A1(i + 1)
                if i + 1 < NT:
                    A2(i + 1)
                B2(i)
                if i + 1 < NT:
                    A2b(i + 1)
                B3(i)

        if "pre_s2" in hooks:
            hooks["pre_s2"]()
        with C.scope():
            cnb = C.sb([128, 512], F32, "cnb")
            C.dma("sp", cnb[:], cn_d.partition_broadcast(128), w=[cnb])
            qTg = [C.sb([128, 8, 512], BF16, "qTg") for _ in range(2)]
            ptt = [C.sb([128, 512], BF16, "ptt") for _ in range(4)]
            ycg = C.sb([128, 4, 512], F32, "ycg")
            szc2 = [C.sb([128, 512], F32, "szc2") for _ in range(2)]
            ygc = [C.sb([128, 512], BF16, "ygc") for _ in range(2)]
            rden = C.sb([128, 4], F32, "rden")
            ssc = C.sb([128, 1], F32, "ssc")
            rc = C.sb([128, 1], F32, "rc")
            NG = SEQ // 512

            def ldq(g8):
                C.dma("sp", qTg[g8 % 2].t[0:96, :, :], qT_s[:, :, g8 * 512:(g8 + 1) * 512],
                      r=[qTB[g8 * 4 + t4] for t4 in range(4)], w=[qTg[g8 % 2]])

            for qg_ in qTg:
                C.op("pool", lambda e: e.memset(qg_[:], 0.0), w=[qg_])
            ldq(0)
            for g_ in hooks.get("s2_thunks", []):
                g_()
            iters = [(g8, hh, kt) for g8 in range(NG) for hh in range(8) for kt in range(NT)]
            LOOK = 2

            def front(n):
                g8, hh, kt = iters[n]
                if hh == 0 and kt == 0 and g8 + 1 < NG:
                    ldq(g8 + 1)
                qg = qTg[g8 % 2]
                stb = PF[n % 4]
                pt_ = ptt[n % 4]
                C.op("pe", lambda e: e.matmul(stb.t[:], lhsT=KT.t[:, hh, kt * 128:(kt + 1) * 128],
                                              rhs=qg.t[:, hh, :], start=True, stop=True),
                     r=[KTB[kt], KTpad, qg], w=[stb])
                C.op("act", lambda e: e.activation(out=pt_[:], in_=stb.t[:], func=AF.Exp, scale=SCALE),
                     r=[stb], w=[pt_])

            def back(n):
                g8, hh, kt = iters[n]
                pt_ = ptt[n % 4]
                pob = PF[4 + hh % 2]
                pov = pob.t[:, 0:260].rearrange("p (q d) -> p q d", q=4)
                for qi in range(4):
                    C.op("pe", lambda e: e.matmul(pov[:, qi, :], lhsT=pt_.t[:, qi * 128:(qi + 1) * 128],
                                                  rhs=vall.t[:, kt, hh, :], start=(kt == 0 and qi == 0),
                                                  stop=(kt == NT - 1), skip_group_check=True),
                         r=[pt_, vB[kt]], w=[pob])
                if kt != NT - 1:
                    return
                C.op("dve", lambda e: e.reciprocal(out=rden[:], in_=pov[:, :, 64]), r=[pob], w=[rden])
                C.op("dve", lambda e: e.tensor_tensor(out=ycg.t[:, :, hh * 64:(hh + 1) * 64], in0=pov[:, :, 0:64],
                                                      in1=rden.t[:].unsqueeze(2).to_broadcast([128, 4, 64]),
                                                      op=ALU.mult), r=[pob, rden], w=[ycg])
                if hh != 7:
                    return
                for qi in range(4):
                    ti = g8 * 4 + qi
                    k = ti % 2
                    C.dma("sp", szc2[k][:], szc_s[ti * 128:(ti + 1) * 128, :], r=[szcB[ti]], w=[szc2[k]])
                    C.op("act", lambda e: e.activation(out=junk.t[:, 0:512], in_=ycg.t[:, qi, :], func=AF.Square,
                                                       accum_out=ssc[:]), r=[ycg], w=[junk, ssc])
                    rstd_from_sum(ssc, 512, rc)
                    C.op("dve", lambda e: e.scalar_tensor_tensor(out=ycg.t[:, qi, :], in0=ycg.t[:, qi, :],
                                                                 scalar=rc.t[:, 0:1], in1=cnb[:], op0=ALU.mult,
                                                                 op1=ALU.mult), r=[ycg, rc, cnb], w=[ycg])
                    C.op("dve", lambda e: e.tensor_tensor(out=ygc[k][:], in0=ycg.t[:, qi, :], in1=szc2[k][:],
                                                          op=ALU.mult), r=[ycg, szc2[k]], w=[ygc[k]])
                    C.dma("sp", yg_s[ti * 128:(ti + 1) * 128, 0:512], ygc[k][:], r=[ygc[k]], w=[ygB[ti]])

            for n in range(len(iters) + LOOK):
                if n < len(iters):
                    front(n)
                if n >= LOOK:
                    back(n - LOOK)
      post_stage(1, xsrc, xsrcB, dst, dstB, wts=hooks.get("wpost"))

    if layers == (0,):
        layer0(x_in, None, out_d, None)
    elif layers == (1,):
        layer1(x_in, None, out_d, None)
    else:
        pre = {}

        def l0_pre_s2():
            pre["w_s1"] = alloc_l1_weights(True)
            pre["wpost0"] = alloc_post_weights(True)
            h0["wpost"] = pre["wpost0"]
            h0["s2_thunks"] = []
            load_post_weights(0, pre["wpost0"], h0["s2_thunks"])

        def l0_pre_p():
            while h0["s2_thunks"]:
                h0["s2_thunks"].pop(0)()
            h0["p_thunks"] = []
            load_l1_weights(pre["w_s1"], h0["p_thunks"])

        h0 = {"pre_s2": l0_pre_s2, "pre_p": l0_pre_p}
        layer0(x_in, None, x_mid, xmidB, hooks=h0)
        while h0.get("p_thunks"):
            h0["p_thunks"].pop(0)()
        C.rpop(1)
        h1 = {"w_s1": pre["w_s1"]}

        def l1_pre_s2():
            C.rpop(4)
            pre["wpost1"] = alloc_post_weights(True)
            h1["wpost"] = pre["wpost1"]
            h1["s2_thunks"] = []
            load_post_weights(1, pre["wpost1"], h1["s2_thunks"])

        h1["pre_s2"] = l1_pre_s2
        layer1(x_mid, xmidB, out_d, None, hooks=h1)
        C.rpop(1)
    C.S.emit()
    nc.dbg_names = dbg_names
    return nc


def _host_inputs(inputs, layers):
    f = lambda a: np.ascontiguousarray(np.asarray(a))
    x = f(inputs["x"])
    p = f(inputs["p"])
    pos = f(inputs["positions"]).astype(np.int32)
    shared = {}
    for l in layers:
        shared[f"ln_g{l}"] = f(inputs["post_ln_g"][l])
        shared[f"ln_b{l}"] = f(inputs["post_ln_b"][l])
        shared[f"proj{l}"] = f(inputs["ple_proj"][l])
        shared[f"gate{l}"] = f(inputs["ple_gate"][l])
    if 0 in layers:
        w = f(inputs["ev_w_in"][0])
        qcols = w[:, 0:512].reshape(1024, 8, 64)
        order = [0, 4, 1, 5, 2, 6, 3, 7]
        wq = qcols[:, order, :].reshape(1024, 512)
        shared["w_in0"] = f(np.concatenate([wq, w[:, 512:]], axis=1))
        shared["conv_w"] = f(inputs["ev_conv_w"][0])
        shared["sink"] = f(inputs["ev_sink"][0])
        shared["a_norm"] = f(inputs["ev_a_norm"][0])
        shared["b_norm"] = f(inputs["ev_b_norm"][0])
        shared["w_out0"] = f(inputs["ev_w_out"][0])
    if 1 in layers:
        shared["w_in1"] = f(inputs["od_w_in"][0])
        shared["w_uq"] = f(inputs["od_w_uq"][0])
        shared["w_ukv"] = f(inputs["od_w_ukv"][0])
        shared["w_sT"] = f(np.transpose(np.asarray(inputs["od_w_s"][0]), (2, 0, 1)))
        shared["b_sT"] = f(np.transpose(np.asarray(inputs["od_b_s"][0]), (1, 0)))
        shared["q_norm"] = f(inputs["od_q_norm"][0])
        shared["kv_norm"] = f(inputs["od_kv_norm"][0])
        shared["v_ln_g"] = f(inputs["od_v_ln_g"][0])
        shared["v_ln_b"] = f(inputs["od_v_ln_b"][0])
        shared["c_norm"] = f(inputs["od_c_norm"][0])
        shared["d_norm"] = f(inputs["od_d_norm"][0])
        shared["w_out1"] = f(inputs["od_w_out"][0])
        half = 16
        shared["inv_freq"] = (10000.0 ** (-np.arange(half, dtype=np.float32) / half)).astype(np.float32)
    maps = []
    for b in range(x.shape[0]):
        m = dict(shared)
        m["x"] = f(x[b])
        for l in layers:
            m[f"p{l}"] = f(p[l, b])
        m["pos"] = f(pos[b])
        m["posT"] = f(pos[b].reshape(NT, 128).T)
        maps.append(m)
    return maps


_CACHE = {}


def _run(layers, inputs, ncores=NCORES, debug=False):
    key = (layers, debug)
    if key not in _CACHE:
        _CACHE[key] = build(layers, debug)
    nc = _CACHE[key]
    maps = _host_inputs(inputs, layers)[:ncores]
    res = run_bass_kernel_spmd(nc, maps, core_ids=list(range(len(maps))))
    if debug:
        return res.results
    return np.stack([r["out"] for r in res.results], axis=0)


FUSED = True


def kernel(**inputs):
    if FUSED:
        out = _run((0, 1), inputs)
    else:
        x1 = _run((0,), inputs)
        inputs1 = dict(inputs)
        inputs1["x"] = x1
        out = _run((1,), inputs1)
    return np.ascontiguousarray(out, dtype=np.float32)
```
